# Optimizing a Trainium2 kernel written in Bass

```python
import jax
import jax.numpy as jnp
from jax import lax
import numpy as np

D_MODEL = 2048
BATCH = 4
SEQ = 4096
DEPTH = 2

GRID_W = 64
CTX_LEN = 256
D_MIX = D_MODEL
D_A = D_MIX // 2
NA_HEAD_DIM = 64
NA_HEADS = D_A // NA_HEAD_DIM
NA_KH_MAX = 8
NA_KW = 16
D_B = D_MIX // 4
GLA_HEADS = 4
GLA_DV = D_B // GLA_HEADS
GLA_DK = GLA_DV // 2
DK_B = GLA_HEADS * GLA_DK
GATE_RANK = 16
GATE_TAU = 16.0
GLA_CHUNK = 64
D_C = D_MIX - D_A - D_B
CONV_K = 31

ROPE_BASE = 10000.0
LN_EPS = 1e-5
RMS_EPS = 1e-6
DEEPNORM_ALPHA = (2 * DEPTH) ** 0.25
DEEPNORM_BETA = (8 * DEPTH) ** -0.25
SPLIT_SIZES = (D_A, D_A, D_A, D_A, DK_B, DK_B, D_B, D_B, 2 * GATE_RANK, D_C, D_C, D_C)
N_IN = 4 * D_A + 2 * DK_B + 2 * D_B + 2 * GATE_RANK + 3 * D_C

kernel_name = 'hybrid_na_gla_conformer_dit'


def _layer_norm(x):
    xf = x.astype(jnp.float32)
    mu = jnp.mean(xf, axis=-1, keepdims=True)
    var = jnp.mean(jnp.square(xf - mu), axis=-1, keepdims=True)
    return ((xf - mu) * lax.rsqrt(var + LN_EPS)).astype(x.dtype)


def _rms_norm(x, g):
    xf = x.astype(jnp.float32)
    y = xf * lax.rsqrt(jnp.mean(jnp.square(xf), axis=-1, keepdims=True) + RMS_EPS)
    return y.astype(x.dtype) * g


def _split_proj(p):
    idx, acc = [], 0
    for s in SPLIT_SIZES[:-1]:
        acc += s
        idx.append(acc)
    return jnp.split(p, idx, axis=-1)


def _rope_1d(x, pos):
    d = x.shape[-1]
    inv = ROPE_BASE ** (-jnp.arange(0, d, 2, dtype=jnp.float32) / d)
    ang = pos[:, None] * inv[None, :]
    cos = jnp.cos(ang)[None, :, None, :].astype(x.dtype)
    sin = jnp.sin(ang)[None, :, None, :].astype(x.dtype)
    x1, x2 = jnp.split(x, 2, axis=-1)
    return jnp.concatenate([x1 * cos - x2 * sin, x1 * sin + x2 * cos], axis=-1)


def _axial_rope(x, rows, cols):
    xr, xc = jnp.split(x, 2, axis=-1)
    return jnp.concatenate([_rope_1d(xr, rows), _rope_1d(xc, cols)], axis=-1)


def _neighbourhood_attention(q, k, v, kc, vc, rpb):
    B, L, H, Dh = q.shape
    rows = L // GRID_W
    kh = min(NA_KH_MAX, rows)
    qg = q.reshape(B, rows, GRID_W, H, Dh)
    kg = k.reshape(B, rows, GRID_W, H, Dh)
    vg = v.reshape(B, rows, GRID_W, H, Dh)
    col = jnp.arange(GRID_W)
    col_start = jnp.clip(col - NA_KW // 2, 0, GRID_W - NA_KW)
    col_idx = col_start[:, None] + jnp.arange(NA_KW)[None, :]
    dcol = col_idx - col[:, None] + (NA_KW - 1)
    scale = Dh ** -0.5
    n_loc = kh * NA_KW

    def row_block(r):
        rs = jnp.clip(r - kh // 2, 0, rows - kh)
        q_r = lax.dynamic_index_in_dim(qg, r, axis=1, keepdims=False)
        k_rows = lax.dynamic_slice_in_dim(kg, rs, kh, axis=1)
        v_rows = lax.dynamic_slice_in_dim(vg, rs, kh, axis=1)
        k_win = k_rows[:, :, col_idx]
        v_win = v_rows[:, :, col_idx]
        drow = rs + jnp.arange(kh) - r + (NA_KH_MAX - 1)
        bias = rpb[:, drow[:, None, None], dcol[None, :, :]]
        s_loc = jnp.einsum('bqhd,biqjhd->bhqij', q_r, k_win) * scale + jnp.transpose(bias, (0, 2, 1, 3))[None]
        s_loc = s_loc.reshape(B, H, GRID_W, n_loc)
        s_ctx = jnp.einsum('bqhd,bkhd->bhqk', q_r, kc) * scale
        p = jax.nn.softmax(jnp.concatenate([s_loc, s_ctx], axis=-1).astype(jnp.float32), axis=-1).astype(v.dtype)
        p_loc = p[..., :n_loc].reshape(B, H, GRID_W, kh, NA_KW)
        p_ctx = p[..., n_loc:]
        return (jnp.einsum('bhqij,biqjhd->bqhd', p_loc, v_win)
                + jnp.einsum('bhqk,bkhd->bqhd', p_ctx, vc))

    out = lax.map(row_block, jnp.arange(rows))
    return jnp.moveaxis(out, 0, 1).reshape(B, L, H * Dh)


def _context_attention(q, k, v):
    B, Lc, H, Dh = q.shape
    s = jnp.einsum('bqhd,bkhd->bhqk', q, k) * Dh ** -0.5
    p = jax.nn.softmax(s.astype(jnp.float32), axis=-1).astype(v.dtype)
    return jnp.einsum('bhqk,bkhd->bqhd', p, v).reshape(B, Lc, H * Dh)


def _to_chunks(t):
    B, L, H, D = t.shape
    return t.reshape(B, L // GLA_CHUNK, GLA_CHUNK, H, D).transpose(0, 1, 3, 2, 4)


def _from_chunks(t):
    B, N, H, C, D = t.shape
    return t.transpose(0, 1, 3, 2, 4).reshape(B, N * C, H, D)


def _gla_chunk_terms(kc, vc, gc):
    b = jnp.cumsum(gc, axis=3)
    b_last = b[:, :, :, -1:, :]
    kv = jnp.einsum('bnhck,bnhcv->bnhkv', kc * jnp.exp(b_last - b), vc)
    decay = jnp.exp(b_last[:, :, :, 0, :])
    return b, kv, decay


def _gla_states(kv, decay, s0):
    def step(s, inp):
        kv_n, d_n = inp
        return d_n[..., None] * s + kv_n, s
    s_fin, starts = lax.scan(step, s0, (jnp.moveaxis(kv, 1, 0), jnp.moveaxis(decay, 1, 0)))
    return jnp.moveaxis(starts, 0, 1), s_fin


def _gla_forward(q, k, v, g, s0):
    dt = v.dtype
    qc = _to_chunks(q.astype(jnp.float32))
    kc = _to_chunks(k.astype(jnp.float32))
    vc = _to_chunks(v.astype(jnp.float32))
    gc = _to_chunks(g)
    b, kv, decay = _gla_chunk_terms(kc, vc, gc)
    starts, s_fin = _gla_states(kv, decay, s0)
    q_t = qc * jnp.exp(b)
    k_t = kc * jnp.exp(-b)
    mask = jnp.tril(jnp.ones((GLA_CHUNK, GLA_CHUNK), dtype=bool))
    att = jnp.where(mask, jnp.einsum('bnhik,bnhjk->bnhij', q_t, k_t), 0.0)
    o = jnp.einsum('bnhck,bnhkv->bnhcv', q_t, starts) + jnp.einsum('bnhij,bnhjv->bnhiv', att, vc)
    return _from_chunks(o).astype(dt), s_fin


def _gla_final_state(k, v, g, s0):
    kc = _to_chunks(k.astype(jnp.float32))
    vc = _to_chunks(v.astype(jnp.float32))
    _, kv, decay = _gla_chunk_terms(kc, vc, _to_chunks(g))
    _, s_fin = _gla_states(kv, decay, s0)
    return s_fin


def _gla_log_decay(glr, w2, b):
    B, L, _ = glr.shape
    lr = glr.astype(jnp.float32).reshape(B, L, 2, GATE_RANK)
    logits = jnp.einsum('bldr,drk->bldk', lr, w2.astype(jnp.float32)) + b.astype(jnp.float32)
    g = (jax.nn.log_sigmoid(logits) / GATE_TAU).reshape(B, L, 2, GLA_HEADS, GLA_DK)
    return g[:, :, 0], g[:, :, 1]


def _conformer_conv(a, gt, w, bconv, ln_g, ln_b):
    u = a * jax.nn.sigmoid(gt)
    u = lax.conv_general_dilated(u, w[:, None, :].astype(u.dtype), window_strides=(1,),
                                 padding=[(CONV_K // 2, CONV_K // 2)],
                                 dimension_numbers=('NWC', 'WIO', 'NWC'),
                                 feature_group_count=u.shape[-1]) + bconv
    return jax.nn.silu(_layer_norm(u) * ln_g + ln_b)


def _layer(x, cx, c, c_ctx, w_ada, b_ada, w_in, rpb, gla_w2, gla_b, gla_norm,
           conv_w, conv_b, conv_ln_g, conv_ln_b, w_out, post_ln_g, post_ln_b, last):
    B, L, _ = x.shape
    Lc = cx.shape[1]
    shift, scale, gate = jnp.split(jax.nn.silu(c) @ w_ada + b_ada, 3, axis=-1)
    shift_c, scale_c, gate_c = jnp.split(jax.nn.silu(c_ctx) @ w_ada + b_ada, 3, axis=-1)
    h = _layer_norm(x) * (1 + scale[:, None]) + shift[:, None]
    hc = _layer_norm(cx) * (1 + scale_c) + shift_c
    qa, ka, va, za, qb, kb, vb, zb, glr, ca, cg, zc = _split_proj(h @ w_in)
    qa_c, ka_c, va_c, za_c, qb_c, kb_c, vb_c, zb_c, glr_c, ca_c, cg_c, zc_c = _split_proj(hc @ w_in)

    def heads_a(t):
        return t.reshape(t.shape[0], t.shape[1], NA_HEADS, NA_HEAD_DIM)

    def heads_k(t):
        return t.reshape(t.shape[0], t.shape[1], GLA_HEADS, GLA_DK)

    def heads_v(t):
        return t.reshape(t.shape[0], t.shape[1], GLA_HEADS, GLA_DV)

    kca, vca = heads_a(ka_c), heads_a(va_c)
    out_a = _neighbourhood_attention(heads_a(qa), heads_a(ka), heads_a(va), kca, vca, rpb) * jax.nn.silu(za)

    pos = jnp.arange(L)
    rows_pos = (pos // GRID_W).astype(jnp.float32)
    cols_pos = (pos % GRID_W).astype(jnp.float32)
    q_scale = GLA_DK ** -0.5
    qbh = _axial_rope(heads_k(qb), rows_pos, cols_pos) * q_scale
    kbh = _axial_rope(heads_k(kb), rows_pos, cols_pos)
    vbh = heads_v(vb)
    g_f, g_b = _gla_log_decay(glr, gla_w2, gla_b)
    kbc, vbc = heads_k(kb_c), heads_v(vb_c)
    gc_f, gc_b = _gla_log_decay(glr_c, gla_w2, gla_b)
    s0 = jnp.zeros((B, GLA_HEADS, GLA_DK, GLA_DV), jnp.float32)
    flip = lambda t: jnp.flip(t, axis=1)
    if last:
        s_f = _gla_final_state(kbc, vbc, gc_f, s0)
        s_b = _gla_final_state(flip(kbc), flip(vbc), flip(gc_b), s0)
    else:
        qbc = heads_k(qb_c) * q_scale
        oc_f, s_f = _gla_forward(qbc, kbc, vbc, gc_f, s0)
        oc_b, s_b = _gla_forward(flip(qbc), flip(kbc), flip(vbc), flip(gc_b), s0)
        oc = oc_f + flip(oc_b)
    o_f, _ = _gla_forward(qbh, kbh, vbh, g_f, s_f)
    o_b, _ = _gla_forward(flip(qbh), flip(kbh), flip(vbh), flip(g_b), s_b)
    out_b = _rms_norm(o_f + flip(o_b), gla_norm).reshape(B, L, D_B) * jax.nn.silu(zb)

    out_c = _conformer_conv(ca, cg, conv_w, conv_b, conv_ln_g, conv_ln_b) * jax.nn.silu(zc)

    y = jnp.concatenate([out_a, out_b, out_c], axis=-1) @ w_out
    x_new = _layer_norm(DEEPNORM_ALPHA * x + gate[:, None] * y) * post_ln_g + post_ln_b
    if last:
        return x_new, None

    out_a_c = _context_attention(heads_a(qa_c), kca, vca) * jax.nn.silu(za_c)
    out_b_c = _rms_norm(oc, gla_norm).reshape(B, Lc, D_B) * jax.nn.silu(zb_c)
    out_c_c = _conformer_conv(ca_c, cg_c, conv_w, conv_b, conv_ln_g, conv_ln_b) * jax.nn.silu(zc_c)
    yc = jnp.concatenate([out_a_c, out_b_c, out_c_c], axis=-1) @ w_out
    cx_new = _layer_norm(DEEPNORM_ALPHA * cx + gate_c * yc) * post_ln_g + post_ln_b
    return x_new, cx_new


def setup_inputs(seed: int = 0) -> dict:
    key = jax.random.key(seed)
    ks = jax.random.split(key, 20)
    nrm = jax.random.normal
    D = D_MODEL
    return {
        'x': nrm(ks[0], (BATCH, SEQ, D), jnp.float32),
        'c': nrm(ks[1], (BATCH, D), jnp.float32),
        'ctx': nrm(ks[2], (BATCH, CTX_LEN, D), jnp.float32),
        'c_ctx': nrm(ks[3], (D,), jnp.float32),
        'w_ada': nrm(ks[4], (DEPTH, D, 3 * D), jnp.float32) * (0.5 * D ** -0.5),
        'b_ada': nrm(ks[5], (DEPTH, 3 * D), jnp.float32) * 0.02,
        'w_in': nrm(ks[6], (DEPTH, D, N_IN), jnp.float32) * D ** -0.5,
        'rpb': nrm(ks[7], (DEPTH, NA_HEADS, 2 * NA_KH_MAX - 1, 2 * NA_KW - 1), jnp.float32) * 0.02,
        'gla_w2': nrm(ks[8], (DEPTH, 2, GATE_RANK, DK_B), jnp.float32) * GATE_RANK ** -0.5,
        'gla_b': nrm(ks[9], (DEPTH, 2, DK_B), jnp.float32) * 0.1,
        'gla_norm': 1.0 + 0.01 * nrm(ks[10], (DEPTH, GLA_DV), jnp.float32),
        'conv_w': nrm(ks[11], (DEPTH, CONV_K, D_C), jnp.float32) * CONV_K ** -0.5,
        'conv_b': nrm(ks[12], (DEPTH, D_C), jnp.float32) * 0.01,
        'conv_ln_g': 1.0 + 0.01 * nrm(ks[13], (DEPTH, D_C), jnp.float32),
        'conv_ln_b': nrm(ks[14], (DEPTH, D_C), jnp.float32) * 0.01,
        'w_out': nrm(ks[15], (DEPTH, D_MIX, D), jnp.float32) * (D_MIX ** -0.5 * DEEPNORM_BETA),
        'post_ln_g': 1.0 + 0.01 * nrm(ks[16], (DEPTH, D), jnp.float32),
        'post_ln_b': nrm(ks[17], (DEPTH, D), jnp.float32) * 0.01,
    }


def reference(x, c, ctx, c_ctx, w_ada, b_ada, w_in, rpb, gla_w2, gla_b, gla_norm,
              conv_w, conv_b, conv_ln_g, conv_ln_b, w_out, post_ln_g, post_ln_b):
    cx = ctx
    for l in range(DEPTH):
        x, cx = _layer(x, cx, c, c_ctx, w_ada[l], b_ada[l], w_in[l], rpb[l], gla_w2[l], gla_b[l],
                       gla_norm[l], conv_w[l], conv_b[l], conv_ln_g[l], conv_ln_b[l], w_out[l],
                       post_ln_g[l], post_ln_b[l], last=(l == DEPTH - 1))
    return x
```

```python
import numpy as np
from contextlib import ExitStack
import concourse.bass as bass
import concourse.mybir as mybir
from concourse.bass_utils import run_bass_kernel_spmd

F32 = mybir.dt.float32
BF16 = mybir.dt.bfloat16
AF = mybir.ActivationFunctionType
ALU = mybir.AluOpType

NDS = 12
D = 2048
NT = 34
T = NT * 128
NIN = 7200
ALPHA = 4.0 ** 0.25
NEG = -30000.0


class Prog:
    ENG = ('pe', 'act', 'dve', 'pool', 'sp')

    def __init__(self, nc):
        self.nc = nc
        self.ops = {e: [] for e in self.ENG}
        self.esem = {e: nc.alloc_semaphore(f"s_{e}") for e in self.ENG}
        self.ecnt = {e: 0 for e in self.ENG}
        self.known = {e: {} for e in self.ENG}
        self.dsem = {e: [nc.alloc_semaphore(f"d_{e}_{i}") for i in range(NDS)] for e in self.ENG}
        self.dcnt = {e: [0] * NDS for e in self.ENG}
        self.dnext = {e: 0 for e in self.ENG}
        self.res = {}
        self.semid = {}
        self.pending = {e: [] for e in self.ENG}

    def _deps(self, eng, r, w):
        deps = list(self.pending[eng])
        self.pending[eng] = []
        for k in r:
            st = self.res.get(k)
            if st and st[0] is not None:
                deps.append(st[0])
        for k in w:
            st = self.res.get(k)
            if st:
                if st[0] is not None:
                    deps.append(st[0])
                deps.extend(st[1])
        best = {}
        for (sem, val, peng, isdma) in deps:
            if (not isdma) and peng == eng and eng == 'pe':
                continue
            key = id(sem)
            self.semid[key] = sem
            if best.get(key, 0) < val:
                best[key] = val
        waits = []
        kn = self.known[eng]
        for key, val in best.items():
            if kn.get(key, 0) >= val:
                continue
            kn[key] = val
            waits.append((self.semid[key], val))
        return waits

    def _commit(self, tok, r, w):
        for k in r:
            st = self.res.get(k)
            if st is None:
                st = [None, []]
                self.res[k] = st
            st[1].append(tok)
        for k in w:
            self.res[k] = [tok, []]

    def op(self, eng, fn, r=(), w=()):
        waits = self._deps(eng, r, w)
        self.ecnt[eng] += 1
        sem = self.esem[eng]
        tok = (sem, self.ecnt[eng], eng, False)
        self.ops[eng].append((waits, fn, (sem, 1)))
        self._commit(tok, r, w)

    def dma(self, eng, fn, r=(), w=()):
        waits = self._deps(eng, r, w)
        j = self.dnext[eng]
        self.dnext[eng] = (j + 1) % NDS
        sem = self.dsem[eng][j]
        prev = self.dcnt[eng][j]
        self.semid[id(sem)] = sem
        if prev > 0 and self.known[eng].get(id(sem), 0) < 16 * prev:
            self.known[eng][id(sem)] = 16 * prev
            waits.append((sem, 16 * prev))
        self.dcnt[eng][j] = prev + 1
        tok = (sem, 16 * (prev + 1), eng, True)
        self.ops[eng].append((waits, fn, (sem, 16)))
        self._commit(tok, r, w)

    def _all_tokens(self):
        toks = []
        for e in self.ENG:
            if self.ecnt[e] > 0:
                toks.append((self.esem[e], self.ecnt[e], e, True))
            for j in range(NDS):
                if self.dcnt[e][j] > 0:
                    toks.append((self.dsem[e][j], 16 * self.dcnt[e][j], e, True))
        return toks

    def barrier(self):
        toks = self._all_tokens()
        for e in self.ENG:
            self.pending[e].extend(toks)
        self.res = {}

    def emit(self):
        nc = self.nc
        fin = [(s, v) for (s, v, _, _) in self._all_tokens()]
        with nc.Block() as block:
            secs = {'pe': block.tensor, 'act': block.scalar, 'dve': block.vector,
                    'pool': block.gpsimd, 'sp': block.sync}
            for e in self.ENG:
                ops = self.ops[e]
                f = fin if e == 'sp' else []

                def body(engine, ops=ops, f=f):
                    for (waits, fn, (sem, amt)) in ops:
                        for (s, v) in waits:
                            engine.wait_ge(s, v)
                        inst = fn(engine)
                        inst.then_inc(sem, amt)
                    for (s, v) in f:
                        engine.wait_ge(s, v)
                if ops or f:
                    secs[e](body)


GROUPS = [
    ('qt', 0, 512, 0), ('qt', 512, 512, 512),
    ('kt', 1024, 512, 0), ('kt', 1536, 512, 512),
    ('va', 2048, 512, 0), ('va', 2560, 512, 512),
    ('tm', 3072, 512, 0), ('tm', 3584, 512, 512), ('tm', 4096, 512, 1024),
    ('tm', 4608, 512, 1536), ('tm', 5120, 512, 2048), ('tm', 5632, 32, 2560),
    ('ct', 5664, 512, 0), ('ct', 6176, 512, 512),
    ('tm', 6688, 512, 2592),
]
TMW = 3104
TM_ZA, TM_QB, TM_KB, TM_VB, TM_ZB, TM_LR, TM_ZC = 0, 1024, 1280, 1536, 2048, 2560, 2592


def build(n_layers=2, dbg=False, stop_after=None):
    nc = bass.Bass("TRN2", target_bir_lowering=False)

    def din(name, shape, dt=F32):
        return nc.dram_tensor(name, list(shape), dt, kind="ExternalInput").ap()

    def dscr(name, shape, dt=F32):
        return nc.dram_tensor(name, list(shape), dt, kind="ExternalOutput" if dbg else "Internal").ap()

    xin = din("xin", [T, D])
    cT_d = din("cT", [128, 2, 16])
    w_ada = din("w_ada", [2, D, 3 * D])
    b_ada = din("b_ada", [2, 3 * D])
    w_in = din("w_in", [2, D, NIN])
    btab = din("btab", [2, 16, 128, 1024])
    nmask_d = din("nmask", [128, 64])
    w2a_d = din("w2a", [2, 2, 17, 256])
    gnorm_d = din("gnorm", [2, 128])
    cwT_d = din("cwT", [2, 512, 31])
    cb_d = din("cb", [2, 128, 4])
    clg_d = din("clg", [2, 512])
    clb_d = din("clb", [2, 512])
    w_out = din("w_out", [2, D, D])
    plg_d = din("plg", [2, D])
    plb_d = din("plb", [2, D])
    cos_d = din("ropecos", [4096, 64])
    sin_d = din("ropesin", [4096, 64])
    ident_d = din("ident", [128, 128])
    tri_d = din("tri", [4, 128, 128])
    out = nc.dram_tensor("out", [4096, D], F32, kind="ExternalOutput").ap()

    MOD = dscr("MOD", [2, 2, 3 * D])
    QT = dscr("QT", [1024, T], BF16)
    KT = dscr("KT", [1024, T], BF16)
    VA = dscr("VA", [T, 1024], BF16)
    TM = dscr("TM", [T, TMW])
    CT = dscr("CT", [1024, T])
    CAT = dscr("CAT", [T, D])
    X1 = dscr("X1", [T, D])

    P = Prog(nc)
    psum = nc.alloc_psum_tensor("psum", [128, 4096], F32)

    def bank(i, n=512, p0=0, p1=128):
        return psum[p0:p1, i * 512:i * 512 + n]

    ident = nc.alloc_sbuf_tensor("ident_sb", [128, 128], F32)
    P.dma('sp', lambda e: e.dma_start(out=ident[:], in_=ident_d), w=['ident'])

    uid = [0]

    def U(s):
        uid[0] += 1
        return f"{s}_{uid[0]}"

    def phase_mod():
        with ExitStack() as es:
            sb = lambda n, s, d=F32: es.enter_context(nc.sbuf_tensor(U(n), s, d))
            cT = sb("cT", [128, 2, 16])
            scT = sb("scT", [128, 2, 16])
            L = sb("L", [128, 16, 128])
            wbuf = [sb("wada", [128, 16, 512]) for _ in range(2)]
            bada = sb("bada", [128, 3 * D])
            modrow = sb("modrow", [128, 3 * D])
            P.dma('sp', lambda e: e.dma_start(out=cT[:], in_=cT_d), w=['cT'])
            P.op('act', lambda e: e.activation(out=scT[:], in_=cT[:], func=AF.Silu), r=['cT'], w=['scT'])
            for wh in range(2):
                P.op('dve', lambda e, wh=wh: e.tensor_copy(
                    out=L[:, :, wh * 64:(wh + 1) * 64],
                    in_=scT[:, wh:wh + 1, :].rearrange("p o k -> p k o").broadcast_to([128, 16, 64])),
                    r=['scT'], w=[('L', wh)])
            gi = 0
            for l in range(n_layers):
                P.dma('sp', lambda e, l=l: e.dma_start(out=bada[:], in_=b_ada[l].partition_broadcast(128)),
                      r=[], w=['bada'])
                wv = w_ada[l].rearrange("(kc p) n -> p kc n", p=128)
                for ng in range(12):
                    wb = wbuf[gi % 2]
                    P.dma('sp', lambda e, wb=wb, wv=wv, ng=ng: e.dma_start(out=wb[:], in_=wv[:, :, ng * 512:(ng + 1) * 512]),
                          w=[('wada', gi % 2)])
                    bk = gi % 2
                    for kc in range(16):
                        P.op('pe', lambda e, wb=wb, kc=kc, bk=bk: e.matmul(
                            bank(bk), lhsT=L[:, kc, :], rhs=wb[:, kc, :], start=(kc == 0), stop=(kc == 15)),
                            r=[('wada', gi % 2), ('L', 0), ('L', 1)], w=[('pb', bk)])
                    P.op('dve', lambda e, ng=ng, bk=bk: e.tensor_tensor(
                        out=modrow[:, ng * 512:(ng + 1) * 512], in0=bank(bk), in1=bada[:, ng * 512:(ng + 1) * 512], op=ALU.add),
                        r=[('pb', bk), 'bada'], w=['modrow'])
                    gi += 1
                for wh in range(2):
                    P.dma('sp', lambda e, l=l, wh=wh: e.dma_start(out=MOD[l, wh:wh + 1, :], in_=modrow[wh * 64:wh * 64 + 1, :]),
                          r=['modrow'], w=[('MOD', l)])
        P.barrier()

    def phase_inproj(l):
        src = xin if l == 0 else X1
        with ExitStack() as es:
            sb = lambda n, s, d=F32: es.enter_context(nc.sbuf_tensor(U(n), s, d))
            GTL = 17
            hT = sb("hT", [128, 16, GTL * 128], BF16)
            wg = [sb("wg", [128, 16, 512], BF16) for _ in range(2)]
            xt = [sb("xt", [128, D]) for _ in range(2)]
            st32 = [sb("st32", [128, 512]) for _ in range(4)]
            st16 = [sb("st16", [128, 512], BF16) for _ in range(4)]
            t16 = sb("t16", [16, 4, 128])
            cols = sb("cols", [128, 4, 16])
            stats = sb("stats", [128, 4, 6])
            mv = sb("mv", [128, 2])
            rs = sb("rs", [128, 1])
            nmr = sb("nmr", [128, 1])
            for wh in range(2):
                for j in range(2):
                    i = wh * 2 + j
                    P.dma('sp', lambda e, wh=wh, j=j, i=i: e.dma_start(
                        out=t16[:, i, :], in_=MOD[l, wh, j * D:(j + 1) * D].rearrange("(k p) -> k p", p=128)),
                        r=[('MOD', l)], w=[('t16', i)])
                    P.op('pe', lambda e, i=i: e.transpose(bank(0, 16), t16[:, i, :], ident[:16, :16]),
                         r=[('t16', i), 'ident'], w=[('pb', 0)])
                    if j == 0:
                        P.op('dve', lambda e, i=i: e.tensor_copy(out=cols[:, i, :], in_=bank(0, 16)),
                             r=[('pb', 0)], w=[('cols', i)])
                    else:
                        P.op('dve', lambda e, i=i: e.tensor_scalar(out=cols[:, i, :], in0=bank(0, 16), scalar1=1.0, scalar2=None, op0=ALU.add),
                             r=[('pb', 0)], w=[('cols', i)])
            if dbg and l == 0:
                DBGC2 = dscr('DBGC2', [128, 4, 16])
                P.dma('sp', lambda e: e.dma_start(out=DBGC2, in_=cols[:]), r=[('cols', i) for i in range(4)], w=['dbgc2'])
            wv = w_in[l].rearrange("(kc p) n -> p kc n", p=128)
            gcount = 0
            stc = [0]
            import os as _os
            for g in range(2 if _os.environ.get('KT_NOLN') is None else 0):
                for tl in range(GTL):
                    ti = g * GTL + tl
                    xb = xt[ti % 2]
                    kx = ('xt', ti % 2)
                    P.dma('sp', lambda e, xb=xb, ti=ti: e.dma_start(out=xb[:], in_=src[ti * 128:(ti + 1) * 128, :]),
                          w=[kx])
                    for q in range(4):
                        P.op('dve', lambda e, xb=xb, q=q: e.bn_stats(out=stats[:, q, :], in_=xb[:, q * 512:(q + 1) * 512]),
                             r=[kx], w=[('stats', q)])
                    P.op('dve', lambda e: e.bn_aggr(out=mv[:], in_=stats[:].rearrange("p a b -> p (a b)")),
                         r=[('stats', q) for q in range(4)], w=['mv'])
                    P.op('act', lambda e: e.activation(out=rs[:], in_=mv[:, 1:2], func=AF.Sqrt, bias=eps5[:, 0:1], scale=1.0),
                         r=['mv'], w=['rs'])
                    P.op('dve', lambda e: e.reciprocal(out=rs[:], in_=rs[:]), r=['rs'], w=['rs'])
                    P.op('dve', lambda e: e.tensor_scalar(out=nmr[:], in0=mv[:, 0:1], scalar1=rs[:, 0:1], scalar2=-1.0,
                                                          op0=ALU.mult, op1=ALU.mult), r=['mv', 'rs'], w=['nmr'])
                    P.op('act', lambda e, xb=xb: e.activation(out=xb[:], in_=xb[:], func=AF.Identity,
                                                              bias=nmr[:, 0:1], scale=rs[:, 0:1]),
                         r=[kx, 'rs', 'nmr'], w=[kx])
                    wh = 1 if ti < 2 else 0
                    for kc in range(16):
                        bk = 4 + kc // 4
                        sl = psum[:, bk * 512 + (kc % 4) * 128: bk * 512 + (kc % 4 + 1) * 128]
                        kp = ('pb', bk)
                        P.op('pe', lambda e, xb=xb, kc=kc, sl=sl: e.transpose(sl, xb[:, kc * 128:(kc + 1) * 128], ident[:]),
                             r=[kx, 'ident'], w=[kp])
                        dst = hT[:, kc, tl * 128:(tl + 1) * 128]
                        if kc % 2 == 0:
                            P.op('act', lambda e, dst=dst, sl=sl, wh=wh, kc=kc: e.activation(
                                out=dst, in_=sl, func=AF.Identity, bias=cols[:, wh * 2, kc:kc + 1], scale=cols[:, wh * 2 + 1, kc:kc + 1]),
                                r=[kp, ('cols', wh * 2), ('cols', wh * 2 + 1)], w=[('hT', tl, kc)])
                        else:
                            P.op('dve', lambda e, dst=dst, sl=sl, wh=wh, kc=kc: e.tensor_scalar(
                                out=dst, in0=sl, scalar1=cols[:, wh * 2 + 1, kc:kc + 1], scalar2=cols[:, wh * 2, kc:kc + 1],
                                op0=ALU.mult, op1=ALU.add),
                                r=[kp, ('cols', wh * 2), ('cols', wh * 2 + 1)], w=[('hT', tl, kc)])
                if dbg and g == 0 and l == 0:
                    DBGH = dscr('DBGH', [128, 16, GTL * 128], BF16)
                    DBGC = dscr('DBGC', [128, 4, 16])
                    P.dma('sp', lambda e: e.dma_start(out=DBGH, in_=hT[:]), r=[('hT', tl, kc) for tl in range(GTL) for kc in range(16)], w=['dbgh'])
                    P.dma('sp', lambda e: e.dma_start(out=DBGC, in_=cols[:]), r=[('cols', i) for i in range(4)], w=['dbgc'])
                    DBGT = dscr('DBGT', [16, 4, 128])
                    P.dma('sp', lambda e: e.dma_start(out=DBGT, in_=t16[:]), r=[('t16', i) for i in range(4)], w=['dbgt'])
                import os as _os
                for (kind, c0, ncols, doff) in (GROUPS if _os.environ.get('KT_NOPROJ') is None else []):
                    wb = wg[gcount % 2]
                    kw = ('wg', gcount % 2)
                    gcount += 1
                    P.dma('pool', lambda e, wb=wb, c0=c0, ncols=ncols: e.dma_start(out=wb[:, :, 0:ncols], in_=wv[:, :, c0:c0 + ncols]),
                          w=[kw])
                    if kind in ('tm', 'va'):
                        for tl in range(GTL):
                            ti = g * GTL + tl
                            bk = stc[0] % 4
                            for kc in range(16):
                                P.op('pe', lambda e, wb=wb, kc=kc, tl=tl, bk=bk, ncols=ncols: e.matmul(
                                    bank(bk, ncols), lhsT=hT[:, kc, tl * 128:(tl + 1) * 128], rhs=wb[:, kc, 0:ncols],
                                    start=(kc == 0), stop=(kc == 15)),
                                    r=[kw, ('hT', tl, kc)], w=[('pb', bk)])
                            si = stc[0] % 4
                            stg = (st32 if kind == 'tm' else st16)[si]
                            ks = ('st', kind == 'tm', si)
                            eng = 'act' if stc[0] % 2 == 0 else 'dve'
                            if eng == 'act':
                                P.op('act', lambda e, stg=stg, bk=bk, ncols=ncols: e.activation(out=stg[:, 0:ncols], in_=bank(bk, ncols), func=AF.Copy),
                                     r=[('pb', bk)], w=[ks])
                            else:
                                P.op('dve', lambda e, stg=stg, bk=bk, ncols=ncols: e.tensor_copy(out=stg[:, 0:ncols], in_=bank(bk, ncols)),
                                     r=[('pb', bk)], w=[ks])
                            dst = (TM if kind == 'tm' else VA)[ti * 128:(ti + 1) * 128, doff:doff + ncols]
                            P.dma('sp', lambda e, dst=dst, stg=stg, ncols=ncols: e.dma_start(out=dst, in_=stg[:, 0:ncols]),
                                  r=[ks], w=[(kind, ti, doff)])
                            stc[0] += 1
                    else:
                        dt_ = {'qt': QT, 'kt': KT, 'ct': CT}[kind]
                        for blk in range(ncols // 128):
                            tok0 = 0
                            while tok0 < GTL * 128:
                                ntok = min(512, GTL * 128 - tok0)
                                bk = stc[0] % 4
                                for kc in range(16):
                                    P.op('pe', lambda e, wb=wb, kc=kc, blk=blk, tok0=tok0, ntok=ntok, bk=bk: e.matmul(
                                        bank(bk, ntok), lhsT=wb[:, kc, blk * 128:(blk + 1) * 128], rhs=hT[:, kc, tok0:tok0 + ntok],
                                        start=(kc == 0), stop=(kc == 15)),
                                        r=[kw] + [('hT', tt, kc) for tt in range(tok0 // 128, (tok0 + ntok) // 128)], w=[('pb', bk)])
                                si = stc[0] % 4
                                is32 = kind == 'ct'
                                stg = (st32 if is32 else st16)[si]
                                ks = ('st', is32, si)
                                sc = 0.125 if kind == 'qt' else 1.0
                                if stc[0] % 2 == 0:
                                    P.op('act', lambda e, stg=stg, bk=bk, ntok=ntok, sc=sc: e.activation(
                                        out=stg[:, 0:ntok], in_=bank(bk, ntok), func=AF.Copy, scale=sc),
                                        r=[('pb', bk)], w=[ks])
                                else:
                                    P.op('dve', lambda e, stg=stg, bk=bk, ntok=ntok, sc=sc: e.tensor_scalar(
                                        out=stg[:, 0:ntok], in0=bank(bk, ntok), scalar1=sc, scalar2=None, op0=ALU.mult),
                                        r=[('pb', bk)], w=[ks])
                                r0 = doff + blk * 128
                                g0 = g * GTL * 128 + tok0
                                dst = dt_[r0:r0 + 128, g0:g0 + ntok]
                                P.dma('sp', lambda e, dst=dst, stg=stg, ntok=ntok: e.dma_start(out=dst, in_=stg[:, 0:ntok]),
                                      r=[ks], w=[(kind, r0, g0)])
                                stc[0] += 1
                                tok0 += ntok
        P.barrier()

    def phase_na(l):
        last = (l == n_layers - 1)
        LAG = 2
        with ExitStack() as es:
            sb = lambda n, s, d=F32: es.enter_context(nc.sbuf_tensor(U(n), s, d))
            qh = [sb("qh", [64, T], BF16) for _ in range(2)]
            kh = [sb("kh", [64, T], BF16) for _ in range(2)]
            vh = [sb("vh", [128, NT, 65], BF16) for _ in range(2)]
            zh = [sb("zh", [128, NT, 64]) for _ in range(2)]
            bt = [sb("bt", [128, 16, 64]) for _ in range(2)]
            oh = [sb("oh", [128, NT, 64]) for _ in range(2)]
            nmask = sb("nmask", [128, 64])
            btB0 = [sb("btB0", [128, 16, 64]) for _ in range(2)]
            btB1 = [sb("btB1", [128, 16, 64]) for _ in range(2)]
            sT = [sb("sT", [128, 640]) for _ in range(3)]
            pT = [sb("pT", [128, 896], BF16) for _ in range(3)]
            rc = [sb("rc", [128, 1]) for _ in range(3)]
            for i in range(3):
                P.op('pool', lambda e, i=i: e.memset(sT[i][:, 512:576], NEG), w=[('sTc', i)])
            P.dma('sp', lambda e: e.dma_start(out=nmask[:], in_=nmask_d), w=['nmask'])
            for i in range(2):
                P.op('pool', lambda e, i=i: e.memset(vh[i][:, :, 64:65], 1.0), w=[('vh1', i)])
            t_first = 2 if last else 0

            def head_load(h):
                hb = h % 2
                P.dma('sp', lambda e: e.dma_start(out=qh[hb][:], in_=QT[h * 64:(h + 1) * 64, :]), w=[('qh', hb)])
                P.dma('sp', lambda e: e.dma_start(out=kh[hb][:], in_=KT[h * 64:(h + 1) * 64, :]), w=[('kh', hb)])
                P.dma('sp', lambda e: e.dma_start(
                    out=vh[hb][:, :, 0:64], in_=VA[:, h * 64:(h + 1) * 64].rearrange("(t p) d -> p t d", p=128)), w=[('vh', hb)])
                P.dma('sp', lambda e: e.dma_start(
                    out=zh[hb][:], in_=TM[:, TM_ZA + h * 64:TM_ZA + (h + 1) * 64].rearrange("(t p) d -> p t d", p=128)), w=[('zh', hb)])
                P.dma('sp', lambda e: e.dma_start(
                    out=bt[hb][:], in_=btab[l, h].rearrange("p (a c) -> p a c", c=64)), w=[('bt', hb)])
                P.op('dve', lambda e: e.tensor_tensor(
                    out=bt[hb][:], in0=bt[hb][:], in1=nmask[:, :].unsqueeze(1).broadcast_to([128, 16, 64]), op=ALU.add),
                    r=[('bt', hb), 'nmask'], w=[('bt', hb)])
                P.op('act', lambda e: e.activation(out=zh[hb][:], in_=zh[hb][:], func=AF.Silu),
                     r=[('zh', hb)], w=[('zh', hb)])
                P.op('pool', lambda e: e.tensor_copy(out=btB0[hb][:], in_=bt[hb][:]), r=[('bt', hb)], w=[('btB0', hb)])
                P.op('pool', lambda e: e.memset(btB0[hb][:, 12, :], NEG), r=[('btB0', hb)], w=[('btB0', hb)])
                P.op('pool', lambda e: e.tensor_copy(out=btB1[hb][:], in_=bt[hb][:]), r=[('bt', hb)], w=[('btB1', hb)])
                P.op('pool', lambda e: e.memset(btB1[hb][0:64, 3, :], NEG), r=[('btB1', hb)], w=[('btB1', hb)])
                P.op('pool', lambda e: e.memset(btB1[hb][64:128, 11, :], NEG), r=[('btB1', hb)], w=[('btB1', hb)])

            def head_store(h):
                hb = h % 2
                P.dma('sp', lambda e: e.dma_start(
                    out=CAT[t_first * 128:, h * 64:(h + 1) * 64].rearrange("(t p) d -> p t d", p=128), in_=oh[hb][:, t_first:, :]),
                    r=[('oh', hb, tt) for tt in range(t_first, NT)], w=[('CATa', h)])

            def geom(ti):
                if ti < 2:
                    return [], 'ctx'
                r = 2 * (ti - 2)
                rs0 = min(max(r - 4, 0), 56)
                rs1 = min(max(r - 3, 0), 56)
                m0 = rs0 // 2
                if rs1 == rs0:
                    return [2 + m0 + j for j in range(4)], 'A'
                return [2 + m0 + j for j in range(5)], 'B'

            def stage1(u, i2):
                h, ti = u
                hb = h % 2
                bx, by = 2 * i2, 2 * i2 + 1
                loc, case = geom(ti)
                q0 = ti * 128
                kq = [('kh', hb), ('qh', hb)]
                for j, tix in enumerate(loc[:4]):
                    P.op('pe', lambda e, j=j, tix=tix: e.matmul(
                        psum[:, bx * 512 + j * 128: bx * 512 + (j + 1) * 128],
                        lhsT=kh[hb][:, tix * 128:(tix + 1) * 128], rhs=qh[hb][:, q0:q0 + 128], start=True, stop=True),
                        r=kq, w=[('pb', bx)])
                if case == 'B':
                    tix = loc[4]
                    P.op('pe', lambda e, tix=tix: e.matmul(
                        psum[:, by * 512: by * 512 + 128],
                        lhsT=kh[hb][:, tix * 128:(tix + 1) * 128], rhs=qh[hb][:, q0:q0 + 128], start=True, stop=True),
                        r=kq, w=[('pb', by)])
                for j in range(2):
                    P.op('pe', lambda e, j=j: e.matmul(
                        psum[:, by * 512 + 128 + j * 128: by * 512 + 256 + j * 128],
                        lhsT=kh[hb][:, j * 128:(j + 1) * 128], rhs=qh[hb][:, q0:q0 + 128], start=True, stop=True),
                        r=kq, w=[('pb', by)])
                nloc = len(loc)
                ks = ('sT', i2)
                if case != 'ctx':
                    r = 2 * (ti - 2)
                    m0 = loc[0] - 2
                    d0i = 2 * m0 - r + 8
                    sT4 = sT[i2][:, 0:512].rearrange("p (a h c) -> p a h c", a=4, h=2, c=64)
                    ps4 = psum[:, bx * 512: bx * 512 + 512].rearrange("p (a h c) -> p a h c", a=4, h=2, c=64)
                    for hf in range(2):
                        di = d0i - hf
                        tab = bt[hb] if case == 'A' else (btB0[hb] if hf == 0 else btB1[hb])
                        ktab = ('bt', hb) if case == 'A' else (('btB0', hb) if hf == 0 else ('btB1', hb))
                        P.op('dve', lambda e, hf=hf, di=di, tab=tab: e.tensor_tensor(
                            out=sT4[:, :, hf, :], in0=ps4[:, :, hf, :], in1=tab[:, di:di + 7:2, :], op=ALU.add),
                            r=[('pb', bx), ktab], w=[ks])
                    if case == 'B':
                        di = d0i - 1 + 8
                        P.op('dve', lambda e, di=di: e.tensor_tensor(
                            out=sT[i2][:, 576:640], in0=psum[:, by * 512 + 64: by * 512 + 128], in1=btB1[hb][:, di, :], op=ALU.add),
                            r=[('pb', by), ('btB1', hb)], w=[ks])
                    P.op('act', lambda e: e.activation(out=pT[i2][:, 0:nloc * 128], in_=sT[i2][:, 0:nloc * 128], func=AF.Exp),
                         r=[ks, ('sTc', i2)], w=[('pT', i2)])
                P.op('act', lambda e: e.activation(
                    out=pT[i2][:, 640:896], in_=psum[:, by * 512 + 128: by * 512 + 384], func=AF.Exp),
                    r=[('pb', by), ('pT', i2)], w=[('pT', i2)])

            def stage2(u, i2, o2):
                h, ti = u
                hb = h % 2
                obk = 6 + o2
                loc, case = geom(ti)
                slots = [(j * 128, tix) for j, tix in enumerate(loc)] + [(640, 0), (768, 1)]
                ns = len(slots)
                for s_, (c0, tix) in enumerate(slots):
                    P.op('pe', lambda e, s_=s_, c0=c0, tix=tix: e.matmul(
                        bank(obk, 65), lhsT=pT[i2][:, c0:c0 + 128], rhs=vh[hb][:, tix, :],
                        start=(s_ == 0), stop=(s_ == ns - 1)),
                        r=[('pT', i2), ('vh', hb), ('vh1', hb)], w=[('pb', obk)])
                P.op('dve', lambda e: e.reciprocal(out=rc[i2][:], in_=psum[:, obk * 512 + 64: obk * 512 + 65]),
                     r=[('pb', obk)], w=[('rc', i2)])
                P.op('dve', lambda e: e.scalar_tensor_tensor(
                    out=oh[hb][:, ti, :], in0=bank(obk, 64), scalar=rc[i2][:, 0:1], in1=zh[hb][:, ti, :],
                    op0=ALU.mult, op1=ALU.mult),
                    r=[('pb', obk), ('rc', i2), ('zh', hb)], w=[('oh', hb, ti)])

            units = [(h, ti) for h in range(16) for ti in range(t_first, NT)]
            nU = len(units)
            head_load(0)
            head_load(1)
            for i in range(nU + LAG):
                if i < nU:
                    stage1(units[i], i % 3)
                j = i - LAG
                if j >= 0:
                    stage2(units[j], j % 3, j % 2)
                    if j == nU - 1 or units[j + 1][0] != units[j][0]:
                        hd = units[j][0]
                        head_store(hd)
                        if hd + 2 < 16:
                            head_load(hd + 2)
        P.barrier()

    def phase_gla(l):
        last = (l == n_layers - 1)
        with ExitStack() as es:
            sb = lambda n, s, d=F32: es.enter_context(nc.sbuf_tensor(U(n), s, d))
            tri = sb("tri", [128, 4, 128])
            w2a = sb("w2a", [17, 2, 256])
            cosb = sb("cosb", [128, 32, 64])
            sinb = sb("sinb", [128, 32, 64])
            gn = sb("gn", [128, 128])
            ost = sb("ost", [128, NT, 512])
            D2 = range(2)
            S = [sb("S", [64, 4, 128]) for _ in D2]
            tb = [[sb("tb", [128, 1568]) for _ in range(2)] for _ in D2]
            qk = [sb("qk", [128, 512]) for _ in D2]
            t2 = [sb("t2", [128, 512]) for _ in D2]
            qT = [sb("qT", [64, 4, 128]) for _ in D2]
            kT = [sb("kT", [64, 4, 128]) for _ in D2]
            lrT = [sb("lrT", [17, 128]) for _ in D2]
            ee = [sb("ee", [128, 256]) for _ in D2]
            spt = [sb("spt", [128, 256]) for _ in D2]
            ekd = [sb("ekd", [128, 256]) for _ in D2]
            kd = [sb("kd", [128, 256]) for _ in D2]
            eb = [sb("eb", [64, 4, 128]) for _ in D2]
            enb = [sb("enb", [64, 4, 128]) for _ in D2]
            qt = [sb("qt", [64, 4, 128]) for _ in D2]
            kt = [sb("kt", [64, 4, 128]) for _ in D2]
            am = [sb("am", [128, 4, 128]) for _ in D2]
            osum = [sb("osum", [128, 512]) for _ in D2]
            zt = [sb("zt", [128, 512]) for _ in D2]
            ob = [sb("ob", [128, 512]) for _ in D2]
            ssq = [sb("ssq", [128, 4]) for _ in D2]
            rstd = [sb("rstd", [128, 4]) for _ in D2]
            P.dma('sp', lambda e: e.dma_start(out=tri[:], in_=tri_d.rearrange("a p q -> p a q")), w=['tri'])
            P.dma('sp', lambda e: e.dma_start(out=w2a[:], in_=w2a_d[l].rearrange("d k n -> k d n")), w=['w2a'])
            P.dma('sp', lambda e: e.dma_start(out=cosb[:], in_=cos_d.rearrange("(t p) e -> p t e", p=128)), w=['cosb'])
            P.dma('sp', lambda e: e.dma_start(out=sinb[:], in_=sin_d.rearrange("(t p) e -> p t e", p=128)), w=['sinb'])
            P.dma('sp', lambda e: e.dma_start(out=gn[:], in_=gnorm_d[l].partition_broadcast(128)), w=['gn'])
            for d in D2:
                P.op('pool', lambda e, d=d: e.memset(lrT[d][:], 1.0), w=[('lrT1', d)])
                P.op('pool', lambda e, d=d: e.memset(S[d][:], 0.0), w=[('S', d, h) for h in range(4)])
            cnt = [0, 0]
            f2 = lambda t: t[:].rearrange("p h t -> p (h t)")
            order = [list(range(NT)), [1, 0] + list(range(NT - 1, 1, -1))]
            idx = [{t: i for i, t in enumerate(order[d])} for d in D2]

            pend = [[], []]

            def flush_store(dr):
                while pend[dr]:
                    tj = pend[dr].pop(0)
                    P.dma('sp', lambda e, tj=tj: e.dma_start(out=CAT[tj * 128:(tj + 1) * 128, 1024:1536], in_=ob[dr][:]),
                          r=[('ob', dr, h) for h in range(4)], w=[('CATb', tj)])

            def gla_tile(ti, dr):
                K = lambda n: (n, dr)
                B0, B1, B2, B3 = 4 * dr, 4 * dr + 1, 4 * dr + 2, 4 * dr + 3
                lat = ti >= 2
                lt = ti - 2
                tbb = tb[dr][cnt[dr] % 2]
                ktb = ('tb', dr, cnt[dr] % 2)
                cnt[dr] += 1
                P.dma('sp', lambda e: e.dma_start(out=tbb[:], in_=TM[ti * 128:(ti + 1) * 128, TM_QB:TM_QB + 1568]), w=[ktb])
                flush_store(dr)
                if lat:
                    src3 = tbb[:, 0:512].rearrange("p (h e) -> p h e", h=8)
                    P.op('pool', lambda e: e.tensor_tensor(
                        out=qk[dr][:].rearrange("p (h e) -> p h e", h=8), in0=src3,
                        in1=cosb[:, lt:lt + 1, :].broadcast_to([128, 8, 64]), op=ALU.mult),
                        r=[ktb, 'cosb'], w=[K('qk')])
                    src5 = tbb[:, 0:512].rearrange("p (h a b e) -> p h a b e", h=8, a=2, b=2, e=16)
                    t25 = t2[dr][:].rearrange("p (h a b e) -> p h a b e", h=8, a=2, b=2, e=16)
                    sin5 = sinb[:, lt, :].rearrange("p (a b e) -> p a b e", a=2, b=2, e=16)
                    for bsel in range(2):
                        P.op('pool', lambda e, bsel=bsel: e.tensor_tensor(
                            out=t25[:, :, :, bsel, :], in0=src5[:, :, :, 1 - bsel, :],
                            in1=sin5[:, :, bsel, :].unsqueeze(1).broadcast_to([128, 8, 2, 16]), op=ALU.mult),
                            r=[ktb, 'sinb'], w=[('t2', dr, bsel)])
                    yield
                    P.op('pool', lambda e: e.tensor_tensor(out=qk[dr][:], in0=qk[dr][:], in1=t2[dr][:], op=ALU.add),
                         r=[K('qk'), ('t2', dr, 0), ('t2', dr, 1)], w=[K('qk')])
                    src = qk[dr]
                    ksrc = K('qk')
                else:
                    src = tbb
                    ksrc = ktb
                P.op('pe', lambda e: e.transpose(bank(B2, 128, 0, 16), tbb[:, 1536 + 16 * dr:1536 + 16 * dr + 16], ident[:]),
                     r=[ktb, 'ident'], w=[('pb', B2)])
                yield
                P.op('dve', lambda e: e.tensor_copy(out=lrT[dr][0:16, :], in_=bank(B2, 128, 0, 16)),
                     r=[('pb', B2), ('lrT1', dr)], w=[K('lrT')])
                yield
                P.op('pe', lambda e: e.matmul(psum[:, B2 * 512 + 256: B2 * 512 + 512], lhsT=lrT[dr][0:17, :], rhs=w2a[0:17, dr, :], start=True, stop=True),
                     r=[K('lrT'), ('lrT1', dr), 'w2a'], w=[('pb', B2)])
                yield
                P.op('act', lambda e: e.activation(out=ee[dr][:], in_=psum[:, B2 * 512 + 256: B2 * 512 + 512], func=AF.Exp, scale=-1.0),
                     r=[('pb', B2)], w=[K('ee')])
                P.op('act', lambda e: e.activation(out=spt[dr][:], in_=ee[dr][:], func=AF.Ln, bias=one1[:, 0:1], scale=1.0),
                     r=[K('ee'), 'one1'], w=[K('spt')])
                yield
                for h in range(4):
                    P.op('pe', lambda e, h=h: e.transpose(psum[0:64, B0 * 512 + h * 128: B0 * 512 + (h + 1) * 128],
                                                          src[:, h * 64:(h + 1) * 64], ident[:]),
                         r=[ksrc, 'ident'], w=[('pb', B0)])
                for h in range(4):
                    P.op('pe', lambda e, h=h: e.transpose(psum[0:64, B1 * 512 + h * 128: B1 * 512 + (h + 1) * 128],
                                                          src[:, 256 + h * 64:256 + (h + 1) * 64], ident[:]),
                         r=[ksrc, 'ident'], w=[('pb', B1)])
                yield
                P.op('act', lambda e: e.activation(out=f2(qT[dr]), in_=bank(B0, 512, 0, 64), func=AF.Copy, scale=0.125),
                     r=[('pb', B0)], w=[K('qT')])
                P.op('act', lambda e: e.activation(out=f2(kT[dr]), in_=bank(B1, 512, 0, 64), func=AF.Copy),
                     r=[('pb', B1)], w=[K('kT')])
                yield
                P.op('pe', lambda e: e.matmul(bank(B0, 256), lhsT=tri[:, 2 * dr, :], rhs=spt[dr][:], start=True, stop=True),
                     r=['tri', K('spt')], w=[('pb', B0)])
                for h in range(4):
                    P.op('pe', lambda e, h=h: e.matmul(psum[0:64, B1 * 512 + h * 128: B1 * 512 + (h + 1) * 128],
                                                       lhsT=spt[dr][:, h * 64:(h + 1) * 64], rhs=tri[:, 2 * dr + 1, :], start=True, stop=True),
                         r=[K('spt'), 'tri'], w=[('pb', B1)])
                yield
                P.op('act', lambda e: e.activation(out=ekd[dr][:], in_=bank(B0, 256), func=AF.Exp, scale=-1.0 / 16),
                     r=[('pb', B0)], w=[K('ekd')])
                P.op('act', lambda e: e.activation(out=f2(eb[dr]), in_=bank(B1, 512, 0, 64), func=AF.Exp, scale=-1.0 / 16),
                     r=[('pb', B1)], w=[K('eb')])
                P.op('act', lambda e: e.activation(out=f2(enb[dr]), in_=bank(B1, 512, 0, 64), func=AF.Exp, scale=1.0 / 16),
                     r=[('pb', B1)], w=[K('enb')])
                yield
                P.op('dve', lambda e: e.tensor_tensor(out=kd[dr][:], in0=src[:, 256:512], in1=ekd[dr][:], op=ALU.mult),
                     r=[ksrc, K('ekd')], w=[K('kd')])
                P.op('dve', lambda e: e.tensor_tensor(out=f2(qt[dr]), in0=f2(qT[dr]), in1=f2(eb[dr]), op=ALU.mult),
                     r=[K('qT'), K('eb')], w=[K('qt')])
                P.op('dve', lambda e: e.tensor_tensor(out=f2(kt[dr]), in0=f2(kT[dr]), in1=f2(enb[dr]), op=ALU.mult),
                     r=[K('kT'), K('enb')], w=[K('kt')])
                yield
                need_o = lat or (not last)
                if need_o:
                    for h in range(4):
                        P.op('pe', lambda e, h=h: e.matmul(psum[:, B2 * 512 + h * 128: B2 * 512 + (h + 1) * 128],
                                                           lhsT=kt[dr][:, h, :], rhs=qt[dr][:, h, :], start=True, stop=True),
                             r=[K('kt'), K('qt')], w=[('pb', B2)])
                for h in range(4):
                    P.op('pe', lambda e, h=h: e.matmul(psum[0:64, B0 * 512 + h * 128: B0 * 512 + (h + 1) * 128],
                                                       lhsT=kd[dr][:, h * 64:(h + 1) * 64], rhs=tbb[:, 512 + h * 128:512 + (h + 1) * 128], start=True, stop=True),
                         r=[K('kd'), ktb], w=[('pb', B0)])
                yield
                if need_o:
                    P.op('dve', lambda e: e.tensor_tensor(
                        out=am[dr][:], in0=bank(B2).rearrange("p (h t) -> p h t", h=4),
                        in1=tri[:, 2 * dr + 1:2 * dr + 2, :].broadcast_to([128, 4, 128]), op=ALU.mult),
                        r=[('pb', B2), 'tri'], w=[K('am')])
                    yield
                    for h in range(4):
                        P.op('pe', lambda e, h=h: e.matmul(psum[:, B3 * 512 + h * 128: B3 * 512 + (h + 1) * 128],
                                                           lhsT=am[dr][:, h, :], rhs=tbb[:, 512 + h * 128:512 + (h + 1) * 128], start=True, stop=False),
                             r=[K('am'), ktb], w=[('pb', B3)])
                        P.op('pe', lambda e, h=h: e.matmul(psum[:, B3 * 512 + h * 128: B3 * 512 + (h + 1) * 128],
                                                           lhsT=qt[dr][:, h, :], rhs=S[dr][:, h, :], start=False, stop=True),
                             r=[K('qt'), ('S', dr, h)], w=[('pb', B3)])
                    yield
                col = 127 if dr == 0 else 0
                for h in range(4):
                    P.op('dve', lambda e, h=h: e.scalar_tensor_tensor(
                        out=S[dr][:, h, :], in0=S[dr][:, h, :], scalar=eb[dr][:, h, col:col + 1],
                        in1=psum[0:64, B0 * 512 + h * 128: B0 * 512 + (h + 1) * 128], op0=ALU.mult, op1=ALU.add),
                        r=[('S', dr, h), K('eb'), ('pb', B0)], w=[('S', dr, h)])
                yield
                if not need_o:
                    return
                first = idx[dr][ti] < idx[1 - dr][ti]
                if first:
                    P.op('act', lambda e: e.activation(out=ost[:, ti, :], in_=bank(B3), func=AF.Copy), r=[('pb', B3)], w=[('ost', ti)])
                    yield
                    return
                P.op('dve', lambda e: e.tensor_tensor(out=osum[dr][:], in0=bank(B3), in1=ost[:, ti, :], op=ALU.add),
                     r=[('pb', B3), ('ost', ti)], w=[K('osum')])
                kob = [('ob', dr, h) for h in range(4)]
                zb = tbb[:, 1024:1536]
                for h in range(4):
                    P.op('act', lambda e, h=h: e.activation(out=ob[dr][:, h * 128:(h + 1) * 128], in_=osum[dr][:, h * 128:(h + 1) * 128],
                                                          func=AF.Square, accum_out=ssq[dr][:, h:h + 1]),
                         r=[K('osum')] + kob, w=[('ob', dr, h), ('ssq', dr, h)])
                yield
                P.op('act', lambda e: e.activation(out=zt[dr][:], in_=zb, func=AF.Silu), r=[ktb], w=[K('zt')])
                yield
                P.op('act', lambda e: e.activation(out=rstd[dr][:], in_=ssq[dr][:], func=AF.Ln, bias=eps6[:, 0:1], scale=1.0 / 128),
                     r=[('ssq', dr, h) for h in range(4)] + ['eps6'], w=[K('rstd')])
                P.op('act', lambda e: e.activation(out=rstd[dr][:], in_=rstd[dr][:], func=AF.Exp, scale=-0.5), r=[K('rstd')], w=[K('rstd')])
                yield
                P.op('pool', lambda e: e.tensor_tensor(
                    out=zt[dr][:].rearrange("p (h t) -> p h t", h=4), in0=zt[dr][:].rearrange("p (h t) -> p h t", h=4),
                    in1=gn[:, :].unsqueeze(1).broadcast_to([128, 4, 128]), op=ALU.mult), r=[K('zt'), 'gn'], w=[K('zt')])
                yield
                for h in range(4):
                    P.op('dve', lambda e, h=h: e.scalar_tensor_tensor(
                        out=ob[dr][:, h * 128:(h + 1) * 128], in0=osum[dr][:, h * 128:(h + 1) * 128], scalar=rstd[dr][:, h:h + 1],
                        in1=zt[dr][:, h * 128:(h + 1) * 128], op0=ALU.mult, op1=ALU.mult),
                        r=[K('osum'), K('rstd'), K('zt')], w=[('ob', dr, h)])
                pend[dr].append(ti)
                yield

            def chain(dr):
                for ti in order[dr]:
                    yield from gla_tile(ti, dr)
                flush_store(dr)

            gens = [chain(0), chain(1)]
            while gens:
                for g in list(gens):
                    try:
                        next(g)
                    except StopIteration:
                        gens.remove(g)
        P.barrier()

    def phase_conv(l):
        last = (l == n_layers - 1)
        with ExitStack() as es:
            sb = lambda n, s, d=F32: es.enter_context(nc.sbuf_tensor(U(n), s, d))
            cg = sb("cg", [128, T])
            upad = [sb("upad", [128, T + 60]) for _ in range(2)]
            ptmp = [sb("ptmp", [128, 4096]) for _ in range(2)]
            accP = sb("accP", [128, T])
            acc = sb("acc", [128, 4, T])
            cw = sb("cw", [128, 4, 31])
            cbias = sb("cbias", [128, 4])
            lg = sb("lg", [128, 512])
            lb = sb("lb", [128, 512])
            zc = [sb("zc", [128, 512]) for _ in range(2)]
            xn = sb("xn", [128, 512])
            oc = [sb("oc", [128, 512]) for _ in range(2)]
            stats = sb("cstats", [128, 6])
            mv = sb("cmv", [128, 2])
            rs = sb("crs", [128, 1])
            nmr = sb("cnmr", [128, 1])
            P.dma('sp', lambda e: e.dma_start(out=cw[:], in_=cwT_d[l].rearrange("(c p) k -> p c k", p=128)), w=['cw'])
            P.dma('sp', lambda e: e.dma_start(out=cbias[:], in_=cb_d[l]), w=['cbias'])
            P.dma('sp', lambda e: e.dma_start(out=lg[:], in_=clg_d[l].partition_broadcast(128)), w=['lg'])
            P.dma('sp', lambda e: e.dma_start(out=lb[:], in_=clb_d[l].partition_broadcast(128)), w=['lb'])
            for i in range(2):
                P.op('pool', lambda e, i=i: e.memset(upad[i][:], 0.0), w=[('upad', i)])
            segs = [(301, 256, 4096)] + ([] if last else [(15, 0, 256)])

            pcount = [0]
            NDT = 11

            def conv_load(cc):
                b2 = cc % 2
                up = upad[b2]
                ku = ('upad', b2)
                for (u0, a0, n) in segs:
                    P.dma('sp', lambda e, u0=u0, a0=a0, n=n: e.dma_start(out=up[:, u0:u0 + n], in_=CT[cc * 128:(cc + 1) * 128, a0:a0 + n]),
                          r=[], w=[ku])
                P.dma('sp', lambda e: e.dma_start(out=cg[:], in_=CT[512 + cc * 128:512 + (cc + 1) * 128, :]), w=['cg'])

            def conv_chunk(cc):
                b2 = cc % 2
                up = upad[b2]
                ku = ('upad', b2)
                P.op('act', lambda e: e.activation(out=cg[:], in_=cg[:], func=AF.Sigmoid), r=['cg'], w=['cg'])
                for (u0, a0, n) in segs:
                    P.op('dve', lambda e, u0=u0, a0=a0, n=n: e.tensor_tensor(
                        out=up[:, u0:u0 + n], in0=up[:, u0:u0 + n], in1=cg[:, a0:a0 + n], op=ALU.mult),
                        r=[ku, 'cg'], w=[ku])
                if cc + 1 < 4:
                    conv_load(cc + 1)
                for (u0, a0, n) in segs:
                    ka = ('acc', cc, a0)
                    kp = ('accP', a0)
                    P.op('dve', lambda e, u0=u0, a0=a0, n=n: e.tensor_scalar(
                        out=acc[:, cc, a0:a0 + n], in0=up[:, u0 - 15:u0 - 15 + n], scalar1=cw[:, cc, 0:1], scalar2=cbias[:, cc:cc + 1],
                        op0=ALU.mult, op1=ALU.add), r=[ku, 'cw', 'cbias'], w=[ka])
                    for k in range(1, NDT):
                        P.op('dve', lambda e, u0=u0, a0=a0, n=n, k=k: e.scalar_tensor_tensor(
                            out=acc[:, cc, a0:a0 + n], in0=up[:, u0 - 15 + k:u0 - 15 + k + n], scalar=cw[:, cc, k:k + 1],
                            in1=acc[:, cc, a0:a0 + n], op0=ALU.mult, op1=ALU.add), r=[ku, 'cw', ka], w=[ka])
                    for k in range(NDT, 31):
                        if k == NDT:
                            P.op('act', lambda e, u0=u0, a0=a0, n=n, k=k: e.activation(
                                out=accP[:, a0:a0 + n], in_=up[:, u0 - 15 + k:u0 - 15 + k + n], func=AF.Identity, scale=cw[:, cc, k:k + 1]),
                                r=[ku, 'cw'], w=[kp])
                            continue
                        pj = pcount[0] % 2
                        pcount[0] += 1
                        pt = ptmp[pj]
                        P.op('act', lambda e, u0=u0, n=n, k=k, pt=pt: e.activation(
                            out=pt[:, 0:n], in_=up[:, u0 - 15 + k:u0 - 15 + k + n], func=AF.Identity, scale=cw[:, cc, k:k + 1]),
                            r=[ku, 'cw'], w=[('ptmp', pj)])
                        P.op('pool', lambda e, a0=a0, n=n, pt=pt: e.tensor_tensor(
                            out=accP[:, a0:a0 + n], in0=accP[:, a0:a0 + n], in1=pt[:, 0:n], op=ALU.add),
                            r=[('ptmp', pj), kp], w=[kp])
                    P.op('pool', lambda e, a0=a0, n=n: e.tensor_tensor(
                        out=acc[:, cc, a0:a0 + n], in0=acc[:, cc, a0:a0 + n], in1=accP[:, a0:a0 + n], op=ALU.add),
                        r=[ka, kp], w=[ka])

            conv_load(0)
            for cc in range(4):
                conv_chunk(cc)
            tiles = list(range(2, NT)) if last else list(range(NT))
            xn2 = [xn, sb("xn_b", [128, 512])]
            st2 = [stats, sb("cstats_b", [128, 6])]
            mv2 = [mv, sb("cmv_b", [128, 2])]
            rs2 = [rs, sb("crs_b", [128, 1])]
            nm2 = [nmr, sb("cnmr_b", [128, 1])]

            def postA(n_):
                ti = tiles[n_]
                b = n_ % 2
                a0 = 0 if ti < 2 else 256
                P.dma('sp', lambda e: e.dma_start(out=zc[b][:], in_=TM[ti * 128:(ti + 1) * 128, TM_ZC:TM_ZC + 512]), w=[('zc', b)])
                for cc in range(4):
                    P.op('pe', lambda e, cc=cc: e.transpose(psum[:, b * 512 + cc * 128: b * 512 + (cc + 1) * 128],
                                                          acc[:, cc, ti * 128:(ti + 1) * 128], ident[:]),
                         r=[('acc', cc, a0), 'ident'], w=[('pb', b)])
                P.op('dve', lambda e: e.bn_stats(out=st2[b][:], in_=bank(b)), r=[('pb', b)], w=[('cstats', b)])
                P.op('dve', lambda e: e.bn_aggr(out=mv2[b][:], in_=st2[b][:]), r=[('cstats', b)], w=[('cmv', b)])
                P.op('act', lambda e: e.activation(out=rs2[b][:], in_=mv2[b][:, 1:2], func=AF.Sqrt, bias=eps5[:, 0:1], scale=1.0),
                     r=[('cmv', b), 'eps5'], w=[('crs', b)])
                P.op('dve', lambda e: e.reciprocal(out=rs2[b][:], in_=rs2[b][:]), r=[('crs', b)], w=[('crs', b)])
                P.op('dve', lambda e: e.tensor_scalar(out=nm2[b][:], in0=mv2[b][:, 0:1], scalar1=rs2[b][:, 0:1], scalar2=-1.0, op0=ALU.mult, op1=ALU.mult),
                     r=[('cmv', b), ('crs', b)], w=[('cnmr', b)])
                P.op('act', lambda e: e.activation(out=xn2[b][:], in_=bank(b), func=AF.Identity, bias=nm2[b][:, 0:1], scale=rs2[b][:, 0:1]),
                     r=[('pb', b), ('crs', b), ('cnmr', b)], w=[('xn', b)])

            def postB(n_):
                ti = tiles[n_]
                b = n_ % 2
                P.op('dve', lambda e: e.tensor_tensor(out=xn2[b][:], in0=xn2[b][:], in1=lg[:], op=ALU.mult), r=[('xn', b), 'lg'], w=[('xn', b)])
                P.op('pool', lambda e: e.tensor_tensor(out=xn2[b][:], in0=xn2[b][:], in1=lb[:], op=ALU.add), r=[('xn', b), 'lb'], w=[('xn', b)])
                P.op('act', lambda e: e.activation(out=xn2[b][:], in_=xn2[b][:], func=AF.Silu), r=[('xn', b)], w=[('xn', b)])
                P.op('act', lambda e: e.activation(out=zc[b][:], in_=zc[b][:], func=AF.Silu), r=[('zc', b)], w=[('zc', b)])
                P.op('dve', lambda e: e.tensor_tensor(out=oc[b][:], in0=xn2[b][:], in1=zc[b][:], op=ALU.mult), r=[('xn', b), ('zc', b)], w=[('oc', b)])
                P.dma('sp', lambda e: e.dma_start(out=CAT[ti * 128:(ti + 1) * 128, 1536:2048], in_=oc[b][:]), r=[('oc', b)], w=[('CATc', ti)])

            nTl = len(tiles)
            postA(0)
            for n_ in range(nTl):
                if n_ + 1 < nTl:
                    postA(n_ + 1)
                postB(n_)
        P.barrier()

    def phase_out(l):
        last = (l == n_layers - 1)
        src = xin if l == 0 else X1
        with ExitStack() as es:
            sb = lambda n, s, d=F32: es.enter_context(nc.sbuf_tensor(U(n), s, d))
            wo = sb("wo", [128, 16, D], BF16)
            gate = [sb("gate", [128, D]) for _ in range(2)]
            plg = sb("plg", [128, D])
            plb = sb("plb", [128, D])
            catt = [sb("catt", [128, D]) for _ in range(2)]
            xres = [sb("xres", [128, D]) for _ in range(2)]
            catT = [sb("catT", [128, 16, 128], BF16) for _ in range(2)]
            rt = [sb("rt", [128, D]) for _ in range(2)]
            stats = [sb("ostats", [128, 4, 6]) for _ in range(2)]
            mv = [sb("omv", [128, 2]) for _ in range(2)]
            rs = [sb("ors", [128, 1]) for _ in range(2)]
            nmr = [sb("onmr", [128, 1]) for _ in range(2)]
            wv = w_out[l].rearrange("(kc p) n -> p kc n", p=128)
            wstg = sb("wstg", [128, 16, 512])
            for ng in range(4):
                for hf in range(2):
                    P.dma('sp', lambda e, ng=ng, hf=hf: e.dma_start(out=wstg[:, hf * 8:(hf + 1) * 8, :], in_=wv[:, hf * 8:(hf + 1) * 8, ng * 512:(ng + 1) * 512]),
                          w=[('wstg', hf)])
                P.op('act', lambda e, ng=ng: e.activation(out=wo[:, 0:8, ng * 512:(ng + 1) * 512], in_=wstg[:, 0:8, :], func=AF.Copy),
                     r=[('wstg', 0)], w=[('wo', ng, 0)])
                P.op('dve', lambda e, ng=ng: e.tensor_copy(out=wo[:, 8:16, ng * 512:(ng + 1) * 512], in_=wstg[:, 8:16, :]),
                     r=[('wstg', 1)], w=[('wo', ng, 1)])
            for wh in range(2):
                P.dma('sp', lambda e, wh=wh: e.dma_start(out=gate[wh][:], in_=MOD[l, wh, 2 * D:3 * D].partition_broadcast(128)), w=[('gate', wh)])
            P.dma('sp', lambda e: e.dma_start(out=plg[:], in_=plg_d[l].partition_broadcast(128)), w=['plg'])
            P.dma('sp', lambda e: e.dma_start(out=plb[:], in_=plb_d[l].partition_broadcast(128)), w=['plb'])
            tiles = list(range(2, NT)) if last else list(range(NT))

            def stageT(n_):
                ti = tiles[n_]
                b = n_ % 2
                P.dma('sp', lambda e: e.dma_start(out=catt[b][:], in_=CAT[ti * 128:(ti + 1) * 128, :]), w=[('catt', b)])
                P.dma('sp', lambda e: e.dma_start(out=xres[b][:], in_=src[ti * 128:(ti + 1) * 128, :]), w=[('xres', b)])
                for kc in range(16):
                    bk = 4 + kc // 4
                    sl = psum[:, bk * 512 + (kc % 4) * 128: bk * 512 + (kc % 4 + 1) * 128]
                    P.op('pe', lambda e, kc=kc, sl=sl: e.transpose(sl, catt[b][:, kc * 128:(kc + 1) * 128], ident[:]),
                         r=[('catt', b), 'ident'], w=[('pb', bk)])
                for j in range(4):
                    bk = 4 + j
                    dstv = catT[b][:, 4 * j:4 * j + 4, :].rearrange("p a t -> p (a t)")
                    if j % 2 == 0:
                        P.op('act', lambda e, bk=bk, dstv=dstv: e.activation(out=dstv, in_=bank(bk), func=AF.Copy), r=[('pb', bk)], w=[('catT', b, j)])
                    else:
                        P.op('dve', lambda e, bk=bk, dstv=dstv: e.tensor_copy(out=dstv, in_=bank(bk)), r=[('pb', bk)], w=[('catT', b, j)])

            def stageM(n_):
                ti = tiles[n_]
                b = n_ % 2
                wh = 1 if ti < 2 else 0
                for ng in range(4):
                    for kc in range(16):
                        P.op('pe', lambda e, ng=ng, kc=kc: e.matmul(bank(ng), lhsT=catT[b][:, kc, :], rhs=wo[:, kc, ng * 512:(ng + 1) * 512],
                                                                  start=(kc == 0), stop=(kc == 15)),
                             r=[('catT', b, kc // 4), ('wo', ng, kc // 8)], w=[('pb', ng)])
                    P.op('dve', lambda e, ng=ng: e.tensor_tensor(
                        out=rt[b][:, ng * 512:(ng + 1) * 512], in0=bank(ng), in1=gate[wh][:, ng * 512:(ng + 1) * 512], op=ALU.mult),
                        r=[('pb', ng), ('gate', wh)], w=[('rt', b, ng)])
                krt = [('rt', b, ng) for ng in range(4)]
                P.op('dve', lambda e: e.scalar_tensor_tensor(out=rt[b][:], in0=xres[b][:], scalar=ALPHA, in1=rt[b][:], op0=ALU.mult, op1=ALU.add),
                     r=krt + [('xres', b)], w=krt)
                for q in range(4):
                    P.op('dve', lambda e, q=q: e.bn_stats(out=stats[b][:, q, :], in_=rt[b][:, q * 512:(q + 1) * 512]), r=krt, w=[('ostats', b, q)])
                P.op('dve', lambda e: e.bn_aggr(out=mv[b][:], in_=stats[b][:].rearrange("p a b -> p (a b)")),
                     r=[('ostats', b, q) for q in range(4)], w=[('omv', b)])
                P.op('act', lambda e: e.activation(out=rs[b][:], in_=mv[b][:, 1:2], func=AF.Sqrt, bias=eps5[:, 0:1], scale=1.0),
                     r=[('omv', b), 'eps5'], w=[('ors', b)])
                P.op('dve', lambda e: e.reciprocal(out=rs[b][:], in_=rs[b][:]), r=[('ors', b)], w=[('ors', b)])
                P.op('dve', lambda e: e.tensor_scalar(out=nmr[b][:], in0=mv[b][:, 0:1], scalar1=rs[b][:, 0:1], scalar2=-1.0, op0=ALU.mult, op1=ALU.mult),
                     r=[('omv', b), ('ors', b)], w=[('onmr', b)])
                P.op('act', lambda e: e.activation(out=rt[b][:], in_=rt[b][:], func=AF.Identity, bias=nmr[b][:, 0:1], scale=rs[b][:, 0:1]),
                     r=krt + [('ors', b), ('onmr', b)], w=krt)
                P.op('pool', lambda e: e.tensor_tensor(out=rt[b][:], in0=rt[b][:], in1=plg[:], op=ALU.mult), r=krt + ['plg'], w=krt)
                P.op('pool', lambda e: e.tensor_tensor(out=rt[b][:], in0=rt[b][:], in1=plb[:], op=ALU.add), r=krt + ['plb'], w=krt)

            def stageS(n_):
                ti = tiles[n_]
                b = n_ % 2
                krt = [('rt', b, ng) for ng in range(4)]
                dstr = out[(ti - 2) * 128:(ti - 1) * 128, :] if last else X1[ti * 128:(ti + 1) * 128, :]
                P.dma('sp', lambda e: e.dma_start(out=dstr, in_=rt[b][:]), r=krt, w=[('xo', ti)])

            nTl = len(tiles)
            stageT(0)
            for n_ in range(nTl):
                if n_ + 1 < nTl:
                    stageT(n_ + 1)
                if n_ >= 1:
                    stageS(n_ - 1)
                stageM(n_)
            stageS(nTl - 1)
        P.barrier()

    eps5 = nc.alloc_sbuf_tensor("eps5", [128, 1], F32)
    P.op('pool', lambda e: e.memset(eps5[:], 1e-5), w=['eps5'])
    eps6 = nc.alloc_sbuf_tensor("eps6", [128, 1], F32)
    P.op('pool', lambda e: e.memset(eps6[:], 1e-6), w=['eps6'])
    one1 = nc.alloc_sbuf_tensor("one1", [128, 1], F32)
    P.op('pool', lambda e: e.memset(one1[:], 1.0), w=['one1'])

    phases = []
    phases.append(('mod', phase_mod))
    for l in range(n_layers):
        phases.append((f'inproj{l}', lambda l=l: phase_inproj(l)))
        phases.append((f'na{l}', lambda l=l: phase_na(l)))
        phases.append((f'gla{l}', lambda l=l: phase_gla(l)))
        phases.append((f'conv{l}', lambda l=l: phase_conv(l)))
        phases.append((f'out{l}', lambda l=l: phase_out(l)))
    import os as _os
    skip = (_os.environ.get('KT_SKIP') or '').split(',')
    for name, fn in phases:
        if name in skip:
            continue
        fn()
        if stop_after == name:
            break
    P.emit()
    global _P
    _P = P
    return nc


def host_inputs(inputs, b):
    f = np.float32
    x = np.asarray(inputs['x'], f)
    ctx = np.asarray(inputs['ctx'], f)
    c = np.asarray(inputs['c'], f)
    c_ctx = np.asarray(inputs['c_ctx'], f)
    m = {}
    m['xin'] = np.ascontiguousarray(np.concatenate([ctx[b], x[b]], axis=0))
    cv = np.stack([c[b], c_ctx], axis=0)
    m['cT'] = np.ascontiguousarray(cv.reshape(2, 16, 128).transpose(2, 0, 1))
    m['w_ada'] = np.asarray(inputs['w_ada'], f)
    m['b_ada'] = np.asarray(inputs['b_ada'], f)
    m['w_in'] = np.asarray(inputs['w_in'], f)
    rpb = np.asarray(inputs['rpb'], f)
    p = np.arange(128)
    cp = p % 64
    half = p // 64
    d0 = np.arange(16) - 8
    cq = np.arange(64)
    dd = np.clip(d0[None, :] + half[:, None] + 7, 0, 14)
    dc = np.clip(cp[:, None] - cq[None, :] + 15, 0, 30)
    bt = rpb[:, :, dd[:, :, None], dc[:, None, :]]
    m['btab'] = np.ascontiguousarray(bt.reshape(2, 16, 128, 1024))
    cs = np.clip(cq - 8, 0, 48)
    ok = (cp[:, None] >= cs[None, :]) & (cp[:, None] < cs[None, :] + 16)
    m['nmask'] = np.where(ok, 0.0, NEG).astype(f)
    w2 = np.asarray(inputs['gla_w2'], f)
    gb = np.asarray(inputs['gla_b'], f)
    m['w2a'] = np.ascontiguousarray(np.concatenate([w2, gb[:, :, None, :]], axis=2))
    m['gnorm'] = np.asarray(inputs['gla_norm'], f)
    m['cwT'] = np.ascontiguousarray(np.asarray(inputs['conv_w'], f).transpose(0, 2, 1))
    m['cb'] = np.ascontiguousarray(np.asarray(inputs['conv_b'], f).reshape(2, 4, 128).transpose(0, 2, 1))
    m['clg'] = np.asarray(inputs['conv_ln_g'], f)
    m['clb'] = np.asarray(inputs['conv_ln_b'], f)
    m['w_out'] = np.asarray(inputs['w_out'], f)
    m['plg'] = np.asarray(inputs['post_ln_g'], f)
    m['plb'] = np.asarray(inputs['post_ln_b'], f)
    pos = np.arange(4096)
    rows = (pos // 64).astype(f)
    colsp = (pos % 64).astype(f)
    inv = (10000.0 ** (-np.arange(0, 32, 2, dtype=f) / 32)).astype(f)
    ar = rows[:, None] * inv[None, :]
    ac = colsp[:, None] * inv[None, :]
    m['ropecos'] = np.concatenate([np.cos(ar), np.cos(ar), np.cos(ac), np.cos(ac)], axis=1).astype(f)
    m['ropesin'] = np.concatenate([-np.sin(ar), np.sin(ar), -np.sin(ac), np.sin(ac)], axis=1).astype(f)
    m['ident'] = np.eye(128, dtype=f)
    j = np.arange(128)[:, None]
    i = np.arange(128)[None, :]
    m['tri'] = np.stack([(j > i), (j <= i), (j < i), (j >= i)]).astype(f)
    return m


_NC = None


def kernel(**inputs):
    global _NC
    if _NC is None:
        _NC = build()
    in_maps = [host_inputs(inputs, cid % 4) for cid in range(8)]
    res = run_bass_kernel_spmd(_NC, in_maps, core_ids=list(range(8)))
    return np.stack([res.results[b]["out"] for b in range(4)], axis=0).astype(np.float32)
```

```python
import numpy as np
from contextlib import ExitStack
import concourse.bass as bass
import concourse.mybir as mybir
from concourse.bass_utils import run_bass_kernel_spmd

F32 = mybir.dt.float32
BF16 = mybir.dt.bfloat16
AF = mybir.ActivationFunctionType
ALU = mybir.AluOpType

NDS = 12
D = 2048
NT = 34
T = NT * 128
NIN = 7200
ALPHA = 4.0 ** 0.25
NEG = -30000.0


class Prog:
    ENG = ('pe', 'act', 'dve', 'pool', 'sp')

    def __init__(self, nc):
        self.nc = nc
        self.ops = {e: [] for e in self.ENG}
        self.esem = {e: nc.alloc_semaphore(f"s_{e}") for e in self.ENG}
        self.ecnt = {e: 0 for e in self.ENG}
        self.known = {e: {} for e in self.ENG}
        self.dsem = {e: [nc.alloc_semaphore(f"d_{e}_{i}") for i in range(NDS)] for e in self.ENG}
        self.dcnt = {e: [0] * NDS for e in self.ENG}
        self.dnext = {e: 0 for e in self.ENG}
        self.res = {}
        self.semid = {}
        self.pending = {e: [] for e in self.ENG}

    def _deps(self, eng, r, w):
        deps = list(self.pending[eng])
        self.pending[eng] = []
        for k in r:
            st = self.res.get(k)
            if st and st[0] is not None:
                deps.append(st[0])
        for k in w:
            st = self.res.get(k)
            if st:
                if st[0] is not None:
                    deps.append(st[0])
                deps.extend(st[1])
        best = {}
        for (sem, val, peng, isdma) in deps:
            if (not isdma) and peng == eng and eng == 'pe':
                continue
            key = id(sem)
            self.semid[key] = sem
            if best.get(key, 0) < val:
                best[key] = val
        waits = []
        kn = self.known[eng]
        for key, val in best.items():
            if kn.get(key, 0) >= val:
                continue
            kn[key] = val
            waits.append((self.semid[key], val))
        return waits

    def _commit(self, tok, r, w):
        for k in r:
            st = self.res.get(k)
            if st is None:
                st = [None, []]
                self.res[k] = st
            st[1].append(tok)
        for k in w:
            self.res[k] = [tok, []]

    def op(self, eng, fn, r=(), w=()):
        waits = self._deps(eng, r, w)
        self.ecnt[eng] += 1
        sem = self.esem[eng]
        tok = (sem, self.ecnt[eng], eng, False)
        self.ops[eng].append((waits, fn, (sem, 1)))
        self._commit(tok, r, w)

    def dma(self, eng, fn, r=(), w=()):
        waits = self._deps(eng, r, w)
        j = self.dnext[eng]
        self.dnext[eng] = (j + 1) % NDS
        sem = self.dsem[eng][j]
        prev = self.dcnt[eng][j]
        self.semid[id(sem)] = sem
        if prev > 0 and self.known[eng].get(id(sem), 0) < 16 * prev:
            self.known[eng][id(sem)] = 16 * prev
            waits.append((sem, 16 * prev))
        self.dcnt[eng][j] = prev + 1
        tok = (sem, 16 * (prev + 1), eng, True)
        self.ops[eng].append((waits, fn, (sem, 16)))
        self._commit(tok, r, w)

    def _all_tokens(self):
        toks = []
        for e in self.ENG:
            if self.ecnt[e] > 0:
                toks.append((self.esem[e], self.ecnt[e], e, True))
            for j in range(NDS):
                if self.dcnt[e][j] > 0:
                    toks.append((self.dsem[e][j], 16 * self.dcnt[e][j], e, True))
        return toks

    def barrier(self):
        toks = self._all_tokens()
        for e in self.ENG:
            self.pending[e].extend(toks)
        self.res = {}

    def emit(self):
        nc = self.nc
        fin = [(s, v) for (s, v, _, _) in self._all_tokens()]
        with nc.Block() as block:
            secs = {'pe': block.tensor, 'act': block.scalar, 'dve': block.vector,
                    'pool': block.gpsimd, 'sp': block.sync}
            for e in self.ENG:
                ops = self.ops[e]
                f = fin if e == 'sp' else []

                def body(engine, ops=ops, f=f):
                    for (waits, fn, (sem, amt)) in ops:
                        for (s, v) in waits:
                            engine.wait_ge(s, v)
                        inst = fn(engine)
                        inst.then_inc(sem, amt)
                    for (s, v) in f:
                        engine.wait_ge(s, v)
                if ops or f:
                    secs[e](body)


GROUPS = [
    ('qt', 0, 512, 0), ('qt', 512, 512, 512),
    ('kt', 1024, 512, 0), ('kt', 1536, 512, 512),
    ('va', 2048, 512, 0), ('va', 2560, 512, 512),
    ('tm', 3072, 512, 0), ('tm', 3584, 512, 512), ('tm', 4096, 512, 1024),
    ('tm', 4608, 512, 1536), ('tm', 5120, 512, 2048), ('tm', 5632, 32, 2560),
    ('ct', 5664, 512, 0), ('ct', 6176, 512, 512),
    ('tm', 6688, 512, 2592),
]
TMW = 3104
TM_ZA, TM_QB, TM_KB, TM_VB, TM_ZB, TM_LR, TM_ZC = 0, 1024, 1280, 1536, 2048, 2560, 2592


def build(n_layers=2, dbg=False, stop_after=None):
    nc = bass.Bass("TRN2", target_bir_lowering=False)

    def din(name, shape, dt=F32):
        return nc.dram_tensor(name, list(shape), dt, kind="ExternalInput").ap()

    def dscr(name, shape, dt=F32):
        return nc.dram_tensor(name, list(shape), dt, kind="ExternalOutput" if dbg else "Internal").ap()

    xin = din("xin", [T, D])
    cT_d = din("cT", [128, 2, 16])
    w_ada = din("w_ada", [2, D, 3 * D])
    b_ada = din("b_ada", [2, 3 * D])
    w_in = din("w_in", [2, D, NIN])
    btab = din("btab", [2, 16, 128, 1024])
    nmask_d = din("nmask", [128, 64])
    w2a_d = din("w2a", [2, 2, 17, 256])
    gnorm_d = din("gnorm", [2, 128])
    cwT_d = din("cwT", [2, 512, 31])
    cb_d = din("cb", [2, 128, 4])
    clg_d = din("clg", [2, 512])
    clb_d = din("clb", [2, 512])
    w_out = din("w_out", [2, D, D])
    plg_d = din("plg", [2, D])
    plb_d = din("plb", [2, D])
    cos_d = din("ropecos", [4096, 64])
    sin_d = din("ropesin", [4096, 64])
    ident_d = din("ident", [128, 128])
    tri_d = din("tri", [4, 128, 128])
    out = nc.dram_tensor("out", [4096, D], F32, kind="ExternalOutput").ap()

    MOD = dscr("MOD", [2, 2, 3 * D])
    QT = dscr("QT", [1024, T], BF16)
    KT = dscr("KT", [1024, T], BF16)
    VA = dscr("VA", [T, 1024], BF16)
    TM = dscr("TM", [T, TMW])
    CT = dscr("CT", [1024, T])
    CAT = dscr("CAT", [T, D])
    X1 = dscr("X1", [T, D])

    P = Prog(nc)
    psum = nc.alloc_psum_tensor("psum", [128, 4096], F32)

    def bank(i, n=512, p0=0, p1=128):
        return psum[p0:p1, i * 512:i * 512 + n]

    ident = nc.alloc_sbuf_tensor("ident_sb", [128, 128], F32)
    P.dma('sp', lambda e: e.dma_start(out=ident[:], in_=ident_d), w=['ident'])

    uid = [0]

    def U(s):
        uid[0] += 1
        return f"{s}_{uid[0]}"

    def phase_mod():
        with ExitStack() as es:
            sb = lambda n, s, d=F32: es.enter_context(nc.sbuf_tensor(U(n), s, d))
            cT = sb("cT", [128, 2, 16])
            scT = sb("scT", [128, 2, 16])
            L = sb("L", [128, 16, 128])
            wbuf = [sb("wada", [128, 16, 512]) for _ in range(4)]
            bada = sb("bada", [128, 3 * D])
            modrow = sb("modrow", [128, 3 * D])
            P.dma('sp', lambda e: e.dma_start(out=cT[:], in_=cT_d), w=['cT'])
            P.op('act', lambda e: e.activation(out=scT[:], in_=cT[:], func=AF.Silu), r=['cT'], w=['scT'])
            for wh in range(2):
                P.op('dve', lambda e, wh=wh: e.tensor_copy(
                    out=L[:, :, wh * 64:(wh + 1) * 64],
                    in_=scT[:, wh:wh + 1, :].rearrange("p o k -> p k o").broadcast_to([128, 16, 64])),
                    r=['scT'], w=[('L', wh)])
            gi = 0
            for l in range(n_layers):
                P.dma('sp', lambda e, l=l: e.dma_start(out=bada[:], in_=b_ada[l].partition_broadcast(128)),
                      r=[], w=['bada'])
                wv = w_ada[l].rearrange("(kc p) n -> p kc n", p=128)
                for ng in range(12):
                    wb = wbuf[gi % 4]
                    for hf in range(2):
                        P.dma('sp' if hf == 0 else 'act', lambda e, wb=wb, wv=wv, ng=ng, hf=hf: e.dma_start(
                            out=wb[:, hf * 8:(hf + 1) * 8, :], in_=wv[:, hf * 8:(hf + 1) * 8, ng * 512:(ng + 1) * 512]),
                            w=[('wada', gi % 4, hf)])
                    bk = gi % 2
                    for kc in range(16):
                        P.op('pe', lambda e, wb=wb, kc=kc, bk=bk: e.matmul(
                            bank(bk), lhsT=L[:, kc, :], rhs=wb[:, kc, :], start=(kc == 0), stop=(kc == 15)),
                            r=[('wada', gi % 4, kc // 8), ('L', 0), ('L', 1)], w=[('pb', bk)])
                    P.op('dve', lambda e, ng=ng, bk=bk: e.tensor_tensor(
                        out=modrow[:, ng * 512:(ng + 1) * 512], in0=bank(bk), in1=bada[:, ng * 512:(ng + 1) * 512], op=ALU.add),
                        r=[('pb', bk), 'bada'], w=['modrow'])
                    gi += 1
                for wh in range(2):
                    P.dma('sp', lambda e, l=l, wh=wh: e.dma_start(out=MOD[l, wh:wh + 1, :], in_=modrow[wh * 64:wh * 64 + 1, :]),
                          r=['modrow'], w=[('MOD', l)])
        P.barrier()

    def phase_inproj(l):
        src = xin if l == 0 else X1
        with ExitStack() as es:
            sb = lambda n, s, d=F32: es.enter_context(nc.sbuf_tensor(U(n), s, d))
            GTL = 17
            hT = sb("hT", [128, 16, GTL * 128], BF16)
            wg = [sb("wg", [128, 16, 512], BF16) for _ in range(2)]
            xt = [sb("xt", [128, D]) for _ in range(2)]
            st32 = [sb("st32", [128, 512]) for _ in range(4)]
            st16 = [sb("st16", [128, 512], BF16) for _ in range(4)]
            t16 = sb("t16", [16, 4, 128])
            cols = sb("cols", [128, 4, 16])
            stats = sb("stats", [128, 4, 6])
            mv = sb("mv", [128, 2])
            rs = sb("rs", [128, 1])
            nmr = sb("nmr", [128, 1])
            for wh in range(2):
                for j in range(2):
                    i = wh * 2 + j
                    P.dma('sp', lambda e, wh=wh, j=j, i=i: e.dma_start(
                        out=t16[:, i, :], in_=MOD[l, wh, j * D:(j + 1) * D].rearrange("(k p) -> k p", p=128)),
                        r=[('MOD', l)], w=[('t16', i)])
                    P.op('pe', lambda e, i=i: e.transpose(bank(0, 16), t16[:, i, :], ident[:16, :16]),
                         r=[('t16', i), 'ident'], w=[('pb', 0)])
                    if j == 0:
                        P.op('dve', lambda e, i=i: e.tensor_copy(out=cols[:, i, :], in_=bank(0, 16)),
                             r=[('pb', 0)], w=[('cols', i)])
                    else:
                        P.op('dve', lambda e, i=i: e.tensor_scalar(out=cols[:, i, :], in0=bank(0, 16), scalar1=1.0, scalar2=None, op0=ALU.add),
                             r=[('pb', 0)], w=[('cols', i)])
            if dbg and l == 0:
                DBGC2 = dscr('DBGC2', [128, 4, 16])
                P.dma('sp', lambda e: e.dma_start(out=DBGC2, in_=cols[:]), r=[('cols', i) for i in range(4)], w=['dbgc2'])
            wv = w_in[l].rearrange("(kc p) n -> p kc n", p=128)
            gcount = 0
            stc = [0]
            import os as _os
            for g in range(2 if _os.environ.get('KT_NOLN') is None else 0):
                for tl in range(GTL):
                    ti = g * GTL + tl
                    xb = xt[ti % 2]
                    kx = ('xt', ti % 2)
                    P.dma('sp', lambda e, xb=xb, ti=ti: e.dma_start(out=xb[:], in_=src[ti * 128:(ti + 1) * 128, :]),
                          w=[kx])
                    for q in range(4):
                        P.op('dve', lambda e, xb=xb, q=q: e.bn_stats(out=stats[:, q, :], in_=xb[:, q * 512:(q + 1) * 512]),
                             r=[kx], w=[('stats', q)])
                    P.op('dve', lambda e: e.bn_aggr(out=mv[:], in_=stats[:].rearrange("p a b -> p (a b)")),
                         r=[('stats', q) for q in range(4)], w=['mv'])
                    P.op('act', lambda e: e.activation(out=rs[:], in_=mv[:, 1:2], func=AF.Sqrt, bias=eps5[:, 0:1], scale=1.0),
                         r=['mv'], w=['rs'])
                    P.op('dve', lambda e: e.reciprocal(out=rs[:], in_=rs[:]), r=['rs'], w=['rs'])
                    P.op('dve', lambda e: e.tensor_scalar(out=nmr[:], in0=mv[:, 0:1], scalar1=rs[:, 0:1], scalar2=-1.0,
                                                          op0=ALU.mult, op1=ALU.mult), r=['mv', 'rs'], w=['nmr'])
                    P.op('act', lambda e, xb=xb: e.activation(out=xb[:], in_=xb[:], func=AF.Identity,
                                                              bias=nmr[:, 0:1], scale=rs[:, 0:1]),
                         r=[kx, 'rs', 'nmr'], w=[kx])
                    wh = 1 if ti < 2 else 0
                    for kc in range(16):
                        bk = 4 + kc // 4
                        sl = psum[:, bk * 512 + (kc % 4) * 128: bk * 512 + (kc % 4 + 1) * 128]
                        kp = ('pb', bk)
                        P.op('pe', lambda e, xb=xb, kc=kc, sl=sl: e.transpose(sl, xb[:, kc * 128:(kc + 1) * 128], ident[:]),
                             r=[kx, 'ident'], w=[kp])
                        dst = hT[:, kc, tl * 128:(tl + 1) * 128]
                        if kc % 2 == 0:
                            P.op('act', lambda e, dst=dst, sl=sl, wh=wh, kc=kc: e.activation(
                                out=dst, in_=sl, func=AF.Identity, bias=cols[:, wh * 2, kc:kc + 1], scale=cols[:, wh * 2 + 1, kc:kc + 1]),
                                r=[kp, ('cols', wh * 2), ('cols', wh * 2 + 1)], w=[('hT', tl, kc)])
                        else:
                            P.op('dve', lambda e, dst=dst, sl=sl, wh=wh, kc=kc: e.tensor_scalar(
                                out=dst, in0=sl, scalar1=cols[:, wh * 2 + 1, kc:kc + 1], scalar2=cols[:, wh * 2, kc:kc + 1],
                                op0=ALU.mult, op1=ALU.add),
                                r=[kp, ('cols', wh * 2), ('cols', wh * 2 + 1)], w=[('hT', tl, kc)])
                if dbg and g == 0 and l == 0:
                    DBGH = dscr('DBGH', [128, 16, GTL * 128], BF16)
                    DBGC = dscr('DBGC', [128, 4, 16])
                    P.dma('sp', lambda e: e.dma_start(out=DBGH, in_=hT[:]), r=[('hT', tl, kc) for tl in range(GTL) for kc in range(16)], w=['dbgh'])
                    P.dma('sp', lambda e: e.dma_start(out=DBGC, in_=cols[:]), r=[('cols', i) for i in range(4)], w=['dbgc'])
                    DBGT = dscr('DBGT', [16, 4, 128])
                    P.dma('sp', lambda e: e.dma_start(out=DBGT, in_=t16[:]), r=[('t16', i) for i in range(4)], w=['dbgt'])
                import os as _os
                for (kind, c0, ncols, doff) in (GROUPS if _os.environ.get('KT_NOPROJ') is None else []):
                    wb = wg[gcount % 2]
                    kw = ('wg', gcount % 2)
                    gcount += 1
                    P.dma('pool', lambda e, wb=wb, c0=c0, ncols=ncols: e.dma_start(out=wb[:, :, 0:ncols], in_=wv[:, :, c0:c0 + ncols]),
                          w=[kw])
                    if kind in ('tm', 'va'):
                        for tl in range(GTL):
                            ti = g * GTL + tl
                            bk = stc[0] % 4
                            for kc in range(16):
                                P.op('pe', lambda e, wb=wb, kc=kc, tl=tl, bk=bk, ncols=ncols: e.matmul(
                                    bank(bk, ncols), lhsT=hT[:, kc, tl * 128:(tl + 1) * 128], rhs=wb[:, kc, 0:ncols],
                                    start=(kc == 0), stop=(kc == 15)),
                                    r=[kw, ('hT', tl, kc)], w=[('pb', bk)])
                            si = stc[0] % 4
                            stg = (st32 if kind == 'tm' else st16)[si]
                            ks = ('st', kind == 'tm', si)
                            eng = 'act' if stc[0] % 2 == 0 else 'dve'
                            if eng == 'act':
                                P.op('act', lambda e, stg=stg, bk=bk, ncols=ncols: e.activation(out=stg[:, 0:ncols], in_=bank(bk, ncols), func=AF.Copy),
                                     r=[('pb', bk)], w=[ks])
                            else:
                                P.op('dve', lambda e, stg=stg, bk=bk, ncols=ncols: e.tensor_copy(out=stg[:, 0:ncols], in_=bank(bk, ncols)),
                                     r=[('pb', bk)], w=[ks])
                            dst = (TM if kind == 'tm' else VA)[ti * 128:(ti + 1) * 128, doff:doff + ncols]
                            P.dma('sp', lambda e, dst=dst, stg=stg, ncols=ncols: e.dma_start(out=dst, in_=stg[:, 0:ncols]),
                                  r=[ks], w=[(kind, ti, doff)])
                            stc[0] += 1
                    else:
                        dt_ = {'qt': QT, 'kt': KT, 'ct': CT}[kind]
                        for blk in range(ncols // 128):
                            tok0 = 0
                            while tok0 < GTL * 128:
                                ntok = min(512, GTL * 128 - tok0)
                                bk = stc[0] % 4
                                for kc in range(16):
                                    P.op('pe', lambda e, wb=wb, kc=kc, blk=blk, tok0=tok0, ntok=ntok, bk=bk: e.matmul(
                                        bank(bk, ntok), lhsT=wb[:, kc, blk * 128:(blk + 1) * 128], rhs=hT[:, kc, tok0:tok0 + ntok],
                                        start=(kc == 0), stop=(kc == 15)),
                                        r=[kw] + [('hT', tt, kc) for tt in range(tok0 // 128, (tok0 + ntok) // 128)], w=[('pb', bk)])
                                si = stc[0] % 4
                                is32 = kind == 'ct'
                                stg = (st32 if is32 else st16)[si]
                                ks = ('st', is32, si)
                                sc = 0.125 if kind == 'qt' else 1.0
                                if stc[0] % 2 == 0:
                                    P.op('act', lambda e, stg=stg, bk=bk, ntok=ntok, sc=sc: e.activation(
                                        out=stg[:, 0:ntok], in_=bank(bk, ntok), func=AF.Copy, scale=sc),
                                        r=[('pb', bk)], w=[ks])
                                else:
                                    P.op('dve', lambda e, stg=stg, bk=bk, ntok=ntok, sc=sc: e.tensor_scalar(
                                        out=stg[:, 0:ntok], in0=bank(bk, ntok), scalar1=sc, scalar2=None, op0=ALU.mult),
                                        r=[('pb', bk)], w=[ks])
                                r0 = doff + blk * 128
                                g0 = g * GTL * 128 + tok0
                                dst = dt_[r0:r0 + 128, g0:g0 + ntok]
                                P.dma('sp', lambda e, dst=dst, stg=stg, ntok=ntok: e.dma_start(out=dst, in_=stg[:, 0:ntok]),
                                      r=[ks], w=[(kind, r0, g0)])
                                stc[0] += 1
                                tok0 += ntok
        P.barrier()

    def phase_na(l):
        last = (l == n_layers - 1)
        LAG = 2
        with ExitStack() as es:
            sb = lambda n, s, d=F32: es.enter_context(nc.sbuf_tensor(U(n), s, d))
            qh = [sb("qh", [64, T], BF16) for _ in range(2)]
            kh = [sb("kh", [64, T], BF16) for _ in range(2)]
            vh = [sb("vh", [128, NT, 65], BF16) for _ in range(2)]
            zh = [sb("zh", [128, NT, 64]) for _ in range(2)]
            bt = [sb("bt", [128, 16, 64]) for _ in range(2)]
            oh = [sb("oh", [128, NT, 64]) for _ in range(2)]
            nmask = sb("nmask", [128, 64])
            btB0 = [sb("btB0", [128, 16, 64]) for _ in range(2)]
            btB1 = [sb("btB1", [128, 16, 64]) for _ in range(2)]
            sT = [sb("sT", [128, 640]) for _ in range(3)]
            pT = [sb("pT", [128, 896], BF16) for _ in range(3)]
            rc = [sb("rc", [128, 1]) for _ in range(3)]
            for i in range(3):
                P.op('pool', lambda e, i=i: e.memset(sT[i][:, 512:576], NEG), w=[('sTc', i)])
            P.dma('sp', lambda e: e.dma_start(out=nmask[:], in_=nmask_d), w=['nmask'])
            for i in range(2):
                P.op('pool', lambda e, i=i: e.memset(vh[i][:, :, 64:65], 1.0), w=[('vh1', i)])
            t_first = 2 if last else 0

            def head_load(h):
                hb = h % 2
                P.dma('sp', lambda e: e.dma_start(out=qh[hb][:], in_=QT[h * 64:(h + 1) * 64, :]), w=[('qh', hb)])
                P.dma('sp', lambda e: e.dma_start(out=kh[hb][:], in_=KT[h * 64:(h + 1) * 64, :]), w=[('kh', hb)])
                P.dma('sp', lambda e: e.dma_start(
                    out=vh[hb][:, :, 0:64], in_=VA[:, h * 64:(h + 1) * 64].rearrange("(t p) d -> p t d", p=128)), w=[('vh', hb)])
                P.dma('sp', lambda e: e.dma_start(
                    out=zh[hb][:], in_=TM[:, TM_ZA + h * 64:TM_ZA + (h + 1) * 64].rearrange("(t p) d -> p t d", p=128)), w=[('zh', hb)])
                P.dma('sp', lambda e: e.dma_start(
                    out=bt[hb][:], in_=btab[l, h].rearrange("p (a c) -> p a c", c=64)), w=[('bt', hb)])
                P.op('dve', lambda e: e.tensor_tensor(
                    out=bt[hb][:], in0=bt[hb][:], in1=nmask[:, :].unsqueeze(1).broadcast_to([128, 16, 64]), op=ALU.add),
                    r=[('bt', hb), 'nmask'], w=[('bt', hb)])
                P.op('act', lambda e: e.activation(out=zh[hb][:], in_=zh[hb][:], func=AF.Silu),
                     r=[('zh', hb)], w=[('zh', hb)])
                P.op('pool', lambda e: e.tensor_copy(out=btB0[hb][:], in_=bt[hb][:]), r=[('bt', hb)], w=[('btB0', hb)])
                P.op('pool', lambda e: e.memset(btB0[hb][:, 12, :], NEG), r=[('btB0', hb)], w=[('btB0', hb)])
                P.op('pool', lambda e: e.tensor_copy(out=btB1[hb][:], in_=bt[hb][:]), r=[('bt', hb)], w=[('btB1', hb)])
                P.op('pool', lambda e: e.memset(btB1[hb][0:64, 3, :], NEG), r=[('btB1', hb)], w=[('btB1', hb)])
                P.op('pool', lambda e: e.memset(btB1[hb][64:128, 11, :], NEG), r=[('btB1', hb)], w=[('btB1', hb)])

            def head_store(h):
                hb = h % 2
                P.dma('sp', lambda e: e.dma_start(
                    out=CAT[t_first * 128:, h * 64:(h + 1) * 64].rearrange("(t p) d -> p t d", p=128), in_=oh[hb][:, t_first:, :]),
                    r=[('oh', hb, tt) for tt in range(t_first, NT)], w=[('CATa', h)])

            def geom(ti):
                if ti < 2:
                    return [], 'ctx'
                r = 2 * (ti - 2)
                rs0 = min(max(r - 4, 0), 56)
                rs1 = min(max(r - 3, 0), 56)
                m0 = rs0 // 2
                if rs1 == rs0:
                    return [2 + m0 + j for j in range(4)], 'A'
                return [2 + m0 + j for j in range(5)], 'B'

            def stage1(u, i2):
                h, ti = u
                hb = h % 2
                bx, by = 2 * i2, 2 * i2 + 1
                loc, case = geom(ti)
                q0 = ti * 128
                kq = [('kh', hb), ('qh', hb)]
                for j, tix in enumerate(loc[:4]):
                    P.op('pe', lambda e, j=j, tix=tix: e.matmul(
                        psum[:, bx * 512 + j * 128: bx * 512 + (j + 1) * 128],
                        lhsT=kh[hb][:, tix * 128:(tix + 1) * 128], rhs=qh[hb][:, q0:q0 + 128], start=True, stop=True),
                        r=kq, w=[('pb', bx)])
                if case == 'B':
                    tix = loc[4]
                    P.op('pe', lambda e, tix=tix: e.matmul(
                        psum[:, by * 512: by * 512 + 128],
                        lhsT=kh[hb][:, tix * 128:(tix + 1) * 128], rhs=qh[hb][:, q0:q0 + 128], start=True, stop=True),
                        r=kq, w=[('pb', by)])
                for j in range(2):
                    P.op('pe', lambda e, j=j: e.matmul(
                        psum[:, by * 512 + 128 + j * 128: by * 512 + 256 + j * 128],
                        lhsT=kh[hb][:, j * 128:(j + 1) * 128], rhs=qh[hb][:, q0:q0 + 128], start=True, stop=True),
                        r=kq, w=[('pb', by)])
                nloc = len(loc)
                ks = ('sT', i2)
                if case != 'ctx':
                    r = 2 * (ti - 2)
                    m0 = loc[0] - 2
                    d0i = 2 * m0 - r + 8
                    sT4 = sT[i2][:, 0:512].rearrange("p (a h c) -> p a h c", a=4, h=2, c=64)
                    ps4 = psum[:, bx * 512: bx * 512 + 512].rearrange("p (a h c) -> p a h c", a=4, h=2, c=64)
                    for hf in range(2):
                        di = d0i - hf
                        tab = bt[hb] if case == 'A' else (btB0[hb] if hf == 0 else btB1[hb])
                        ktab = ('bt', hb) if case == 'A' else (('btB0', hb) if hf == 0 else ('btB1', hb))
                        P.op('dve', lambda e, hf=hf, di=di, tab=tab: e.tensor_tensor(
                            out=sT4[:, :, hf, :], in0=ps4[:, :, hf, :], in1=tab[:, di:di + 7:2, :], op=ALU.add),
                            r=[('pb', bx), ktab], w=[ks])
                    if case == 'B':
                        di = d0i - 1 + 8
                        P.op('dve', lambda e, di=di: e.tensor_tensor(
                            out=sT[i2][:, 576:640], in0=psum[:, by * 512 + 64: by * 512 + 128], in1=btB1[hb][:, di, :], op=ALU.add),
                            r=[('pb', by), ('btB1', hb)], w=[ks])
                    P.op('act', lambda e: e.activation(out=pT[i2][:, 0:nloc * 128], in_=sT[i2][:, 0:nloc * 128], func=AF.Exp),
                         r=[ks, ('sTc', i2)], w=[('pT', i2)])
                P.op('act', lambda e: e.activation(
                    out=pT[i2][:, 640:896], in_=psum[:, by * 512 + 128: by * 512 + 384], func=AF.Exp),
                    r=[('pb', by), ('pT', i2)], w=[('pT', i2)])

            def stage2(u, i2, o2):
                h, ti = u
                hb = h % 2
                obk = 6 + o2
                loc, case = geom(ti)
                slots = [(j * 128, tix) for j, tix in enumerate(loc)] + [(640, 0), (768, 1)]
                ns = len(slots)
                for s_, (c0, tix) in enumerate(slots):
                    P.op('pe', lambda e, s_=s_, c0=c0, tix=tix: e.matmul(
                        bank(obk, 65), lhsT=pT[i2][:, c0:c0 + 128], rhs=vh[hb][:, tix, :],
                        start=(s_ == 0), stop=(s_ == ns - 1)),
                        r=[('pT', i2), ('vh', hb), ('vh1', hb)], w=[('pb', obk)])
                P.op('dve', lambda e: e.reciprocal(out=rc[i2][:], in_=psum[:, obk * 512 + 64: obk * 512 + 65]),
                     r=[('pb', obk)], w=[('rc', i2)])
                P.op('dve', lambda e: e.scalar_tensor_tensor(
                    out=oh[hb][:, ti, :], in0=bank(obk, 64), scalar=rc[i2][:, 0:1], in1=zh[hb][:, ti, :],
                    op0=ALU.mult, op1=ALU.mult),
                    r=[('pb', obk), ('rc', i2), ('zh', hb)], w=[('oh', hb, ti)])

            units = [(h, ti) for h in range(16) for ti in range(t_first, NT)]
            nU = len(units)
            head_load(0)
            head_load(1)
            for i in range(nU + LAG):
                if i < nU:
                    stage1(units[i], i % 3)
                j = i - LAG
                if j >= 0:
                    stage2(units[j], j % 3, j % 2)
                    if j == nU - 1 or units[j + 1][0] != units[j][0]:
                        hd = units[j][0]
                        head_store(hd)
                        if hd + 2 < 16:
                            head_load(hd + 2)
        P.barrier()

    def phase_gla(l):
        last = (l == n_layers - 1)
        with ExitStack() as es:
            sb = lambda n, s, d=F32: es.enter_context(nc.sbuf_tensor(U(n), s, d))
            tri = sb("tri", [128, 4, 128])
            w2a = sb("w2a", [17, 2, 256])
            cosb = sb("cosb", [128, 32, 64])
            sinb = sb("sinb", [128, 32, 64])
            gn = sb("gn", [128, 128])
            ost = sb("ost", [128, NT, 512])
            D2 = range(2)
            S = [sb("S", [64, 4, 128]) for _ in D2]
            tb = [[sb("tb", [128, 1568]) for _ in range(2)] for _ in D2]
            qk = [sb("qk", [128, 512]) for _ in D2]
            t2 = [sb("t2", [128, 512]) for _ in D2]
            qT = [sb("qT", [64, 4, 128]) for _ in D2]
            kT = [sb("kT", [64, 4, 128]) for _ in D2]
            lrT = [sb("lrT", [17, 128]) for _ in D2]
            ee = [sb("ee", [128, 256]) for _ in D2]
            spt = [sb("spt", [128, 256]) for _ in D2]
            ekd = [sb("ekd", [128, 256]) for _ in D2]
            kd = [sb("kd", [128, 256]) for _ in D2]
            eb = [sb("eb", [64, 4, 128]) for _ in D2]
            enb = [sb("enb", [64, 4, 128]) for _ in D2]
            qt = [sb("qt", [64, 4, 128]) for _ in D2]
            kt = [sb("kt", [64, 4, 128]) for _ in D2]
            am = [sb("am", [128, 4, 128]) for _ in D2]
            osum = [sb("osum", [128, 512]) for _ in D2]
            zt = [sb("zt", [128, 512]) for _ in D2]
            ob = [sb("ob", [128, 512]) for _ in D2]
            ssq = [sb("ssq", [128, 4]) for _ in D2]
            rstd = [sb("rstd", [128, 4]) for _ in D2]
            P.dma('sp', lambda e: e.dma_start(out=tri[:], in_=tri_d.rearrange("a p q -> p a q")), w=['tri'])
            P.dma('sp', lambda e: e.dma_start(out=w2a[:], in_=w2a_d[l].rearrange("d k n -> k d n")), w=['w2a'])
            P.dma('sp', lambda e: e.dma_start(out=cosb[:], in_=cos_d.rearrange("(t p) e -> p t e", p=128)), w=['cosb'])
            P.dma('sp', lambda e: e.dma_start(out=sinb[:], in_=sin_d.rearrange("(t p) e -> p t e", p=128)), w=['sinb'])
            P.dma('sp', lambda e: e.dma_start(out=gn[:], in_=gnorm_d[l].partition_broadcast(128)), w=['gn'])
            for d in D2:
                P.op('pool', lambda e, d=d: e.memset(lrT[d][:], 1.0), w=[('lrT1', d)])
                P.op('pool', lambda e, d=d: e.memset(S[d][:], 0.0), w=[('S', d, h) for h in range(4)])
            cnt = [0, 0]
            f2 = lambda t: t[:].rearrange("p h t -> p (h t)")
            order = [list(range(NT)), [1, 0] + list(range(NT - 1, 1, -1))]
            idx = [{t: i for i, t in enumerate(order[d])} for d in D2]

            pend = [[], []]

            def flush_store(dr):
                while pend[dr]:
                    tj = pend[dr].pop(0)
                    P.dma('sp', lambda e, tj=tj: e.dma_start(out=CAT[tj * 128:(tj + 1) * 128, 1024:1536], in_=ob[dr][:]),
                          r=[('ob', dr, h) for h in range(4)], w=[('CATb', tj)])

            def gla_tile(ti, dr):
                K = lambda n: (n, dr)
                B0, B1, B2, B3 = 4 * dr, 4 * dr + 1, 4 * dr + 2, 4 * dr + 3
                lat = ti >= 2
                lt = ti - 2
                tbb = tb[dr][cnt[dr] % 2]
                ktb = ('tb', dr, cnt[dr] % 2)
                cnt[dr] += 1
                P.dma('sp', lambda e: e.dma_start(out=tbb[:], in_=TM[ti * 128:(ti + 1) * 128, TM_QB:TM_QB + 1568]), w=[ktb])
                flush_store(dr)
                if lat:
                    src3 = tbb[:, 0:512].rearrange("p (h e) -> p h e", h=8)
                    P.op('pool', lambda e: e.tensor_tensor(
                        out=qk[dr][:].rearrange("p (h e) -> p h e", h=8), in0=src3,
                        in1=cosb[:, lt:lt + 1, :].broadcast_to([128, 8, 64]), op=ALU.mult),
                        r=[ktb, 'cosb'], w=[K('qk')])
                    src5 = tbb[:, 0:512].rearrange("p (h a b e) -> p h a b e", h=8, a=2, b=2, e=16)
                    t25 = t2[dr][:].rearrange("p (h a b e) -> p h a b e", h=8, a=2, b=2, e=16)
                    sin5 = sinb[:, lt, :].rearrange("p (a b e) -> p a b e", a=2, b=2, e=16)
                    for bsel in range(2):
                        P.op('pool', lambda e, bsel=bsel: e.tensor_tensor(
                            out=t25[:, :, :, bsel, :], in0=src5[:, :, :, 1 - bsel, :],
                            in1=sin5[:, :, bsel, :].unsqueeze(1).broadcast_to([128, 8, 2, 16]), op=ALU.mult),
                            r=[ktb, 'sinb'], w=[('t2', dr, bsel)])
                    yield
                    P.op('pool', lambda e: e.tensor_tensor(out=qk[dr][:], in0=qk[dr][:], in1=t2[dr][:], op=ALU.add),
                         r=[K('qk'), ('t2', dr, 0), ('t2', dr, 1)], w=[K('qk')])
                    src = qk[dr]
                    ksrc = K('qk')
                else:
                    src = tbb
                    ksrc = ktb
                P.op('pe', lambda e: e.transpose(bank(B2, 128, 0, 16), tbb[:, 1536 + 16 * dr:1536 + 16 * dr + 16], ident[:]),
                     r=[ktb, 'ident'], w=[('pb', B2)])
                yield
                P.op('dve', lambda e: e.tensor_copy(out=lrT[dr][0:16, :], in_=bank(B2, 128, 0, 16)),
                     r=[('pb', B2), ('lrT1', dr)], w=[K('lrT')])
                yield
                P.op('pe', lambda e: e.matmul(psum[:, B2 * 512 + 256: B2 * 512 + 512], lhsT=lrT[dr][0:17, :], rhs=w2a[0:17, dr, :], start=True, stop=True),
                     r=[K('lrT'), ('lrT1', dr), 'w2a'], w=[('pb', B2)])
                yield
                P.op('act', lambda e: e.activation(out=ee[dr][:], in_=psum[:, B2 * 512 + 256: B2 * 512 + 512], func=AF.Exp, scale=-1.0),
                     r=[('pb', B2)], w=[K('ee')])
                P.op('act', lambda e: e.activation(out=spt[dr][:], in_=ee[dr][:], func=AF.Ln, bias=one1[:, 0:1], scale=1.0),
                     r=[K('ee'), 'one1'], w=[K('spt')])
                yield
                for h in range(4):
                    P.op('pe', lambda e, h=h: e.transpose(psum[0:64, B0 * 512 + h * 128: B0 * 512 + (h + 1) * 128],
                                                          src[:, h * 64:(h + 1) * 64], ident[:]),
                         r=[ksrc, 'ident'], w=[('pb', B0)])
                for h in range(4):
                    P.op('pe', lambda e, h=h: e.transpose(psum[0:64, B1 * 512 + h * 128: B1 * 512 + (h + 1) * 128],
                                                          src[:, 256 + h * 64:256 + (h + 1) * 64], ident[:]),
                         r=[ksrc, 'ident'], w=[('pb', B1)])
                yield
                P.op('act', lambda e: e.activation(out=f2(qT[dr]), in_=bank(B0, 512, 0, 64), func=AF.Copy, scale=0.125),
                     r=[('pb', B0)], w=[K('qT')])
                P.op('act', lambda e: e.activation(out=f2(kT[dr]), in_=bank(B1, 512, 0, 64), func=AF.Copy),
                     r=[('pb', B1)], w=[K('kT')])
                yield
                P.op('pe', lambda e: e.matmul(bank(B0, 256), lhsT=tri[:, 2 * dr, :], rhs=spt[dr][:], start=True, stop=True),
                     r=['tri', K('spt')], w=[('pb', B0)])
                for h in range(4):
                    P.op('pe', lambda e, h=h: e.matmul(psum[0:64, B1 * 512 + h * 128: B1 * 512 + (h + 1) * 128],
                                                       lhsT=spt[dr][:, h * 64:(h + 1) * 64], rhs=tri[:, 2 * dr + 1, :], start=True, stop=True),
                         r=[K('spt'), 'tri'], w=[('pb', B1)])
                yield
                P.op('act', lambda e: e.activation(out=ekd[dr][:], in_=bank(B0, 256), func=AF.Exp, scale=-1.0 / 16),
                     r=[('pb', B0)], w=[K('ekd')])
                P.op('act', lambda e: e.activation(out=f2(eb[dr]), in_=bank(B1, 512, 0, 64), func=AF.Exp, scale=-1.0 / 16),
                     r=[('pb', B1)], w=[K('eb')])
                P.op('act', lambda e: e.activation(out=f2(enb[dr]), in_=bank(B1, 512, 0, 64), func=AF.Exp, scale=1.0 / 16),
                     r=[('pb', B1)], w=[K('enb')])
                yield
                P.op('dve', lambda e: e.tensor_tensor(out=kd[dr][:], in0=src[:, 256:512], in1=ekd[dr][:], op=ALU.mult),
                     r=[ksrc, K('ekd')], w=[K('kd')])
                P.op('dve', lambda e: e.tensor_tensor(out=f2(qt[dr]), in0=f2(qT[dr]), in1=f2(eb[dr]), op=ALU.mult),
                     r=[K('qT'), K('eb')], w=[K('qt')])
                P.op('dve', lambda e: e.tensor_tensor(out=f2(kt[dr]), in0=f2(kT[dr]), in1=f2(enb[dr]), op=ALU.mult),
                     r=[K('kT'), K('enb')], w=[K('kt')])
                yield
                need_o = lat or (not last)
                if need_o:
                    for h in range(4):
                        P.op('pe', lambda e, h=h: e.matmul(psum[:, B2 * 512 + h * 128: B2 * 512 + (h + 1) * 128],
                                                           lhsT=kt[dr][:, h, :], rhs=qt[dr][:, h, :], start=True, stop=True),
                             r=[K('kt'), K('qt')], w=[('pb', B2)])
                for h in range(4):
                    P.op('pe', lambda e, h=h: e.matmul(psum[0:64, B0 * 512 + h * 128: B0 * 512 + (h + 1) * 128],
                                                       lhsT=kd[dr][:, h * 64:(h + 1) * 64], rhs=tbb[:, 512 + h * 128:512 + (h + 1) * 128], start=True, stop=True),
                         r=[K('kd'), ktb], w=[('pb', B0)])
                yield
                if need_o:
                    P.op('dve', lambda e: e.tensor_tensor(
                        out=am[dr][:], in0=bank(B2).rearrange("p (h t) -> p h t", h=4),
                        in1=tri[:, 2 * dr + 1:2 * dr + 2, :].broadcast_to([128, 4, 128]), op=ALU.mult),
                        r=[('pb', B2), 'tri'], w=[K('am')])
                    yield
                    for h in range(4):
                        P.op('pe', lambda e, h=h: e.matmul(psum[:, B3 * 512 + h * 128: B3 * 512 + (h + 1) * 128],
                                                           lhsT=am[dr][:, h, :], rhs=tbb[:, 512 + h * 128:512 + (h + 1) * 128], start=True, stop=False),
                             r=[K('am'), ktb], w=[('pb', B3)])
                        P.op('pe', lambda e, h=h: e.matmul(psum[:, B3 * 512 + h * 128: B3 * 512 + (h + 1) * 128],
                                                           lhsT=qt[dr][:, h, :], rhs=S[dr][:, h, :], start=False, stop=True),
                             r=[K('qt'), ('S', dr, h)], w=[('pb', B3)])
                    yield
                col = 127 if dr == 0 else 0
                for h in range(4):
                    P.op('dve', lambda e, h=h: e.scalar_tensor_tensor(
                        out=S[dr][:, h, :], in0=S[dr][:, h, :], scalar=eb[dr][:, h, col:col + 1],
                        in1=psum[0:64, B0 * 512 + h * 128: B0 * 512 + (h + 1) * 128], op0=ALU.mult, op1=ALU.add),
                        r=[('S', dr, h), K('eb'), ('pb', B0)], w=[('S', dr, h)])
                yield
                if not need_o:
                    return
                first = idx[dr][ti] < idx[1 - dr][ti]
                if first:
                    P.op('act', lambda e: e.activation(out=ost[:, ti, :], in_=bank(B3), func=AF.Copy), r=[('pb', B3)], w=[('ost', ti)])
                    yield
                    return
                P.op('dve', lambda e: e.tensor_tensor(out=osum[dr][:], in0=bank(B3), in1=ost[:, ti, :], op=ALU.add),
                     r=[('pb', B3), ('ost', ti)], w=[K('osum')])
                kob = [('ob', dr, h) for h in range(4)]
                zb = tbb[:, 1024:1536]
                for h in range(4):
                    P.op('act', lambda e, h=h: e.activation(out=ob[dr][:, h * 128:(h + 1) * 128], in_=osum[dr][:, h * 128:(h + 1) * 128],
                                                          func=AF.Square, accum_out=ssq[dr][:, h:h + 1]),
                         r=[K('osum')] + kob, w=[('ob', dr, h), ('ssq', dr, h)])
                yield
                P.op('act', lambda e: e.activation(out=zt[dr][:], in_=zb, func=AF.Silu), r=[ktb], w=[K('zt')])
                yield
                P.op('act', lambda e: e.activation(out=rstd[dr][:], in_=ssq[dr][:], func=AF.Ln, bias=eps6[:, 0:1], scale=1.0 / 128),
                     r=[('ssq', dr, h) for h in range(4)] + ['eps6'], w=[K('rstd')])
                P.op('act', lambda e: e.activation(out=rstd[dr][:], in_=rstd[dr][:], func=AF.Exp, scale=-0.5), r=[K('rstd')], w=[K('rstd')])
                yield
                P.op('pool', lambda e: e.tensor_tensor(
                    out=zt[dr][:].rearrange("p (h t) -> p h t", h=4), in0=zt[dr][:].rearrange("p (h t) -> p h t", h=4),
                    in1=gn[:, :].unsqueeze(1).broadcast_to([128, 4, 128]), op=ALU.mult), r=[K('zt'), 'gn'], w=[K('zt')])
                yield
                for h in range(4):
                    P.op('dve', lambda e, h=h: e.scalar_tensor_tensor(
                        out=ob[dr][:, h * 128:(h + 1) * 128], in0=osum[dr][:, h * 128:(h + 1) * 128], scalar=rstd[dr][:, h:h + 1],
                        in1=zt[dr][:, h * 128:(h + 1) * 128], op0=ALU.mult, op1=ALU.mult),
                        r=[K('osum'), K('rstd'), K('zt')], w=[('ob', dr, h)])
                pend[dr].append(ti)
                yield

            def chain(dr):
                for ti in order[dr]:
                    yield from gla_tile(ti, dr)
                flush_store(dr)

            gens = [chain(0), chain(1)]
            while gens:
                for g in list(gens):
                    try:
                        next(g)
                    except StopIteration:
                        gens.remove(g)
        P.barrier()

    def phase_conv(l):
        last = (l == n_layers - 1)
        with ExitStack() as es:
            sb = lambda n, s, d=F32: es.enter_context(nc.sbuf_tensor(U(n), s, d))
            cg = sb("cg", [128, T])
            upad = [sb("upad", [128, T + 60]) for _ in range(2)]
            ptmp = [sb("ptmp", [128, 4096]) for _ in range(2)]
            accP = sb("accP", [128, T])
            acc = sb("acc", [128, 4, T])
            cw = sb("cw", [128, 4, 31])
            cbias = sb("cbias", [128, 4])
            lg = sb("lg", [128, 512])
            lb = sb("lb", [128, 512])
            zc = [sb("zc", [128, 512]) for _ in range(2)]
            xn = sb("xn", [128, 512])
            oc = [sb("oc", [128, 512]) for _ in range(2)]
            stats = sb("cstats", [128, 6])
            mv = sb("cmv", [128, 2])
            rs = sb("crs", [128, 1])
            nmr = sb("cnmr", [128, 1])
            P.dma('sp', lambda e: e.dma_start(out=cw[:], in_=cwT_d[l].rearrange("(c p) k -> p c k", p=128)), w=['cw'])
            P.dma('sp', lambda e: e.dma_start(out=cbias[:], in_=cb_d[l]), w=['cbias'])
            P.dma('sp', lambda e: e.dma_start(out=lg[:], in_=clg_d[l].partition_broadcast(128)), w=['lg'])
            P.dma('sp', lambda e: e.dma_start(out=lb[:], in_=clb_d[l].partition_broadcast(128)), w=['lb'])
            for i in range(2):
                P.op('pool', lambda e, i=i: e.memset(upad[i][:], 0.0), w=[('upad', i)])
            segs = [(301, 256, 4096)] + ([] if last else [(15, 0, 256)])

            pcount = [0]
            NDT = 11

            def conv_load(cc):
                b2 = cc % 2
                up = upad[b2]
                ku = ('upad', b2)
                for (u0, a0, n) in segs:
                    P.dma('sp', lambda e, u0=u0, a0=a0, n=n: e.dma_start(out=up[:, u0:u0 + n], in_=CT[cc * 128:(cc + 1) * 128, a0:a0 + n]),
                          r=[], w=[ku])
                P.dma('sp', lambda e: e.dma_start(out=cg[:], in_=CT[512 + cc * 128:512 + (cc + 1) * 128, :]), w=['cg'])

            def conv_chunk(cc):
                b2 = cc % 2
                up = upad[b2]
                ku = ('upad', b2)
                P.op('act', lambda e: e.activation(out=cg[:], in_=cg[:], func=AF.Sigmoid), r=['cg'], w=['cg'])
                for (u0, a0, n) in segs:
                    P.op('dve', lambda e, u0=u0, a0=a0, n=n: e.tensor_tensor(
                        out=up[:, u0:u0 + n], in0=up[:, u0:u0 + n], in1=cg[:, a0:a0 + n], op=ALU.mult),
                        r=[ku, 'cg'], w=[ku])
                if cc + 1 < 4:
                    conv_load(cc + 1)
                for (u0, a0, n) in segs:
                    ka = ('acc', cc, a0)
                    kp = ('accP', a0)
                    P.op('dve', lambda e, u0=u0, a0=a0, n=n: e.tensor_scalar(
                        out=acc[:, cc, a0:a0 + n], in0=up[:, u0 - 15:u0 - 15 + n], scalar1=cw[:, cc, 0:1], scalar2=cbias[:, cc:cc + 1],
                        op0=ALU.mult, op1=ALU.add), r=[ku, 'cw', 'cbias'], w=[ka])
                    for k in range(1, NDT):
                        P.op('dve', lambda e, u0=u0, a0=a0, n=n, k=k: e.scalar_tensor_tensor(
                            out=acc[:, cc, a0:a0 + n], in0=up[:, u0 - 15 + k:u0 - 15 + k + n], scalar=cw[:, cc, k:k + 1],
                            in1=acc[:, cc, a0:a0 + n], op0=ALU.mult, op1=ALU.add), r=[ku, 'cw', ka], w=[ka])
                    for k in range(NDT, 31):
                        if k == NDT:
                            P.op('act', lambda e, u0=u0, a0=a0, n=n, k=k: e.activation(
                                out=accP[:, a0:a0 + n], in_=up[:, u0 - 15 + k:u0 - 15 + k + n], func=AF.Identity, scale=cw[:, cc, k:k + 1]),
                                r=[ku, 'cw'], w=[kp])
                            continue
                        pj = pcount[0] % 2
                        pcount[0] += 1
                        pt = ptmp[pj]
                        P.op('act', lambda e, u0=u0, n=n, k=k, pt=pt: e.activation(
                            out=pt[:, 0:n], in_=up[:, u0 - 15 + k:u0 - 15 + k + n], func=AF.Identity, scale=cw[:, cc, k:k + 1]),
                            r=[ku, 'cw'], w=[('ptmp', pj)])
                        P.op('pool', lambda e, a0=a0, n=n, pt=pt: e.tensor_tensor(
                            out=accP[:, a0:a0 + n], in0=accP[:, a0:a0 + n], in1=pt[:, 0:n], op=ALU.add),
                            r=[('ptmp', pj), kp], w=[kp])
                    P.op('pool', lambda e, a0=a0, n=n: e.tensor_tensor(
                        out=acc[:, cc, a0:a0 + n], in0=acc[:, cc, a0:a0 + n], in1=accP[:, a0:a0 + n], op=ALU.add),
                        r=[ka, kp], w=[ka])

            conv_load(0)
            for cc in range(4):
                conv_chunk(cc)
            tiles = list(range(2, NT)) if last else list(range(NT))
            xn2 = [xn, sb("xn_b", [128, 512])]
            st2 = [stats, sb("cstats_b", [128, 6])]
            mv2 = [mv, sb("cmv_b", [128, 2])]
            rs2 = [rs, sb("crs_b", [128, 1])]
            nm2 = [nmr, sb("cnmr_b", [128, 1])]

            def postA(n_):
                ti = tiles[n_]
                b = n_ % 2
                a0 = 0 if ti < 2 else 256
                P.dma('sp', lambda e: e.dma_start(out=zc[b][:], in_=TM[ti * 128:(ti + 1) * 128, TM_ZC:TM_ZC + 512]), w=[('zc', b)])
                for cc in range(4):
                    P.op('pe', lambda e, cc=cc: e.transpose(psum[:, b * 512 + cc * 128: b * 512 + (cc + 1) * 128],
                                                          acc[:, cc, ti * 128:(ti + 1) * 128], ident[:]),
                         r=[('acc', cc, a0), 'ident'], w=[('pb', b)])
                P.op('dve', lambda e: e.bn_stats(out=st2[b][:], in_=bank(b)), r=[('pb', b)], w=[('cstats', b)])
                P.op('dve', lambda e: e.bn_aggr(out=mv2[b][:], in_=st2[b][:]), r=[('cstats', b)], w=[('cmv', b)])
                P.op('act', lambda e: e.activation(out=rs2[b][:], in_=mv2[b][:, 1:2], func=AF.Sqrt, bias=eps5[:, 0:1], scale=1.0),
                     r=[('cmv', b), 'eps5'], w=[('crs', b)])
                P.op('dve', lambda e: e.reciprocal(out=rs2[b][:], in_=rs2[b][:]), r=[('crs', b)], w=[('crs', b)])
                P.op('dve', lambda e: e.tensor_scalar(out=nm2[b][:], in0=mv2[b][:, 0:1], scalar1=rs2[b][:, 0:1], scalar2=-1.0, op0=ALU.mult, op1=ALU.mult),
                     r=[('cmv', b), ('crs', b)], w=[('cnmr', b)])
                P.op('act', lambda e: e.activation(out=xn2[b][:], in_=bank(b), func=AF.Identity, bias=nm2[b][:, 0:1], scale=rs2[b][:, 0:1]),
                     r=[('pb', b), ('crs', b), ('cnmr', b)], w=[('xn', b)])

            def postB(n_):
                ti = tiles[n_]
                b = n_ % 2
                P.op('dve', lambda e: e.tensor_tensor(out=xn2[b][:], in0=xn2[b][:], in1=lg[:], op=ALU.mult), r=[('xn', b), 'lg'], w=[('xn', b)])
                P.op('pool', lambda e: e.tensor_tensor(out=xn2[b][:], in0=xn2[b][:], in1=lb[:], op=ALU.add), r=[('xn', b), 'lb'], w=[('xn', b)])
                P.op('act', lambda e: e.activation(out=xn2[b][:], in_=xn2[b][:], func=AF.Silu), r=[('xn', b)], w=[('xn', b)])
                P.op('act', lambda e: e.activation(out=zc[b][:], in_=zc[b][:], func=AF.Silu), r=[('zc', b)], w=[('zc', b)])
                P.op('dve', lambda e: e.tensor_tensor(out=oc[b][:], in0=xn2[b][:], in1=zc[b][:], op=ALU.mult), r=[('xn', b), ('zc', b)], w=[('oc', b)])
                P.dma('sp', lambda e: e.dma_start(out=CAT[ti * 128:(ti + 1) * 128, 1536:2048], in_=oc[b][:]), r=[('oc', b)], w=[('CATc', ti)])

            nTl = len(tiles)
            postA(0)
            for n_ in range(nTl):
                if n_ + 1 < nTl:
                    postA(n_ + 1)
                postB(n_)
        P.barrier()

    def phase_out(l):
        last = (l == n_layers - 1)
        src = xin if l == 0 else X1
        with ExitStack() as es:
            sb = lambda n, s, d=F32: es.enter_context(nc.sbuf_tensor(U(n), s, d))
            wo = sb("wo", [128, 16, D], BF16)
            gate = [sb("gate", [128, D]) for _ in range(2)]
            plg = sb("plg", [128, D])
            plb = sb("plb", [128, D])
            catt = [sb("catt", [128, D]) for _ in range(2)]
            xres = [sb("xres", [128, D]) for _ in range(2)]
            catT = [sb("catT", [128, 16, 128], BF16) for _ in range(2)]
            rt = [sb("rt", [128, D]) for _ in range(2)]
            stats = [sb("ostats", [128, 4, 6]) for _ in range(2)]
            mv = [sb("omv", [128, 2]) for _ in range(2)]
            rs = [sb("ors", [128, 1]) for _ in range(2)]
            nmr = [sb("onmr", [128, 1]) for _ in range(2)]
            wv = w_out[l].rearrange("(kc p) n -> p kc n", p=128)
            wstg = sb("wstg", [128, 16, 512])
            for ng in range(4):
                for hf in range(2):
                    P.dma('sp', lambda e, ng=ng, hf=hf: e.dma_start(out=wstg[:, hf * 8:(hf + 1) * 8, :], in_=wv[:, hf * 8:(hf + 1) * 8, ng * 512:(ng + 1) * 512]),
                          w=[('wstg', hf)])
                P.op('act', lambda e, ng=ng: e.activation(out=wo[:, 0:8, ng * 512:(ng + 1) * 512], in_=wstg[:, 0:8, :], func=AF.Copy),
                     r=[('wstg', 0)], w=[('wo', ng, 0)])
                P.op('dve', lambda e, ng=ng: e.tensor_copy(out=wo[:, 8:16, ng * 512:(ng + 1) * 512], in_=wstg[:, 8:16, :]),
                     r=[('wstg', 1)], w=[('wo', ng, 1)])
            for wh in range(2):
                P.dma('sp', lambda e, wh=wh: e.dma_start(out=gate[wh][:], in_=MOD[l, wh, 2 * D:3 * D].partition_broadcast(128)), w=[('gate', wh)])
            P.dma('sp', lambda e: e.dma_start(out=plg[:], in_=plg_d[l].partition_broadcast(128)), w=['plg'])
            P.dma('sp', lambda e: e.dma_start(out=plb[:], in_=plb_d[l].partition_broadcast(128)), w=['plb'])
            tiles = list(range(2, NT)) if last else list(range(NT))

            def stageT(n_):
                ti = tiles[n_]
                b = n_ % 2
                P.dma('sp', lambda e: e.dma_start(out=catt[b][:], in_=CAT[ti * 128:(ti + 1) * 128, :]), w=[('catt', b)])
                P.dma('sp', lambda e: e.dma_start(out=xres[b][:], in_=src[ti * 128:(ti + 1) * 128, :]), w=[('xres', b)])
                for kc in range(16):
                    bk = 4 + kc // 4
                    sl = psum[:, bk * 512 + (kc % 4) * 128: bk * 512 + (kc % 4 + 1) * 128]
                    P.op('pe', lambda e, kc=kc, sl=sl: e.transpose(sl, catt[b][:, kc * 128:(kc + 1) * 128], ident[:]),
                         r=[('catt', b), 'ident'], w=[('pb', bk)])
                for j in range(4):
                    bk = 4 + j
                    dstv = catT[b][:, 4 * j:4 * j + 4, :].rearrange("p a t -> p (a t)")
                    if j % 2 == 0:
                        P.op('act', lambda e, bk=bk, dstv=dstv: e.activation(out=dstv, in_=bank(bk), func=AF.Copy), r=[('pb', bk)], w=[('catT', b, j)])
                    else:
                        P.op('dve', lambda e, bk=bk, dstv=dstv: e.tensor_copy(out=dstv, in_=bank(bk)), r=[('pb', bk)], w=[('catT', b, j)])

            def stageM(n_):
                ti = tiles[n_]
                b = n_ % 2
                wh = 1 if ti < 2 else 0
                for ng in range(4):
                    for kc in range(16):
                        P.op('pe', lambda e, ng=ng, kc=kc: e.matmul(bank(ng), lhsT=catT[b][:, kc, :], rhs=wo[:, kc, ng * 512:(ng + 1) * 512],
                                                                  start=(kc == 0), stop=(kc == 15)),
                             r=[('catT', b, kc // 4), ('wo', ng, kc // 8)], w=[('pb', ng)])
                    P.op('dve', lambda e, ng=ng: e.tensor_tensor(
                        out=rt[b][:, ng * 512:(ng + 1) * 512], in0=bank(ng), in1=gate[wh][:, ng * 512:(ng + 1) * 512], op=ALU.mult),
                        r=[('pb', ng), ('gate', wh)], w=[('rt', b, ng)])
                krt = [('rt', b, ng) for ng in range(4)]
                P.op('dve', lambda e: e.scalar_tensor_tensor(out=rt[b][:], in0=xres[b][:], scalar=ALPHA, in1=rt[b][:], op0=ALU.mult, op1=ALU.add),
                     r=krt + [('xres', b)], w=krt)
                for q in range(4):
                    P.op('dve', lambda e, q=q: e.bn_stats(out=stats[b][:, q, :], in_=rt[b][:, q * 512:(q + 1) * 512]), r=krt, w=[('ostats', b, q)])
                P.op('dve', lambda e: e.bn_aggr(out=mv[b][:], in_=stats[b][:].rearrange("p a b -> p (a b)")),
                     r=[('ostats', b, q) for q in range(4)], w=[('omv', b)])
                P.op('act', lambda e: e.activation(out=rs[b][:], in_=mv[b][:, 1:2], func=AF.Sqrt, bias=eps5[:, 0:1], scale=1.0),
                     r=[('omv', b), 'eps5'], w=[('ors', b)])
                P.op('dve', lambda e: e.reciprocal(out=rs[b][:], in_=rs[b][:]), r=[('ors', b)], w=[('ors', b)])
                P.op('dve', lambda e: e.tensor_scalar(out=nmr[b][:], in0=mv[b][:, 0:1], scalar1=rs[b][:, 0:1], scalar2=-1.0, op0=ALU.mult, op1=ALU.mult),
                     r=[('omv', b), ('ors', b)], w=[('onmr', b)])
                P.op('act', lambda e: e.activation(out=rt[b][:], in_=rt[b][:], func=AF.Identity, bias=nmr[b][:, 0:1], scale=rs[b][:, 0:1]),
                     r=krt + [('ors', b), ('onmr', b)], w=krt)
                P.op('pool', lambda e: e.tensor_tensor(out=rt[b][:], in0=rt[b][:], in1=plg[:], op=ALU.mult), r=krt + ['plg'], w=krt)
                P.op('pool', lambda e: e.tensor_tensor(out=rt[b][:], in0=rt[b][:], in1=plb[:], op=ALU.add), r=krt + ['plb'], w=krt)

            def stageS(n_):
                ti = tiles[n_]
                b = n_ % 2
                krt = [('rt', b, ng) for ng in range(4)]
                dstr = out[(ti - 2) * 128:(ti - 1) * 128, :] if last else X1[ti * 128:(ti + 1) * 128, :]
                P.dma('sp', lambda e: e.dma_start(out=dstr, in_=rt[b][:]), r=krt, w=[('xo', ti)])

            nTl = len(tiles)
            stageT(0)
            for n_ in range(nTl):
                if n_ + 1 < nTl:
                    stageT(n_ + 1)
                if n_ >= 1:
                    stageS(n_ - 1)
                stageM(n_)
            stageS(nTl - 1)
        P.barrier()

    eps5 = nc.alloc_sbuf_tensor("eps5", [128, 1], F32)
    P.op('pool', lambda e: e.memset(eps5[:], 1e-5), w=['eps5'])
    eps6 = nc.alloc_sbuf_tensor("eps6", [128, 1], F32)
    P.op('pool', lambda e: e.memset(eps6[:], 1e-6), w=['eps6'])
    one1 = nc.alloc_sbuf_tensor("one1", [128, 1], F32)
    P.op('pool', lambda e: e.memset(one1[:], 1.0), w=['one1'])

    phases = []
    phases.append(('mod', phase_mod))
    for l in range(n_layers):
        phases.append((f'inproj{l}', lambda l=l: phase_inproj(l)))
        phases.append((f'na{l}', lambda l=l: phase_na(l)))
        phases.append((f'gla{l}', lambda l=l: phase_gla(l)))
        phases.append((f'conv{l}', lambda l=l: phase_conv(l)))
        phases.append((f'out{l}', lambda l=l: phase_out(l)))
    import os as _os
    skip = (_os.environ.get('KT_SKIP') or '').split(',')
    for name, fn in phases:
        if name in skip:
            continue
        fn()
        if stop_after == name:
            break
    P.emit()
    global _P
    _P = P
    return nc


def host_inputs(inputs, b):
    f = np.float32
    x = np.asarray(inputs['x'], f)
    ctx = np.asarray(inputs['ctx'], f)
    c = np.asarray(inputs['c'], f)
    c_ctx = np.asarray(inputs['c_ctx'], f)
    m = {}
    m['xin'] = np.ascontiguousarray(np.concatenate([ctx[b], x[b]], axis=0))
    cv = np.stack([c[b], c_ctx], axis=0)
    m['cT'] = np.ascontiguousarray(cv.reshape(2, 16, 128).transpose(2, 0, 1))
    m['w_ada'] = np.asarray(inputs['w_ada'], f)
    m['b_ada'] = np.asarray(inputs['b_ada'], f)
    m['w_in'] = np.asarray(inputs['w_in'], f)
    rpb = np.asarray(inputs['rpb'], f)
    p = np.arange(128)
    cp = p % 64
    half = p // 64
    d0 = np.arange(16) - 8
    cq = np.arange(64)
    dd = np.clip(d0[None, :] + half[:, None] + 7, 0, 14)
    dc = np.clip(cp[:, None] - cq[None, :] + 15, 0, 30)
    bt = rpb[:, :, dd[:, :, None], dc[:, None, :]]
    m['btab'] = np.ascontiguousarray(bt.reshape(2, 16, 128, 1024))
    cs = np.clip(cq - 8, 0, 48)
    ok = (cp[:, None] >= cs[None, :]) & (cp[:, None] < cs[None, :] + 16)
    m['nmask'] = np.where(ok, 0.0, NEG).astype(f)
    w2 = np.asarray(inputs['gla_w2'], f)
    gb = np.asarray(inputs['gla_b'], f)
    m['w2a'] = np.ascontiguousarray(np.concatenate([w2, gb[:, :, None, :]], axis=2))
    m['gnorm'] = np.asarray(inputs['gla_norm'], f)
    m['cwT'] = np.ascontiguousarray(np.asarray(inputs['conv_w'], f).transpose(0, 2, 1))
    m['cb'] = np.ascontiguousarray(np.asarray(inputs['conv_b'], f).reshape(2, 4, 128).transpose(0, 2, 1))
    m['clg'] = np.asarray(inputs['conv_ln_g'], f)
    m['clb'] = np.asarray(inputs['conv_ln_b'], f)
    m['w_out'] = np.asarray(inputs['w_out'], f)
    m['plg'] = np.asarray(inputs['post_ln_g'], f)
    m['plb'] = np.asarray(inputs['post_ln_b'], f)
    pos = np.arange(4096)
    rows = (pos // 64).astype(f)
    colsp = (pos % 64).astype(f)
    inv = (10000.0 ** (-np.arange(0, 32, 2, dtype=f) / 32)).astype(f)
    ar = rows[:, None] * inv[None, :]
    ac = colsp[:, None] * inv[None, :]
    m['ropecos'] = np.concatenate([np.cos(ar), np.cos(ar), np.cos(ac), np.cos(ac)], axis=1).astype(f)
    m['ropesin'] = np.concatenate([-np.sin(ar), np.sin(ar), -np.sin(ac), np.sin(ac)], axis=1).astype(f)
    m['ident'] = np.eye(128, dtype=f)
    j = np.arange(128)[:, None]
    i = np.arange(128)[None, :]
    m['tri'] = np.stack([(j > i), (j <= i), (j < i), (j >= i)]).astype(f)
    return m


_NC = None


def kernel(**inputs):
    global _NC
    if _NC is None:
        _NC = build()
    in_maps = [host_inputs(inputs, cid % 4) for cid in range(8)]
    res = run_bass_kernel_spmd(_NC, in_maps, core_ids=list(range(8)))
    return np.stack([res.results[b]["out"] for b in range(4)], axis=0).astype(np.float32)
```

```python
import numpy as np
from contextlib import ExitStack
import concourse.bass as bass
import concourse.mybir as mybir
from concourse.bass_utils import run_bass_kernel_spmd

F32 = mybir.dt.float32
BF16 = mybir.dt.bfloat16
AF = mybir.ActivationFunctionType
ALU = mybir.AluOpType

NDS = 12
D = 2048
NT = 34
T = NT * 128
NIN = 7200
ALPHA = 4.0 ** 0.25
NEG = -30000.0


class Prog:
    ENG = ('pe', 'act', 'dve', 'pool', 'sp')

    def __init__(self, nc):
        self.nc = nc
        self.ops = {e: [] for e in self.ENG}
        self.esem = {e: nc.alloc_semaphore(f"s_{e}") for e in self.ENG}
        self.ecnt = {e: 0 for e in self.ENG}
        self.known = {e: {} for e in self.ENG}
        self.dsem = {e: [nc.alloc_semaphore(f"d_{e}_{i}") for i in range(NDS)] for e in self.ENG}
        self.dcnt = {e: [0] * NDS for e in self.ENG}
        self.dnext = {e: 0 for e in self.ENG}
        self.res = {}
        self.semid = {}
        self.pending = {e: [] for e in self.ENG}

    def _deps(self, eng, r, w):
        deps = list(self.pending[eng])
        self.pending[eng] = []
        for k in r:
            st = self.res.get(k)
            if st and st[0] is not None:
                deps.append(st[0])
        for k in w:
            st = self.res.get(k)
            if st:
                if st[0] is not None:
                    deps.append(st[0])
                deps.extend(st[1])
        best = {}
        for (sem, val, peng, isdma) in deps:
            if (not isdma) and peng == eng and eng == 'pe':
                continue
            key = id(sem)
            self.semid[key] = sem
            if best.get(key, 0) < val:
                best[key] = val
        waits = []
        kn = self.known[eng]
        for key, val in best.items():
            if kn.get(key, 0) >= val:
                continue
            kn[key] = val
            waits.append((self.semid[key], val))
        return waits

    def _commit(self, tok, r, w):
        for k in r:
            st = self.res.get(k)
            if st is None:
                st = [None, []]
                self.res[k] = st
            st[1].append(tok)
        for k in w:
            self.res[k] = [tok, []]

    def op(self, eng, fn, r=(), w=()):
        waits = self._deps(eng, r, w)
        self.ecnt[eng] += 1
        sem = self.esem[eng]
        tok = (sem, self.ecnt[eng], eng, False)
        self.ops[eng].append((waits, fn, (sem, 1)))
        self._commit(tok, r, w)

    def dma(self, eng, fn, r=(), w=()):
        waits = self._deps(eng, r, w)
        j = self.dnext[eng]
        self.dnext[eng] = (j + 1) % NDS
        sem = self.dsem[eng][j]
        prev = self.dcnt[eng][j]
        self.semid[id(sem)] = sem
        if prev > 0 and self.known[eng].get(id(sem), 0) < 16 * prev:
            self.known[eng][id(sem)] = 16 * prev
            waits.append((sem, 16 * prev))
        self.dcnt[eng][j] = prev + 1
        tok = (sem, 16 * (prev + 1), eng, True)
        self.ops[eng].append((waits, fn, (sem, 16)))
        self._commit(tok, r, w)

    def _all_tokens(self):
        toks = []
        for e in self.ENG:
            if self.ecnt[e] > 0:
                toks.append((self.esem[e], self.ecnt[e], e, True))
            for j in range(NDS):
                if self.dcnt[e][j] > 0:
                    toks.append((self.dsem[e][j], 16 * self.dcnt[e][j], e, True))
        return toks

    def barrier(self):
        toks = self._all_tokens()
        for e in self.ENG:
            self.pending[e].extend(toks)
        self.res = {}

    def emit(self):
        nc = self.nc
        fin = [(s, v) for (s, v, _, _) in self._all_tokens()]
        with nc.Block() as block:
            secs = {'pe': block.tensor, 'act': block.scalar, 'dve': block.vector,
                    'pool': block.gpsimd, 'sp': block.sync}
            for e in self.ENG:
                ops = self.ops[e]
                f = fin if e == 'sp' else []

                def body(engine, ops=ops, f=f):
                    for (waits, fn, (sem, amt)) in ops:
                        for (s, v) in waits:
                            engine.wait_ge(s, v)
                        inst = fn(engine)
                        inst.then_inc(sem, amt)
                    for (s, v) in f:
                        engine.wait_ge(s, v)
                if ops or f:
                    secs[e](body)


GROUPS = [
    ('qt', 0, 512, 0), ('qt', 512, 512, 512),
    ('kt', 1024, 512, 0), ('kt', 1536, 512, 512),
    ('va', 2048, 512, 0), ('va', 2560, 512, 512),
    ('tm', 3072, 512, 0), ('tm', 3584, 512, 512), ('tm', 4096, 512, 1024),
    ('tm', 4608, 512, 1536), ('tm', 5120, 512, 2048), ('tm', 5632, 32, 2560),
    ('ct', 5664, 512, 0), ('ct', 6176, 512, 512),
    ('tm', 6688, 512, 2592),
]
TMW = 3104
TM_ZA, TM_QB, TM_KB, TM_VB, TM_ZB, TM_LR, TM_ZC = 0, 1024, 1280, 1536, 2048, 2560, 2592


def build(n_layers=2, dbg=False, stop_after=None):
    nc = bass.Bass("TRN2", target_bir_lowering=False)

    def din(name, shape, dt=F32):
        return nc.dram_tensor(name, list(shape), dt, kind="ExternalInput").ap()

    def dscr(name, shape, dt=F32):
        return nc.dram_tensor(name, list(shape), dt, kind="ExternalOutput" if dbg else "Internal").ap()

    xin = din("xin", [T, D])
    cT_d = din("cT", [128, 2, 16])
    w_ada = din("w_ada", [2, D, 3 * D])
    b_ada = din("b_ada", [2, 3 * D])
    w_in = din("w_in", [2, D, NIN])
    btab = din("btab", [2, 16, 128, 1024])
    nmask_d = din("nmask", [128, 64])
    w2a_d = din("w2a", [2, 2, 17, 256])
    gnorm_d = din("gnorm", [2, 128])
    cwT_d = din("cwT", [2, 512, 31])
    cb_d = din("cb", [2, 128, 4])
    clg_d = din("clg", [2, 512])
    clb_d = din("clb", [2, 512])
    w_out = din("w_out", [2, D, D])
    plg_d = din("plg", [2, D])
    plb_d = din("plb", [2, D])
    cos_d = din("ropecos", [4096, 64])
    sin_d = din("ropesin", [4096, 64])
    ident_d = din("ident", [128, 128])
    tri_d = din("tri", [4, 128, 128])
    out = nc.dram_tensor("out", [4096, D], F32, kind="ExternalOutput").ap()

    MOD = dscr("MOD", [2, 2, 3 * D])
    QT = dscr("QT", [1024, T], BF16)
    KT = dscr("KT", [1024, T], BF16)
    VA = dscr("VA", [T, 1024], BF16)
    TM = dscr("TM", [T, TMW])
    CT = dscr("CT", [1024, T])
    CAT = dscr("CAT", [T, D])
    X1 = dscr("X1", [T, D])

    P = Prog(nc)
    psum = nc.alloc_psum_tensor("psum", [128, 4096], F32)

    def bank(i, n=512, p0=0, p1=128):
        return psum[p0:p1, i * 512:i * 512 + n]

    ident = nc.alloc_sbuf_tensor("ident_sb", [128, 128], F32)
    P.dma('sp', lambda e: e.dma_start(out=ident[:], in_=ident_d), w=['ident'])

    uid = [0]

    def U(s):
        uid[0] += 1
        return f"{s}_{uid[0]}"

    def phase_mod():
        with ExitStack() as es:
            sb = lambda n, s, d=F32: es.enter_context(nc.sbuf_tensor(U(n), s, d))
            cT = sb("cT", [128, 2, 16])
            scT = sb("scT", [128, 2, 16])
            L = sb("L", [128, 16, 128])
            wbuf = [sb("wada", [128, 16, 512]) for _ in range(4)]
            bada = sb("bada", [128, 3 * D])
            modrow = sb("modrow", [128, 3 * D])
            P.dma('sp', lambda e: e.dma_start(out=cT[:], in_=cT_d), w=['cT'])
            P.op('act', lambda e: e.activation(out=scT[:], in_=cT[:], func=AF.Silu), r=['cT'], w=['scT'])
            for wh in range(2):
                P.op('dve', lambda e, wh=wh: e.tensor_copy(
                    out=L[:, :, wh * 64:(wh + 1) * 64],
                    in_=scT[:, wh:wh + 1, :].rearrange("p o k -> p k o").broadcast_to([128, 16, 64])),
                    r=['scT'], w=[('L', wh)])
            gi = 0
            for l in range(n_layers):
                P.dma('sp', lambda e, l=l: e.dma_start(out=bada[:], in_=b_ada[l].partition_broadcast(128)),
                      r=[], w=['bada'])
                wv = w_ada[l].rearrange("(kc p) n -> p kc n", p=128)
                for ng in range(12):
                    wb = wbuf[gi % 4]
                    for hf in range(2):
                        P.dma('sp' if hf == 0 else 'act', lambda e, wb=wb, wv=wv, ng=ng, hf=hf: e.dma_start(
                            out=wb[:, hf * 8:(hf + 1) * 8, :], in_=wv[:, hf * 8:(hf + 1) * 8, ng * 512:(ng + 1) * 512]),
                            w=[('wada', gi % 4, hf)])
                    bk = gi % 2
                    for kc in range(16):
                        P.op('pe', lambda e, wb=wb, kc=kc, bk=bk: e.matmul(
                            bank(bk), lhsT=L[:, kc, :], rhs=wb[:, kc, :], start=(kc == 0), stop=(kc == 15)),
                            r=[('wada', gi % 4, kc // 8), ('L', 0), ('L', 1)], w=[('pb', bk)])
                    P.op('dve', lambda e, ng=ng, bk=bk: e.tensor_tensor(
                        out=modrow[:, ng * 512:(ng + 1) * 512], in0=bank(bk), in1=bada[:, ng * 512:(ng + 1) * 512], op=ALU.add),
                        r=[('pb', bk), 'bada'], w=['modrow'])
                    gi += 1
                for wh in range(2):
                    P.dma('sp', lambda e, l=l, wh=wh: e.dma_start(out=MOD[l, wh:wh + 1, :], in_=modrow[wh * 64:wh * 64 + 1, :]),
                          r=['modrow'], w=[('MOD', l)])
        P.barrier()

    def phase_inproj(l):
        src = xin if l == 0 else X1
        with ExitStack() as es:
            sb = lambda n, s, d=F32: es.enter_context(nc.sbuf_tensor(U(n), s, d))
            GTL = 17
            hT = sb("hT", [128, 16, GTL * 128], BF16)
            wg = [sb("wg", [128, 16, 512], BF16) for _ in range(2)]
            xt = [sb("xt", [128, D]) for _ in range(3)]
            st32 = [sb("st32", [128, 512]) for _ in range(4)]
            st16 = [sb("st16", [128, 512], BF16) for _ in range(4)]
            t16 = sb("t16", [16, 4, 128])
            cols = sb("cols", [128, 4, 16])
            stats = [sb("stats", [128, 4, 6]) for _ in range(3)]
            mv = [sb("mv", [128, 2]) for _ in range(3)]
            rs = [sb("rs", [128, 1]) for _ in range(3)]
            nmr = [sb("nmr", [128, 1]) for _ in range(3)]
            for wh in range(2):
                for j in range(2):
                    i = wh * 2 + j
                    P.dma('sp', lambda e, wh=wh, j=j, i=i: e.dma_start(
                        out=t16[:, i, :], in_=MOD[l, wh, j * D:(j + 1) * D].rearrange("(k p) -> k p", p=128)),
                        r=[('MOD', l)], w=[('t16', i)])
                    P.op('pe', lambda e, i=i: e.transpose(bank(0, 16), t16[:, i, :], ident[:16, :16]),
                         r=[('t16', i), 'ident'], w=[('pb', 0)])
                    if j == 0:
                        P.op('dve', lambda e, i=i: e.tensor_copy(out=cols[:, i, :], in_=bank(0, 16)),
                             r=[('pb', 0)], w=[('cols', i)])
                    else:
                        P.op('dve', lambda e, i=i: e.tensor_scalar(out=cols[:, i, :], in0=bank(0, 16), scalar1=1.0, scalar2=None, op0=ALU.add),
                             r=[('pb', 0)], w=[('cols', i)])
            if dbg and l == 0:
                DBGC2 = dscr('DBGC2', [128, 4, 16])
                P.dma('sp', lambda e: e.dma_start(out=DBGC2, in_=cols[:]), r=[('cols', i) for i in range(4)], w=['dbgc2'])
            wv = w_in[l].rearrange("(kc p) n -> p kc n", p=128)
            gcount = 0
            stc = [0]
            import os as _os
            for g in range(2 if _os.environ.get('KT_NOLN') is None else 0):
                def ln1(tl):
                    ti = g * GTL + tl
                    i3 = ti % 3
                    xb = xt[i3]
                    kx = ('xt', i3)
                    P.dma('sp', lambda e: e.dma_start(out=xb[:], in_=src[ti * 128:(ti + 1) * 128, :]), w=[kx])
                    for q in range(4):
                        P.op('dve', lambda e, q=q: e.bn_stats(out=stats[i3][:, q, :], in_=xb[:, q * 512:(q + 1) * 512]),
                             r=[kx], w=[('stats', i3, q)])
                    P.op('dve', lambda e: e.bn_aggr(out=mv[i3][:], in_=stats[i3][:].rearrange("p a b -> p (a b)")),
                         r=[('stats', i3, q) for q in range(4)], w=[('mv', i3)])
                    P.op('act', lambda e: e.activation(out=rs[i3][:], in_=mv[i3][:, 1:2], func=AF.Sqrt, bias=eps5[:, 0:1], scale=1.0),
                         r=[('mv', i3)], w=[('rs', i3)])
                    P.op('dve', lambda e: e.reciprocal(out=rs[i3][:], in_=rs[i3][:]), r=[('rs', i3)], w=[('rs', i3)])
                    P.op('dve', lambda e: e.tensor_scalar(out=nmr[i3][:], in0=mv[i3][:, 0:1], scalar1=rs[i3][:, 0:1], scalar2=-1.0,
                                                          op0=ALU.mult, op1=ALU.mult), r=[('mv', i3), ('rs', i3)], w=[('nmr', i3)])
                    P.op('act', lambda e: e.activation(out=xb[:], in_=xb[:], func=AF.Identity,
                                                       bias=nmr[i3][:, 0:1], scale=rs[i3][:, 0:1]),
                         r=[kx, ('rs', i3), ('nmr', i3)], w=[kx])

                def ln2(tl):
                    ti = g * GTL + tl
                    i3 = ti % 3
                    xb = xt[i3]
                    kx = ('xt', i3)
                    wh = 1 if ti < 2 else 0
                    for kc in range(16):
                        bk = 4 + kc // 4
                        sl = psum[:, bk * 512 + (kc % 4) * 128: bk * 512 + (kc % 4 + 1) * 128]
                        kp = ('pb', bk)
                        P.op('pe', lambda e, kc=kc, sl=sl: e.transpose(sl, xb[:, kc * 128:(kc + 1) * 128], ident[:]),
                             r=[kx, 'ident'], w=[kp])
                        dst = hT[:, kc, tl * 128:(tl + 1) * 128]
                        if kc % 2 == 0:
                            P.op('act', lambda e, dst=dst, sl=sl, kc=kc: e.activation(
                                out=dst, in_=sl, func=AF.Identity, bias=cols[:, wh * 2, kc:kc + 1], scale=cols[:, wh * 2 + 1, kc:kc + 1]),
                                r=[kp, ('cols', wh * 2), ('cols', wh * 2 + 1)], w=[('hT', tl, kc)])
                        else:
                            P.op('dve', lambda e, dst=dst, sl=sl, kc=kc: e.tensor_scalar(
                                out=dst, in0=sl, scalar1=cols[:, wh * 2 + 1, kc:kc + 1], scalar2=cols[:, wh * 2, kc:kc + 1],
                                op0=ALU.mult, op1=ALU.add),
                                r=[kp, ('cols', wh * 2), ('cols', wh * 2 + 1)], w=[('hT', tl, kc)])

                import os as _os
                if _os.environ.get('KT_NOLN') is None:
                    ln1(0)
                    for tl in range(GTL):
                        if tl + 1 < GTL:
                            ln1(tl + 1)
                        ln2(tl)
                if dbg and g == 0 and l == 0:
                    DBGH = dscr('DBGH', [128, 16, GTL * 128], BF16)
                    DBGC = dscr('DBGC', [128, 4, 16])
                    P.dma('sp', lambda e: e.dma_start(out=DBGH, in_=hT[:]), r=[('hT', tl, kc) for tl in range(GTL) for kc in range(16)], w=['dbgh'])
                    P.dma('sp', lambda e: e.dma_start(out=DBGC, in_=cols[:]), r=[('cols', i) for i in range(4)], w=['dbgc'])
                    DBGT = dscr('DBGT', [16, 4, 128])
                    P.dma('sp', lambda e: e.dma_start(out=DBGT, in_=t16[:]), r=[('t16', i) for i in range(4)], w=['dbgt'])
                import os as _os
                for (kind, c0, ncols, doff) in (GROUPS if _os.environ.get('KT_NOPROJ') is None else []):
                    wb = wg[gcount % 2]
                    kw = ('wg', gcount % 2)
                    gcount += 1
                    P.dma('pool', lambda e, wb=wb, c0=c0, ncols=ncols: e.dma_start(out=wb[:, :, 0:ncols], in_=wv[:, :, c0:c0 + ncols]),
                          w=[kw])
                    if kind in ('tm', 'va'):
                        for tl in range(GTL):
                            ti = g * GTL + tl
                            bk = stc[0] % 4
                            for kc in range(16):
                                P.op('pe', lambda e, wb=wb, kc=kc, tl=tl, bk=bk, ncols=ncols: e.matmul(
                                    bank(bk, ncols), lhsT=hT[:, kc, tl * 128:(tl + 1) * 128], rhs=wb[:, kc, 0:ncols],
                                    start=(kc == 0), stop=(kc == 15)),
                                    r=[kw, ('hT', tl, kc)], w=[('pb', bk)])
                            si = stc[0] % 4
                            stg = (st32 if kind == 'tm' else st16)[si]
                            ks = ('st', kind == 'tm', si)
                            eng = 'act' if stc[0] % 2 == 0 else 'dve'
                            if eng == 'act':
                                P.op('act', lambda e, stg=stg, bk=bk, ncols=ncols: e.activation(out=stg[:, 0:ncols], in_=bank(bk, ncols), func=AF.Copy),
                                     r=[('pb', bk)], w=[ks])
                            else:
                                P.op('dve', lambda e, stg=stg, bk=bk, ncols=ncols: e.tensor_copy(out=stg[:, 0:ncols], in_=bank(bk, ncols)),
                                     r=[('pb', bk)], w=[ks])
                            dst = (TM if kind == 'tm' else VA)[ti * 128:(ti + 1) * 128, doff:doff + ncols]
                            P.dma('sp', lambda e, dst=dst, stg=stg, ncols=ncols: e.dma_start(out=dst, in_=stg[:, 0:ncols]),
                                  r=[ks], w=[(kind, ti, doff)])
                            stc[0] += 1
                    else:
                        dt_ = {'qt': QT, 'kt': KT, 'ct': CT}[kind]
                        for blk in range(ncols // 128):
                            tok0 = 0
                            while tok0 < GTL * 128:
                                ntok = min(512, GTL * 128 - tok0)
                                bk = stc[0] % 4
                                for kc in range(16):
                                    P.op('pe', lambda e, wb=wb, kc=kc, blk=blk, tok0=tok0, ntok=ntok, bk=bk: e.matmul(
                                        bank(bk, ntok), lhsT=wb[:, kc, blk * 128:(blk + 1) * 128], rhs=hT[:, kc, tok0:tok0 + ntok],
                                        start=(kc == 0), stop=(kc == 15)),
                                        r=[kw] + [('hT', tt, kc) for tt in range(tok0 // 128, (tok0 + ntok) // 128)], w=[('pb', bk)])
                                si = stc[0] % 4
                                is32 = kind == 'ct'
                                stg = (st32 if is32 else st16)[si]
                                ks = ('st', is32, si)
                                sc = 0.125 if kind == 'qt' else 1.0
                                if stc[0] % 2 == 0:
                                    P.op('act', lambda e, stg=stg, bk=bk, ntok=ntok, sc=sc: e.activation(
                                        out=stg[:, 0:ntok], in_=bank(bk, ntok), func=AF.Copy, scale=sc),
                                        r=[('pb', bk)], w=[ks])
                                else:
                                    P.op('dve', lambda e, stg=stg, bk=bk, ntok=ntok, sc=sc: e.tensor_scalar(
                                        out=stg[:, 0:ntok], in0=bank(bk, ntok), scalar1=sc, scalar2=None, op0=ALU.mult),
                                        r=[('pb', bk)], w=[ks])
                                r0 = doff + blk * 128
                                g0 = g * GTL * 128 + tok0
                                dst = dt_[r0:r0 + 128, g0:g0 + ntok]
                                P.dma('sp', lambda e, dst=dst, stg=stg, ntok=ntok: e.dma_start(out=dst, in_=stg[:, 0:ntok]),
                                      r=[ks], w=[(kind, r0, g0)])
                                stc[0] += 1
                                tok0 += ntok
        P.barrier()

    def phase_na(l):
        last = (l == n_layers - 1)
        LAG = 2
        with ExitStack() as es:
            sb = lambda n, s, d=F32: es.enter_context(nc.sbuf_tensor(U(n), s, d))
            qh = [sb("qh", [64, T], BF16) for _ in range(2)]
            kh = [sb("kh", [64, T], BF16) for _ in range(2)]
            vh = [sb("vh", [128, NT, 65], BF16) for _ in range(2)]
            zh = [sb("zh", [128, NT, 64]) for _ in range(2)]
            bt = [sb("bt", [128, 16, 64]) for _ in range(2)]
            oh = [sb("oh", [128, NT, 64]) for _ in range(2)]
            nmask = sb("nmask", [128, 64])
            btB0 = [sb("btB0", [128, 16, 64]) for _ in range(2)]
            btB1 = [sb("btB1", [128, 16, 64]) for _ in range(2)]
            sT = [sb("sT", [128, 640]) for _ in range(3)]
            pT = [sb("pT", [128, 896], BF16) for _ in range(3)]
            rc = [sb("rc", [128, 1]) for _ in range(3)]
            for i in range(3):
                P.op('pool', lambda e, i=i: e.memset(sT[i][:, 512:576], NEG), w=[('sTc', i)])
            P.dma('sp', lambda e: e.dma_start(out=nmask[:], in_=nmask_d), w=['nmask'])
            for i in range(2):
                P.op('pool', lambda e, i=i: e.memset(vh[i][:, :, 64:65], 1.0), w=[('vh1', i)])
            t_first = 2 if last else 0

            def head_load(h):
                hb = h % 2
                P.dma('sp', lambda e: e.dma_start(out=qh[hb][:], in_=QT[h * 64:(h + 1) * 64, :]), w=[('qh', hb)])
                P.dma('sp', lambda e: e.dma_start(out=kh[hb][:], in_=KT[h * 64:(h + 1) * 64, :]), w=[('kh', hb)])
                P.dma('sp', lambda e: e.dma_start(
                    out=vh[hb][:, :, 0:64], in_=VA[:, h * 64:(h + 1) * 64].rearrange("(t p) d -> p t d", p=128)), w=[('vh', hb)])
                P.dma('sp', lambda e: e.dma_start(
                    out=zh[hb][:], in_=TM[:, TM_ZA + h * 64:TM_ZA + (h + 1) * 64].rearrange("(t p) d -> p t d", p=128)), w=[('zh', hb)])
                P.dma('sp', lambda e: e.dma_start(
                    out=bt[hb][:], in_=btab[l, h].rearrange("p (a c) -> p a c", c=64)), w=[('bt', hb)])
                P.op('dve', lambda e: e.tensor_tensor(
                    out=bt[hb][:], in0=bt[hb][:], in1=nmask[:, :].unsqueeze(1).broadcast_to([128, 16, 64]), op=ALU.add),
                    r=[('bt', hb), 'nmask'], w=[('bt', hb)])
                P.op('act', lambda e: e.activation(out=zh[hb][:], in_=zh[hb][:], func=AF.Silu),
                     r=[('zh', hb)], w=[('zh', hb)])
                P.op('pool', lambda e: e.tensor_copy(out=btB0[hb][:], in_=bt[hb][:]), r=[('bt', hb)], w=[('btB0', hb)])
                P.op('pool', lambda e: e.memset(btB0[hb][:, 12, :], NEG), r=[('btB0', hb)], w=[('btB0', hb)])
                P.op('pool', lambda e: e.tensor_copy(out=btB1[hb][:], in_=bt[hb][:]), r=[('bt', hb)], w=[('btB1', hb)])
                P.op('pool', lambda e: e.memset(btB1[hb][0:64, 3, :], NEG), r=[('btB1', hb)], w=[('btB1', hb)])
                P.op('pool', lambda e: e.memset(btB1[hb][64:128, 11, :], NEG), r=[('btB1', hb)], w=[('btB1', hb)])

            def head_store(h):
                hb = h % 2
                P.dma('sp', lambda e: e.dma_start(
                    out=CAT[t_first * 128:, h * 64:(h + 1) * 64].rearrange("(t p) d -> p t d", p=128), in_=oh[hb][:, t_first:, :]),
                    r=[('oh', hb, tt) for tt in range(t_first, NT)], w=[('CATa', h)])

            def geom(ti):
                if ti < 2:
                    return [], 'ctx'
                r = 2 * (ti - 2)
                rs0 = min(max(r - 4, 0), 56)
                rs1 = min(max(r - 3, 0), 56)
                m0 = rs0 // 2
                if rs1 == rs0:
                    return [2 + m0 + j for j in range(4)], 'A'
                return [2 + m0 + j for j in range(5)], 'B'

            def stage1(u, i2):
                h, ti = u
                hb = h % 2
                bx, by = 2 * i2, 2 * i2 + 1
                loc, case = geom(ti)
                q0 = ti * 128
                kq = [('kh', hb), ('qh', hb)]
                for j, tix in enumerate(loc[:4]):
                    P.op('pe', lambda e, j=j, tix=tix: e.matmul(
                        psum[:, bx * 512 + j * 128: bx * 512 + (j + 1) * 128],
                        lhsT=kh[hb][:, tix * 128:(tix + 1) * 128], rhs=qh[hb][:, q0:q0 + 128], start=True, stop=True),
                        r=kq, w=[('pb', bx)])
                if case == 'B':
                    tix = loc[4]
                    P.op('pe', lambda e, tix=tix: e.matmul(
                        psum[:, by * 512: by * 512 + 128],
                        lhsT=kh[hb][:, tix * 128:(tix + 1) * 128], rhs=qh[hb][:, q0:q0 + 128], start=True, stop=True),
                        r=kq, w=[('pb', by)])
                for j in range(2):
                    P.op('pe', lambda e, j=j: e.matmul(
                        psum[:, by * 512 + 128 + j * 128: by * 512 + 256 + j * 128],
                        lhsT=kh[hb][:, j * 128:(j + 1) * 128], rhs=qh[hb][:, q0:q0 + 128], start=True, stop=True),
                        r=kq, w=[('pb', by)])
                nloc = len(loc)
                ks = ('sT', i2)
                if case != 'ctx':
                    r = 2 * (ti - 2)
                    m0 = loc[0] - 2
                    d0i = 2 * m0 - r + 8
                    sT4 = sT[i2][:, 0:512].rearrange("p (a h c) -> p a h c", a=4, h=2, c=64)
                    ps4 = psum[:, bx * 512: bx * 512 + 512].rearrange("p (a h c) -> p a h c", a=4, h=2, c=64)
                    for hf in range(2):
                        di = d0i - hf
                        tab = bt[hb] if case == 'A' else (btB0[hb] if hf == 0 else btB1[hb])
                        ktab = ('bt', hb) if case == 'A' else (('btB0', hb) if hf == 0 else ('btB1', hb))
                        P.op('dve', lambda e, hf=hf, di=di, tab=tab: e.tensor_tensor(
                            out=sT4[:, :, hf, :], in0=ps4[:, :, hf, :], in1=tab[:, di:di + 7:2, :], op=ALU.add),
                            r=[('pb', bx), ktab], w=[ks])
                    if case == 'B':
                        di = d0i - 1 + 8
                        P.op('dve', lambda e, di=di: e.tensor_tensor(
                            out=sT[i2][:, 576:640], in0=psum[:, by * 512 + 64: by * 512 + 128], in1=btB1[hb][:, di, :], op=ALU.add),
                            r=[('pb', by), ('btB1', hb)], w=[ks])
                    P.op('act', lambda e: e.activation(out=pT[i2][:, 0:nloc * 128], in_=sT[i2][:, 0:nloc * 128], func=AF.Exp),
                         r=[ks, ('sTc', i2)], w=[('pT', i2)])
                P.op('act', lambda e: e.activation(
                    out=pT[i2][:, 640:896], in_=psum[:, by * 512 + 128: by * 512 + 384], func=AF.Exp),
                    r=[('pb', by), ('pT', i2)], w=[('pT', i2)])

            def stage2(u, i2, o2):
                h, ti = u
                hb = h % 2
                obk = 6 + o2
                loc, case = geom(ti)
                slots = [(j * 128, tix) for j, tix in enumerate(loc)] + [(640, 0), (768, 1)]
                ns = len(slots)
                for s_, (c0, tix) in enumerate(slots):
                    P.op('pe', lambda e, s_=s_, c0=c0, tix=tix: e.matmul(
                        bank(obk, 65), lhsT=pT[i2][:, c0:c0 + 128], rhs=vh[hb][:, tix, :],
                        start=(s_ == 0), stop=(s_ == ns - 1)),
                        r=[('pT', i2), ('vh', hb), ('vh1', hb)], w=[('pb', obk)])
                P.op('dve', lambda e: e.reciprocal(out=rc[i2][:], in_=psum[:, obk * 512 + 64: obk * 512 + 65]),
                     r=[('pb', obk)], w=[('rc', i2)])
                P.op('dve', lambda e: e.scalar_tensor_tensor(
                    out=oh[hb][:, ti, :], in0=bank(obk, 64), scalar=rc[i2][:, 0:1], in1=zh[hb][:, ti, :],
                    op0=ALU.mult, op1=ALU.mult),
                    r=[('pb', obk), ('rc', i2), ('zh', hb)], w=[('oh', hb, ti)])

            units = [(h, ti) for h in range(16) for ti in range(t_first, NT)]
            nU = len(units)
            head_load(0)
            head_load(1)
            for i in range(nU + LAG):
                if i < nU:
                    stage1(units[i], i % 3)
                j = i - LAG
                if j >= 0:
                    stage2(units[j], j % 3, j % 2)
                    if j == nU - 1 or units[j + 1][0] != units[j][0]:
                        hd = units[j][0]
                        head_store(hd)
                        if hd + 2 < 16:
                            head_load(hd + 2)
        P.barrier()

    def phase_gla(l):
        last = (l == n_layers - 1)
        with ExitStack() as es:
            sb = lambda n, s, d=F32: es.enter_context(nc.sbuf_tensor(U(n), s, d))
            tri = sb("tri", [128, 4, 128])
            w2a = sb("w2a", [17, 2, 256])
            cosb = sb("cosb", [128, 32, 64])
            sinb = sb("sinb", [128, 32, 64])
            gn = sb("gn", [128, 128])
            ost = sb("ost", [128, NT, 512])
            D2 = range(2)
            S = [sb("S", [64, 4, 128]) for _ in D2]
            tb = [[sb("tb", [128, 1568]) for _ in range(2)] for _ in D2]
            qk = [sb("qk", [128, 512]) for _ in D2]
            t2 = [sb("t2", [128, 512]) for _ in D2]
            qT = [sb("qT", [64, 4, 128]) for _ in D2]
            kT = [sb("kT", [64, 4, 128]) for _ in D2]
            lrT = [sb("lrT", [17, 128]) for _ in D2]
            ee = [sb("ee", [128, 256]) for _ in D2]
            spt = [sb("spt", [128, 256]) for _ in D2]
            ekd = [sb("ekd", [128, 256]) for _ in D2]
            kd = [sb("kd", [128, 256]) for _ in D2]
            eb = [sb("eb", [64, 4, 128]) for _ in D2]
            enb = [sb("enb", [64, 4, 128]) for _ in D2]
            qt = [sb("qt", [64, 4, 128]) for _ in D2]
            kt = [sb("kt", [64, 4, 128]) for _ in D2]
            am = [sb("am", [128, 4, 128]) for _ in D2]
            osum = [sb("osum", [128, 512]) for _ in D2]
            zt = [sb("zt", [128, 512]) for _ in D2]
            ob = [sb("ob", [128, 512]) for _ in D2]
            ssq = [sb("ssq", [128, 4]) for _ in D2]
            rstd = [sb("rstd", [128, 4]) for _ in D2]
            P.dma('sp', lambda e: e.dma_start(out=tri[:], in_=tri_d.rearrange("a p q -> p a q")), w=['tri'])
            P.dma('sp', lambda e: e.dma_start(out=w2a[:], in_=w2a_d[l].rearrange("d k n -> k d n")), w=['w2a'])
            P.dma('sp', lambda e: e.dma_start(out=cosb[:], in_=cos_d.rearrange("(t p) e -> p t e", p=128)), w=['cosb'])
            P.dma('sp', lambda e: e.dma_start(out=sinb[:], in_=sin_d.rearrange("(t p) e -> p t e", p=128)), w=['sinb'])
            P.dma('sp', lambda e: e.dma_start(out=gn[:], in_=gnorm_d[l].partition_broadcast(128)), w=['gn'])
            for d in D2:
                P.op('pool', lambda e, d=d: e.memset(lrT[d][:], 1.0), w=[('lrT1', d)])
                P.op('pool', lambda e, d=d: e.memset(S[d][:], 0.0), w=[('S', d, h) for h in range(4)])
            cnt = [0, 0]
            f2 = lambda t: t[:].rearrange("p h t -> p (h t)")
            order = [list(range(NT)), [1, 0] + list(range(NT - 1, 1, -1))]
            idx = [{t: i for i, t in enumerate(order[d])} for d in D2]

            pend = [[], []]

            def flush_store(dr):
                while pend[dr]:
                    tj = pend[dr].pop(0)
                    P.dma('sp', lambda e, tj=tj: e.dma_start(out=CAT[tj * 128:(tj + 1) * 128, 1024:1536], in_=ob[dr][:]),
                          r=[('ob', dr, h) for h in range(4)], w=[('CATb', tj)])

            def gla_tile(ti, dr):
                K = lambda n: (n, dr)
                B0, B1, B2, B3 = 4 * dr, 4 * dr + 1, 4 * dr + 2, 4 * dr + 3
                lat = ti >= 2
                lt = ti - 2
                tbb = tb[dr][cnt[dr] % 2]
                ktb = ('tb', dr, cnt[dr] % 2)
                cnt[dr] += 1
                P.dma('sp', lambda e: e.dma_start(out=tbb[:], in_=TM[ti * 128:(ti + 1) * 128, TM_QB:TM_QB + 1568]), w=[ktb])
                flush_store(dr)
                if lat:
                    src3 = tbb[:, 0:512].rearrange("p (h e) -> p h e", h=8)
                    P.op('pool', lambda e: e.tensor_tensor(
                        out=qk[dr][:].rearrange("p (h e) -> p h e", h=8), in0=src3,
                        in1=cosb[:, lt:lt + 1, :].broadcast_to([128, 8, 64]), op=ALU.mult),
                        r=[ktb, 'cosb'], w=[K('qk')])
                    src5 = tbb[:, 0:512].rearrange("p (h a b e) -> p h a b e", h=8, a=2, b=2, e=16)
                    t25 = t2[dr][:].rearrange("p (h a b e) -> p h a b e", h=8, a=2, b=2, e=16)
                    sin5 = sinb[:, lt, :].rearrange("p (a b e) -> p a b e", a=2, b=2, e=16)
                    for bsel in range(2):
                        P.op('pool', lambda e, bsel=bsel: e.tensor_tensor(
                            out=t25[:, :, :, bsel, :], in0=src5[:, :, :, 1 - bsel, :],
                            in1=sin5[:, :, bsel, :].unsqueeze(1).broadcast_to([128, 8, 2, 16]), op=ALU.mult),
                            r=[ktb, 'sinb'], w=[('t2', dr, bsel)])
                    yield
                    P.op('pool', lambda e: e.tensor_tensor(out=qk[dr][:], in0=qk[dr][:], in1=t2[dr][:], op=ALU.add),
                         r=[K('qk'), ('t2', dr, 0), ('t2', dr, 1)], w=[K('qk')])
                    src = qk[dr]
                    ksrc = K('qk')
                else:
                    src = tbb
                    ksrc = ktb
                P.op('pe', lambda e: e.transpose(bank(B2, 128, 0, 16), tbb[:, 1536 + 16 * dr:1536 + 16 * dr + 16], ident[:]),
                     r=[ktb, 'ident'], w=[('pb', B2)])
                yield
                P.op('dve', lambda e: e.tensor_copy(out=lrT[dr][0:16, :], in_=bank(B2, 128, 0, 16)),
                     r=[('pb', B2), ('lrT1', dr)], w=[K('lrT')])
                yield
                P.op('pe', lambda e: e.matmul(psum[:, B2 * 512 + 256: B2 * 512 + 512], lhsT=lrT[dr][0:17, :], rhs=w2a[0:17, dr, :], start=True, stop=True),
                     r=[K('lrT'), ('lrT1', dr), 'w2a'], w=[('pb', B2)])
                yield
                P.op('act', lambda e: e.activation(out=ee[dr][:], in_=psum[:, B2 * 512 + 256: B2 * 512 + 512], func=AF.Exp, scale=-1.0),
                     r=[('pb', B2)], w=[K('ee')])
                P.op('act', lambda e: e.activation(out=spt[dr][:], in_=ee[dr][:], func=AF.Ln, bias=one1[:, 0:1], scale=1.0),
                     r=[K('ee'), 'one1'], w=[K('spt')])
                yield
                for h in range(4):
                    P.op('pe', lambda e, h=h: e.transpose(psum[0:64, B0 * 512 + h * 128: B0 * 512 + (h + 1) * 128],
                                                          src[:, h * 64:(h + 1) * 64], ident[:]),
                         r=[ksrc, 'ident'], w=[('pb', B0)])
                for h in range(4):
                    P.op('pe', lambda e, h=h: e.transpose(psum[0:64, B1 * 512 + h * 128: B1 * 512 + (h + 1) * 128],
                                                          src[:, 256 + h * 64:256 + (h + 1) * 64], ident[:]),
                         r=[ksrc, 'ident'], w=[('pb', B1)])
                yield
                P.op('act', lambda e: e.activation(out=f2(qT[dr]), in_=bank(B0, 512, 0, 64), func=AF.Copy, scale=0.125),
                     r=[('pb', B0)], w=[K('qT')])
                P.op('act', lambda e: e.activation(out=f2(kT[dr]), in_=bank(B1, 512, 0, 64), func=AF.Copy),
                     r=[('pb', B1)], w=[K('kT')])
                yield
                P.op('pe', lambda e: e.matmul(bank(B0, 256), lhsT=tri[:, 2 * dr, :], rhs=spt[dr][:], start=True, stop=True),
                     r=['tri', K('spt')], w=[('pb', B0)])
                for h in range(4):
                    P.op('pe', lambda e, h=h: e.matmul(psum[0:64, B1 * 512 + h * 128: B1 * 512 + (h + 1) * 128],
                                                       lhsT=spt[dr][:, h * 64:(h + 1) * 64], rhs=tri[:, 2 * dr + 1, :], start=True, stop=True),
                         r=[K('spt'), 'tri'], w=[('pb', B1)])
                yield
                P.op('act', lambda e: e.activation(out=ekd[dr][:], in_=bank(B0, 256), func=AF.Exp, scale=-1.0 / 16),
                     r=[('pb', B0)], w=[K('ekd')])
                P.op('act', lambda e: e.activation(out=f2(eb[dr]), in_=bank(B1, 512, 0, 64), func=AF.Exp, scale=-1.0 / 16),
                     r=[('pb', B1)], w=[K('eb')])
                P.op('act', lambda e: e.activation(out=f2(enb[dr]), in_=bank(B1, 512, 0, 64), func=AF.Exp, scale=1.0 / 16),
                     r=[('pb', B1)], w=[K('enb')])
                yield
                P.op('dve', lambda e: e.tensor_tensor(out=kd[dr][:], in0=src[:, 256:512], in1=ekd[dr][:], op=ALU.mult),
                     r=[ksrc, K('ekd')], w=[K('kd')])
                P.op('dve', lambda e: e.tensor_tensor(out=f2(qt[dr]), in0=f2(qT[dr]), in1=f2(eb[dr]), op=ALU.mult),
                     r=[K('qT'), K('eb')], w=[K('qt')])
                P.op('dve', lambda e: e.tensor_tensor(out=f2(kt[dr]), in0=f2(kT[dr]), in1=f2(enb[dr]), op=ALU.mult),
                     r=[K('kT'), K('enb')], w=[K('kt')])
                yield
                need_o = lat or (not last)
                if need_o:
                    for h in range(4):
                        P.op('pe', lambda e, h=h: e.matmul(psum[:, B2 * 512 + h * 128: B2 * 512 + (h + 1) * 128],
                                                           lhsT=kt[dr][:, h, :], rhs=qt[dr][:, h, :], start=True, stop=True),
                             r=[K('kt'), K('qt')], w=[('pb', B2)])
                for h in range(4):
                    P.op('pe', lambda e, h=h: e.matmul(psum[0:64, B0 * 512 + h * 128: B0 * 512 + (h + 1) * 128],
                                                       lhsT=kd[dr][:, h * 64:(h + 1) * 64], rhs=tbb[:, 512 + h * 128:512 + (h + 1) * 128], start=True, stop=True),
                         r=[K('kd'), ktb], w=[('pb', B0)])
                yield
                if need_o:
                    P.op('dve', lambda e: e.tensor_tensor(
                        out=am[dr][:], in0=bank(B2).rearrange("p (h t) -> p h t", h=4),
                        in1=tri[:, 2 * dr + 1:2 * dr + 2, :].broadcast_to([128, 4, 128]), op=ALU.mult),
                        r=[('pb', B2), 'tri'], w=[K('am')])
                    yield
                    for h in range(4):
                        P.op('pe', lambda e, h=h: e.matmul(psum[:, B3 * 512 + h * 128: B3 * 512 + (h + 1) * 128],
                                                           lhsT=am[dr][:, h, :], rhs=tbb[:, 512 + h * 128:512 + (h + 1) * 128], start=True, stop=False),
                             r=[K('am'), ktb], w=[('pb', B3)])
                        P.op('pe', lambda e, h=h: e.matmul(psum[:, B3 * 512 + h * 128: B3 * 512 + (h + 1) * 128],
                                                           lhsT=qt[dr][:, h, :], rhs=S[dr][:, h, :], start=False, stop=True),
                             r=[K('qt'), ('S', dr, h)], w=[('pb', B3)])
                    yield
                col = 127 if dr == 0 else 0
                for h in range(4):
                    P.op('dve', lambda e, h=h: e.scalar_tensor_tensor(
                        out=S[dr][:, h, :], in0=S[dr][:, h, :], scalar=eb[dr][:, h, col:col + 1],
                        in1=psum[0:64, B0 * 512 + h * 128: B0 * 512 + (h + 1) * 128], op0=ALU.mult, op1=ALU.add),
                        r=[('S', dr, h), K('eb'), ('pb', B0)], w=[('S', dr, h)])
                yield
                if not need_o:
                    return
                first = idx[dr][ti] < idx[1 - dr][ti]
                if first:
                    P.op('act', lambda e: e.activation(out=ost[:, ti, :], in_=bank(B3), func=AF.Copy), r=[('pb', B3)], w=[('ost', ti)])
                    yield
                    return
                P.op('dve', lambda e: e.tensor_tensor(out=osum[dr][:], in0=bank(B3), in1=ost[:, ti, :], op=ALU.add),
                     r=[('pb', B3), ('ost', ti)], w=[K('osum')])
                kob = [('ob', dr, h) for h in range(4)]
                zb = tbb[:, 1024:1536]
                for h in range(4):
                    P.op('act', lambda e, h=h: e.activation(out=ob[dr][:, h * 128:(h + 1) * 128], in_=osum[dr][:, h * 128:(h + 1) * 128],
                                                          func=AF.Square, accum_out=ssq[dr][:, h:h + 1]),
                         r=[K('osum')] + kob, w=[('ob', dr, h), ('ssq', dr, h)])
                yield
                P.op('act', lambda e: e.activation(out=zt[dr][:], in_=zb, func=AF.Silu), r=[ktb], w=[K('zt')])
                yield
                P.op('act', lambda e: e.activation(out=rstd[dr][:], in_=ssq[dr][:], func=AF.Ln, bias=eps6[:, 0:1], scale=1.0 / 128),
                     r=[('ssq', dr, h) for h in range(4)] + ['eps6'], w=[K('rstd')])
                P.op('act', lambda e: e.activation(out=rstd[dr][:], in_=rstd[dr][:], func=AF.Exp, scale=-0.5), r=[K('rstd')], w=[K('rstd')])
                yield
                P.op('pool', lambda e: e.tensor_tensor(
                    out=zt[dr][:].rearrange("p (h t) -> p h t", h=4), in0=zt[dr][:].rearrange("p (h t) -> p h t", h=4),
                    in1=gn[:, :].unsqueeze(1).broadcast_to([128, 4, 128]), op=ALU.mult), r=[K('zt'), 'gn'], w=[K('zt')])
                yield
                for h in range(4):
                    P.op('dve', lambda e, h=h: e.scalar_tensor_tensor(
                        out=ob[dr][:, h * 128:(h + 1) * 128], in0=osum[dr][:, h * 128:(h + 1) * 128], scalar=rstd[dr][:, h:h + 1],
                        in1=zt[dr][:, h * 128:(h + 1) * 128], op0=ALU.mult, op1=ALU.mult),
                        r=[K('osum'), K('rstd'), K('zt')], w=[('ob', dr, h)])
                pend[dr].append(ti)
                yield

            def chain(dr):
                for ti in order[dr]:
                    yield from gla_tile(ti, dr)
                flush_store(dr)

            gens = [chain(0), chain(1)]
            while gens:
                for g in list(gens):
                    try:
                        next(g)
                    except StopIteration:
                        gens.remove(g)
        P.barrier()

    def phase_conv(l):
        last = (l == n_layers - 1)
        with ExitStack() as es:
            sb = lambda n, s, d=F32: es.enter_context(nc.sbuf_tensor(U(n), s, d))
            cg = sb("cg", [128, T])
            upad = [sb("upad", [128, T + 60]) for _ in range(2)]
            ptmp = [sb("ptmp", [128, 4096]) for _ in range(2)]
            accP = sb("accP", [128, T])
            acc = sb("acc", [128, 4, T])
            cw = sb("cw", [128, 4, 31])
            cbias = sb("cbias", [128, 4])
            lg = sb("lg", [128, 512])
            lb = sb("lb", [128, 512])
            zc = [sb("zc", [128, 512]) for _ in range(2)]
            xn = sb("xn", [128, 512])
            oc = [sb("oc", [128, 512]) for _ in range(2)]
            stats = sb("cstats", [128, 6])
            mv = sb("cmv", [128, 2])
            rs = sb("crs", [128, 1])
            nmr = sb("cnmr", [128, 1])
            P.dma('sp', lambda e: e.dma_start(out=cw[:], in_=cwT_d[l].rearrange("(c p) k -> p c k", p=128)), w=['cw'])
            P.dma('sp', lambda e: e.dma_start(out=cbias[:], in_=cb_d[l]), w=['cbias'])
            P.dma('sp', lambda e: e.dma_start(out=lg[:], in_=clg_d[l].partition_broadcast(128)), w=['lg'])
            P.dma('sp', lambda e: e.dma_start(out=lb[:], in_=clb_d[l].partition_broadcast(128)), w=['lb'])
            for i in range(2):
                P.op('pool', lambda e, i=i: e.memset(upad[i][:], 0.0), w=[('upad', i)])
            segs = [(301, 256, 4096)] + ([] if last else [(15, 0, 256)])

            pcount = [0]
            NDT = 11

            def conv_load(cc):
                b2 = cc % 2
                up = upad[b2]
                ku = ('upad', b2)
                for (u0, a0, n) in segs:
                    P.dma('sp', lambda e, u0=u0, a0=a0, n=n: e.dma_start(out=up[:, u0:u0 + n], in_=CT[cc * 128:(cc + 1) * 128, a0:a0 + n]),
                          r=[], w=[ku])
                P.dma('sp', lambda e: e.dma_start(out=cg[:], in_=CT[512 + cc * 128:512 + (cc + 1) * 128, :]), w=['cg'])

            def conv_chunk(cc):
                b2 = cc % 2
                up = upad[b2]
                ku = ('upad', b2)
                P.op('act', lambda e: e.activation(out=cg[:], in_=cg[:], func=AF.Sigmoid), r=['cg'], w=['cg'])
                for (u0, a0, n) in segs:
                    P.op('dve', lambda e, u0=u0, a0=a0, n=n: e.tensor_tensor(
                        out=up[:, u0:u0 + n], in0=up[:, u0:u0 + n], in1=cg[:, a0:a0 + n], op=ALU.mult),
                        r=[ku, 'cg'], w=[ku])
                if cc + 1 < 4:
                    conv_load(cc + 1)
                for (u0, a0, n) in segs:
                    ka = ('acc', cc, a0)
                    kp = ('accP', a0)
                    P.op('dve', lambda e, u0=u0, a0=a0, n=n: e.tensor_scalar(
                        out=acc[:, cc, a0:a0 + n], in0=up[:, u0 - 15:u0 - 15 + n], scalar1=cw[:, cc, 0:1], scalar2=cbias[:, cc:cc + 1],
                        op0=ALU.mult, op1=ALU.add), r=[ku, 'cw', 'cbias'], w=[ka])
                    for k in range(1, NDT):
                        P.op('dve', lambda e, u0=u0, a0=a0, n=n, k=k: e.scalar_tensor_tensor(
                            out=acc[:, cc, a0:a0 + n], in0=up[:, u0 - 15 + k:u0 - 15 + k + n], scalar=cw[:, cc, k:k + 1],
                            in1=acc[:, cc, a0:a0 + n], op0=ALU.mult, op1=ALU.add), r=[ku, 'cw', ka], w=[ka])
                    for k in range(NDT, 31):
                        if k == NDT:
                            P.op('act', lambda e, u0=u0, a0=a0, n=n, k=k: e.activation(
                                out=accP[:, a0:a0 + n], in_=up[:, u0 - 15 + k:u0 - 15 + k + n], func=AF.Identity, scale=cw[:, cc, k:k + 1]),
                                r=[ku, 'cw'], w=[kp])
                            continue
                        pj = pcount[0] % 2
                        pcount[0] += 1
                        pt = ptmp[pj]
                        P.op('act', lambda e, u0=u0, n=n, k=k, pt=pt: e.activation(
                            out=pt[:, 0:n], in_=up[:, u0 - 15 + k:u0 - 15 + k + n], func=AF.Identity, scale=cw[:, cc, k:k + 1]),
                            r=[ku, 'cw'], w=[('ptmp', pj)])
                        P.op('pool', lambda e, a0=a0, n=n, pt=pt: e.tensor_tensor(
                            out=accP[:, a0:a0 + n], in0=accP[:, a0:a0 + n], in1=pt[:, 0:n], op=ALU.add),
                            r=[('ptmp', pj), kp], w=[kp])
                    P.op('pool', lambda e, a0=a0, n=n: e.tensor_tensor(
                        out=acc[:, cc, a0:a0 + n], in0=acc[:, cc, a0:a0 + n], in1=accP[:, a0:a0 + n], op=ALU.add),
                        r=[ka, kp], w=[ka])

            conv_load(0)
            for cc in range(4):
                conv_chunk(cc)
            tiles = list(range(2, NT)) if last else list(range(NT))
            xn2 = [xn, sb("xn_b", [128, 512])]
            st2 = [stats, sb("cstats_b", [128, 6])]
            mv2 = [mv, sb("cmv_b", [128, 2])]
            rs2 = [rs, sb("crs_b", [128, 1])]
            nm2 = [nmr, sb("cnmr_b", [128, 1])]

            def postA(n_):
                ti = tiles[n_]
                b = n_ % 2
                a0 = 0 if ti < 2 else 256
                P.dma('sp', lambda e: e.dma_start(out=zc[b][:], in_=TM[ti * 128:(ti + 1) * 128, TM_ZC:TM_ZC + 512]), w=[('zc', b)])
                for cc in range(4):
                    P.op('pe', lambda e, cc=cc: e.transpose(psum[:, b * 512 + cc * 128: b * 512 + (cc + 1) * 128],
                                                          acc[:, cc, ti * 128:(ti + 1) * 128], ident[:]),
                         r=[('acc', cc, a0), 'ident'], w=[('pb', b)])
                P.op('dve', lambda e: e.bn_stats(out=st2[b][:], in_=bank(b)), r=[('pb', b)], w=[('cstats', b)])
                P.op('dve', lambda e: e.bn_aggr(out=mv2[b][:], in_=st2[b][:]), r=[('cstats', b)], w=[('cmv', b)])
                P.op('act', lambda e: e.activation(out=rs2[b][:], in_=mv2[b][:, 1:2], func=AF.Sqrt, bias=eps5[:, 0:1], scale=1.0),
                     r=[('cmv', b), 'eps5'], w=[('crs', b)])
                P.op('dve', lambda e: e.reciprocal(out=rs2[b][:], in_=rs2[b][:]), r=[('crs', b)], w=[('crs', b)])
                P.op('dve', lambda e: e.tensor_scalar(out=nm2[b][:], in0=mv2[b][:, 0:1], scalar1=rs2[b][:, 0:1], scalar2=-1.0, op0=ALU.mult, op1=ALU.mult),
                     r=[('cmv', b), ('crs', b)], w=[('cnmr', b)])
                P.op('act', lambda e: e.activation(out=xn2[b][:], in_=bank(b), func=AF.Identity, bias=nm2[b][:, 0:1], scale=rs2[b][:, 0:1]),
                     r=[('pb', b), ('crs', b), ('cnmr', b)], w=[('xn', b)])

            def postB(n_):
                ti = tiles[n_]
                b = n_ % 2
                P.op('dve', lambda e: e.tensor_tensor(out=xn2[b][:], in0=xn2[b][:], in1=lg[:], op=ALU.mult), r=[('xn', b), 'lg'], w=[('xn', b)])
                P.op('pool', lambda e: e.tensor_tensor(out=xn2[b][:], in0=xn2[b][:], in1=lb[:], op=ALU.add), r=[('xn', b), 'lb'], w=[('xn', b)])
                P.op('act', lambda e: e.activation(out=xn2[b][:], in_=xn2[b][:], func=AF.Silu), r=[('xn', b)], w=[('xn', b)])
                P.op('act', lambda e: e.activation(out=zc[b][:], in_=zc[b][:], func=AF.Silu), r=[('zc', b)], w=[('zc', b)])
                P.op('dve', lambda e: e.tensor_tensor(out=oc[b][:], in0=xn2[b][:], in1=zc[b][:], op=ALU.mult), r=[('xn', b), ('zc', b)], w=[('oc', b)])
                P.dma('sp', lambda e: e.dma_start(out=CAT[ti * 128:(ti + 1) * 128, 1536:2048], in_=oc[b][:]), r=[('oc', b)], w=[('CATc', ti)])

            nTl = len(tiles)
            postA(0)
            for n_ in range(nTl):
                if n_ + 1 < nTl:
                    postA(n_ + 1)
                postB(n_)
        P.barrier()

    def phase_out(l):
        last = (l == n_layers - 1)
        src = xin if l == 0 else X1
        with ExitStack() as es:
            sb = lambda n, s, d=F32: es.enter_context(nc.sbuf_tensor(U(n), s, d))
            wo = sb("wo", [128, 16, D], BF16)
            gate = [sb("gate", [128, D]) for _ in range(2)]
            plg = sb("plg", [128, D])
            plb = sb("plb", [128, D])
            catt = [sb("catt", [128, D]) for _ in range(2)]
            xres = [sb("xres", [128, D]) for _ in range(2)]
            catT = [sb("catT", [128, 16, 128], BF16) for _ in range(2)]
            rt = [sb("rt", [128, D]) for _ in range(2)]
            stats = [sb("ostats", [128, 4, 6]) for _ in range(2)]
            mv = [sb("omv", [128, 2]) for _ in range(2)]
            rs = [sb("ors", [128, 1]) for _ in range(2)]
            nmr = [sb("onmr", [128, 1]) for _ in range(2)]
            wv = w_out[l].rearrange("(kc p) n -> p kc n", p=128)
            wstg = sb("wstg", [128, 16, 512])
            for ng in range(4):
                for hf in range(2):
                    P.dma('sp', lambda e, ng=ng, hf=hf: e.dma_start(out=wstg[:, hf * 8:(hf + 1) * 8, :], in_=wv[:, hf * 8:(hf + 1) * 8, ng * 512:(ng + 1) * 512]),
                          w=[('wstg', hf)])
                P.op('act', lambda e, ng=ng: e.activation(out=wo[:, 0:8, ng * 512:(ng + 1) * 512], in_=wstg[:, 0:8, :], func=AF.Copy),
                     r=[('wstg', 0)], w=[('wo', ng, 0)])
                P.op('dve', lambda e, ng=ng: e.tensor_copy(out=wo[:, 8:16, ng * 512:(ng + 1) * 512], in_=wstg[:, 8:16, :]),
                     r=[('wstg', 1)], w=[('wo', ng, 1)])
            for wh in range(2):
                P.dma('sp', lambda e, wh=wh: e.dma_start(out=gate[wh][:], in_=MOD[l, wh, 2 * D:3 * D].partition_broadcast(128)), w=[('gate', wh)])
            P.dma('sp', lambda e: e.dma_start(out=plg[:], in_=plg_d[l].partition_broadcast(128)), w=['plg'])
            P.dma('sp', lambda e: e.dma_start(out=plb[:], in_=plb_d[l].partition_broadcast(128)), w=['plb'])
            tiles = list(range(2, NT)) if last else list(range(NT))

            def stageT(n_):
                ti = tiles[n_]
                b = n_ % 2
                P.dma('sp', lambda e: e.dma_start(out=catt[b][:], in_=CAT[ti * 128:(ti + 1) * 128, :]), w=[('catt', b)])
                P.dma('sp', lambda e: e.dma_start(out=xres[b][:], in_=src[ti * 128:(ti + 1) * 128, :]), w=[('xres', b)])
                for kc in range(16):
                    bk = 4 + kc // 4
                    sl = psum[:, bk * 512 + (kc % 4) * 128: bk * 512 + (kc % 4 + 1) * 128]
                    P.op('pe', lambda e, kc=kc, sl=sl: e.transpose(sl, catt[b][:, kc * 128:(kc + 1) * 128], ident[:]),
                         r=[('catt', b), 'ident'], w=[('pb', bk)])
                for j in range(4):
                    bk = 4 + j
                    dstv = catT[b][:, 4 * j:4 * j + 4, :].rearrange("p a t -> p (a t)")
                    if j % 2 == 0:
                        P.op('act', lambda e, bk=bk, dstv=dstv: e.activation(out=dstv, in_=bank(bk), func=AF.Copy), r=[('pb', bk)], w=[('catT', b, j)])
                    else:
                        P.op('dve', lambda e, bk=bk, dstv=dstv: e.tensor_copy(out=dstv, in_=bank(bk)), r=[('pb', bk)], w=[('catT', b, j)])

            def stageM(n_):
                ti = tiles[n_]
                b = n_ % 2
                wh = 1 if ti < 2 else 0
                for ng in range(4):
                    for kc in range(16):
                        P.op('pe', lambda e, ng=ng, kc=kc: e.matmul(bank(ng), lhsT=catT[b][:, kc, :], rhs=wo[:, kc, ng * 512:(ng + 1) * 512],
                                                                  start=(kc == 0), stop=(kc == 15)),
                             r=[('catT', b, kc // 4), ('wo', ng, kc // 8)], w=[('pb', ng)])
                    P.op('dve', lambda e, ng=ng: e.tensor_tensor(
                        out=rt[b][:, ng * 512:(ng + 1) * 512], in0=bank(ng), in1=gate[wh][:, ng * 512:(ng + 1) * 512], op=ALU.mult),
                        r=[('pb', ng), ('gate', wh)], w=[('rt', b, ng)])
                krt = [('rt', b, ng) for ng in range(4)]
                P.op('dve', lambda e: e.scalar_tensor_tensor(out=rt[b][:], in0=xres[b][:], scalar=ALPHA, in1=rt[b][:], op0=ALU.mult, op1=ALU.add),
                     r=krt + [('xres', b)], w=krt)
                for q in range(4):
                    P.op('dve', lambda e, q=q: e.bn_stats(out=stats[b][:, q, :], in_=rt[b][:, q * 512:(q + 1) * 512]), r=krt, w=[('ostats', b, q)])
                P.op('dve', lambda e: e.bn_aggr(out=mv[b][:], in_=stats[b][:].rearrange("p a b -> p (a b)")),
                     r=[('ostats', b, q) for q in range(4)], w=[('omv', b)])
                P.op('act', lambda e: e.activation(out=rs[b][:], in_=mv[b][:, 1:2], func=AF.Sqrt, bias=eps5[:, 0:1], scale=1.0),
                     r=[('omv', b), 'eps5'], w=[('ors', b)])
                P.op('dve', lambda e: e.reciprocal(out=rs[b][:], in_=rs[b][:]), r=[('ors', b)], w=[('ors', b)])
                P.op('dve', lambda e: e.tensor_scalar(out=nmr[b][:], in0=mv[b][:, 0:1], scalar1=rs[b][:, 0:1], scalar2=-1.0, op0=ALU.mult, op1=ALU.mult),
                     r=[('omv', b), ('ors', b)], w=[('onmr', b)])
                P.op('act', lambda e: e.activation(out=rt[b][:], in_=rt[b][:], func=AF.Identity, bias=nmr[b][:, 0:1], scale=rs[b][:, 0:1]),
                     r=krt + [('ors', b), ('onmr', b)], w=krt)
                P.op('pool', lambda e: e.tensor_tensor(out=rt[b][:], in0=rt[b][:], in1=plg[:], op=ALU.mult), r=krt + ['plg'], w=krt)
                P.op('pool', lambda e: e.tensor_tensor(out=rt[b][:], in0=rt[b][:], in1=plb[:], op=ALU.add), r=krt + ['plb'], w=krt)

            def stageS(n_):
                ti = tiles[n_]
                b = n_ % 2
                krt = [('rt', b, ng) for ng in range(4)]
                dstr = out[(ti - 2) * 128:(ti - 1) * 128, :] if last else X1[ti * 128:(ti + 1) * 128, :]
                P.dma('sp', lambda e: e.dma_start(out=dstr, in_=rt[b][:]), r=krt, w=[('xo', ti)])

            nTl = len(tiles)
            stageT(0)
            for n_ in range(nTl):
                if n_ + 1 < nTl:
                    stageT(n_ + 1)
                if n_ >= 1:
                    stageS(n_ - 1)
                stageM(n_)
            stageS(nTl - 1)
        P.barrier()

    eps5 = nc.alloc_sbuf_tensor("eps5", [128, 1], F32)
    P.op('pool', lambda e: e.memset(eps5[:], 1e-5), w=['eps5'])
    eps6 = nc.alloc_sbuf_tensor("eps6", [128, 1], F32)
    P.op('pool', lambda e: e.memset(eps6[:], 1e-6), w=['eps6'])
    one1 = nc.alloc_sbuf_tensor("one1", [128, 1], F32)
    P.op('pool', lambda e: e.memset(one1[:], 1.0), w=['one1'])

    phases = []
    phases.append(('mod', phase_mod))
    for l in range(n_layers):
        phases.append((f'inproj{l}', lambda l=l: phase_inproj(l)))
        phases.append((f'na{l}', lambda l=l: phase_na(l)))
        phases.append((f'gla{l}', lambda l=l: phase_gla(l)))
        phases.append((f'conv{l}', lambda l=l: phase_conv(l)))
        phases.append((f'out{l}', lambda l=l: phase_out(l)))
    import os as _os
    skip = (_os.environ.get('KT_SKIP') or '').split(',')
    for name, fn in phases:
        if name in skip:
            continue
        fn()
        if stop_after == name:
            break
    P.emit()
    global _P
    _P = P
    return nc


def host_inputs(inputs, b):
    f = np.float32
    x = np.asarray(inputs['x'], f)
    ctx = np.asarray(inputs['ctx'], f)
    c = np.asarray(inputs['c'], f)
    c_ctx = np.asarray(inputs['c_ctx'], f)
    m = {}
    m['xin'] = np.ascontiguousarray(np.concatenate([ctx[b], x[b]], axis=0))
    cv = np.stack([c[b], c_ctx], axis=0)
    m['cT'] = np.ascontiguousarray(cv.reshape(2, 16, 128).transpose(2, 0, 1))
    m['w_ada'] = np.asarray(inputs['w_ada'], f)
    m['b_ada'] = np.asarray(inputs['b_ada'], f)
    m['w_in'] = np.asarray(inputs['w_in'], f)
    rpb = np.asarray(inputs['rpb'], f)
    p = np.arange(128)
    cp = p % 64
    half = p // 64
    d0 = np.arange(16) - 8
    cq = np.arange(64)
    dd = np.clip(d0[None, :] + half[:, None] + 7, 0, 14)
    dc = np.clip(cp[:, None] - cq[None, :] + 15, 0, 30)
    bt = rpb[:, :, dd[:, :, None], dc[:, None, :]]
    m['btab'] = np.ascontiguousarray(bt.reshape(2, 16, 128, 1024))
    cs = np.clip(cq - 8, 0, 48)
    ok = (cp[:, None] >= cs[None, :]) & (cp[:, None] < cs[None, :] + 16)
    m['nmask'] = np.where(ok, 0.0, NEG).astype(f)
    w2 = np.asarray(inputs['gla_w2'], f)
    gb = np.asarray(inputs['gla_b'], f)
    m['w2a'] = np.ascontiguousarray(np.concatenate([w2, gb[:, :, None, :]], axis=2))
    m['gnorm'] = np.asarray(inputs['gla_norm'], f)
    m['cwT'] = np.ascontiguousarray(np.asarray(inputs['conv_w'], f).transpose(0, 2, 1))
    m['cb'] = np.ascontiguousarray(np.asarray(inputs['conv_b'], f).reshape(2, 4, 128).transpose(0, 2, 1))
    m['clg'] = np.asarray(inputs['conv_ln_g'], f)
    m['clb'] = np.asarray(inputs['conv_ln_b'], f)
    m['w_out'] = np.asarray(inputs['w_out'], f)
    m['plg'] = np.asarray(inputs['post_ln_g'], f)
    m['plb'] = np.asarray(inputs['post_ln_b'], f)
    pos = np.arange(4096)
    rows = (pos // 64).astype(f)
    colsp = (pos % 64).astype(f)
    inv = (10000.0 ** (-np.arange(0, 32, 2, dtype=f) / 32)).astype(f)
    ar = rows[:, None] * inv[None, :]
    ac = colsp[:, None] * inv[None, :]
    m['ropecos'] = np.concatenate([np.cos(ar), np.cos(ar), np.cos(ac), np.cos(ac)], axis=1).astype(f)
    m['ropesin'] = np.concatenate([-np.sin(ar), np.sin(ar), -np.sin(ac), np.sin(ac)], axis=1).astype(f)
    m['ident'] = np.eye(128, dtype=f)
    j = np.arange(128)[:, None]
    i = np.arange(128)[None, :]
    m['tri'] = np.stack([(j > i), (j <= i), (j < i), (j >= i)]).astype(f)
    return m


_NC = None


def kernel(**inputs):
    global _NC
    if _NC is None:
        _NC = build()
    in_maps = [host_inputs(inputs, cid % 4) for cid in range(8)]
    res = run_bass_kernel_spmd(_NC, in_maps, core_ids=list(range(8)))
    return np.stack([res.results[b]["out"] for b in range(4)], axis=0).astype(np.float32)
```
